# Optimizing a Trainium2 kernel written in Bass

```python
import math
import jax, jax.numpy as jnp
from jax import lax
import numpy as np

D_MODEL = 1024
BATCH = 2
SEQ = 16384
DEPTH = 2

CHUNK = 64
P_DIM = 256
EPS = 1e-6
DN_HEADS = 4
DN_DK = 128
DN_DV = 128
DN_CONV = 4
GLA_HEADS = 4
GLA_DK = 64
GLA_DV = 128
GLA_RANK = 16
GLA_NORMALIZER = 16.0
S5_WIDTH = 512
S5_GROUP = 16
S5_GROUPS = S5_WIDTH // S5_GROUP
S5_STATE = 64
N_BRANCH = 3
BRANCH_WIDTH = 512
D_FF = 2816
FFN_CONV = 3

DN_QK = DN_HEADS * DN_DK
DN_V = DN_HEADS * DN_DV
GLA_QK = GLA_HEADS * GLA_DK
GLA_V = GLA_HEADS * GLA_DV
IN_SPLITS = (DN_QK, DN_QK, DN_V, DN_HEADS, DN_HEADS, DN_V,
             GLA_QK, GLA_QK, GLA_V, GLA_RANK, GLA_V, S5_WIDTH)
D_IN = sum(IN_SPLITS)

kernel_name = "hybrid_deltanet_gla_s5_gated_merge"


def rmsnorm(x, g):
    xf = x.astype(jnp.float32)
    y = xf * lax.rsqrt(jnp.mean(xf * xf, axis=-1, keepdims=True) + EPS)
    return (y * g.astype(jnp.float32)).astype(x.dtype)


def l2norm(x):
    return x * lax.rsqrt(jnp.sum(x * x, axis=-1, keepdims=True) + EPS)


def split_columns(z, sizes):
    offsets = np.cumsum(sizes)[:-1].tolist()
    return jnp.split(z, offsets, axis=-1)


def causal_dwconv(x, w):
    K = w.shape[0]
    L = x.shape[1]
    xp = jnp.pad(x, ((0, 0), (K - 1, 0), (0, 0)))
    return sum(w[k] * xp[:, k:k + L] for k in range(K))


def to_chunks(t):
    Bsz, L = t.shape[:2]
    return jnp.moveaxis(t.reshape(Bsz, L // CHUNK, CHUNK, *t.shape[2:]), 2, 3)


def gated_delta_rule(q, k, v, beta, g_log):
    Bsz, L, H, dk = q.shape
    dv = v.shape[-1]
    q, k, v, beta, g_log = map(to_chunks, (q, k, v, beta, g_log))
    gam = jnp.cumsum(g_log, axis=-1)
    diff = gam[..., :, None] - gam[..., None, :]
    incl = jnp.tril(jnp.ones((CHUNK, CHUNK), bool))
    strict = jnp.tril(jnp.ones((CHUNK, CHUNK), bool), -1)
    dec_incl = jnp.exp(jnp.where(incl, diff, -jnp.inf))
    dec_strict = jnp.where(strict, dec_incl, 0.0)
    kk = jnp.einsum('bnhtd,bnhsd->bnhts', k, k)
    tri = jnp.eye(CHUNK, dtype=q.dtype) + beta[..., :, None] * kk * dec_strict
    rhs = jnp.concatenate([beta[..., None] * v, (beta * jnp.exp(gam))[..., None] * k], axis=-1)
    sol = lax.linalg.triangular_solve(tri, rhs, left_side=True, lower=True, unit_diagonal=True)
    u_new, w = sol[..., :dv], sol[..., dv:]
    attn = jnp.einsum('bnhtd,bnhsd->bnhts', q, k) * dec_incl
    q_dec = q * jnp.exp(gam)[..., None]
    k_dec = k * jnp.exp(gam[..., -1:] - gam)[..., None]
    g_end = jnp.exp(gam[..., -1])

    def step(S, inp):
        u_c, w_c, a_c, qd, kd, ge = inp
        u = u_c - jnp.einsum('bhck,bhkv->bhcv', w_c, S)
        o = jnp.einsum('bhck,bhkv->bhcv', qd, S) + jnp.einsum('bhts,bhsv->bhtv', a_c, u)
        S = ge[..., None, None] * S + jnp.einsum('bhck,bhcv->bhkv', kd, u)
        return S, o

    xs = tuple(jnp.moveaxis(t, 1, 0) for t in (u_new, w, attn, q_dec, k_dec, g_end))
    S0 = jnp.zeros((Bsz, H, dk, dv), q.dtype)
    _, o = lax.scan(step, S0, xs)
    return o.transpose(1, 0, 3, 2, 4).reshape(Bsz, L, H, dv)


def gla_rule(q, k, v, g_log):
    Bsz, L, H, dk = q.shape
    dv = v.shape[-1]
    q, k, v, g_log = map(to_chunks, (q, k, v, g_log))
    b = jnp.cumsum(g_log, axis=3)
    q_e = q * jnp.exp(b)
    k_e = k * jnp.exp(-b)
    incl = jnp.tril(jnp.ones((CHUNK, CHUNK), bool))
    attn = jnp.where(incl, jnp.einsum('bnhtd,bnhsd->bnhts', q_e, k_e), 0.0)
    intra = jnp.einsum('bnhts,bnhsv->bnhtv', attn, v)
    k_dec = k * jnp.exp(b[..., -1:, :] - b)
    dS = jnp.einsum('bnhck,bnhcv->bnhkv', k_dec, v)
    g_end = jnp.exp(b[..., -1, :])

    def step(S, inp):
        qe, ds, ge = inp
        o = jnp.einsum('bhck,bhkv->bhcv', qe, S)
        return ge[..., None] * S + ds, o

    xs = tuple(jnp.moveaxis(t, 1, 0) for t in (q_e, dS, g_end))
    S0 = jnp.zeros((Bsz, H, dk, dv), q.dtype)
    _, inter = lax.scan(step, S0, xs)
    o = intra + jnp.moveaxis(inter, 0, 1)
    return o.transpose(0, 1, 3, 2, 4).reshape(Bsz, L, H, dv)


def deltanet_branch(q, k, v, b_raw, a_raw, gate, conv_w, a_log, dt_bias, norm_g):
    Bsz, L, _ = q.shape
    f32 = jnp.float32
    qkv = jax.nn.silu(causal_dwconv(jnp.concatenate([q, k, v], axis=-1), conv_w))
    q, k, v = split_columns(qkv.astype(f32), (DN_QK, DN_QK, DN_V))
    q = l2norm(q.reshape(Bsz, L, DN_HEADS, DN_DK)) * DN_DK ** -0.5
    k = l2norm(k.reshape(Bsz, L, DN_HEADS, DN_DK))
    v = v.reshape(Bsz, L, DN_HEADS, DN_DV)
    beta = jax.nn.sigmoid(b_raw.astype(f32))
    g_log = -jnp.exp(a_log.astype(f32)) * jax.nn.softplus(a_raw.astype(f32) + dt_bias.astype(f32))
    o = gated_delta_rule(q, k, v, beta, g_log)
    o = rmsnorm(o, norm_g) * jax.nn.silu(gate.astype(f32).reshape(Bsz, L, DN_HEADS, DN_DV))
    return o.reshape(Bsz, L, DN_V).astype(gate.dtype)


def gla_branch(q, k, v, lr, r, w2, b2, norm_g):
    Bsz, L, _ = q.shape
    f32 = jnp.float32
    q = q.astype(f32).reshape(Bsz, L, GLA_HEADS, GLA_DK) * GLA_DK ** -0.5
    k = k.astype(f32).reshape(Bsz, L, GLA_HEADS, GLA_DK)
    v = v.astype(f32).reshape(Bsz, L, GLA_HEADS, GLA_DV)
    g_log = jax.nn.log_sigmoid((lr @ w2 + b2).astype(f32)) / GLA_NORMALIZER
    g_log = g_log.reshape(Bsz, L, GLA_HEADS, GLA_DK)
    o = gla_rule(q, k, v, g_log)
    o = rmsnorm(o, norm_g) * jax.nn.silu(r.astype(f32).reshape(Bsz, L, GLA_HEADS, GLA_DV))
    return o.reshape(Bsz, L, GLA_V).astype(r.dtype)


def s5_branch(u, lam_re, lam_im, log_step, b_re, b_im, c_re, c_im, d, w_glu, b_glu):
    Bsz, L, _ = u.shape
    f32 = jnp.float32
    uf = u.astype(f32)
    lr, li = lam_re.astype(f32), lam_im.astype(f32)
    step = jnp.exp(log_step.astype(f32))[:, None]
    mag = jnp.exp(lr * step)
    abar_re, abar_im = mag * jnp.cos(li * step), mag * jnp.sin(li * step)
    den = lr * lr + li * li
    nr, ni = abar_re - 1.0, abar_im
    fr = (nr * lr + ni * li) / den
    fi = (ni * lr - nr * li) / den
    br, bi = b_re.astype(f32), b_im.astype(f32)
    bbar_re = fr[..., None] * br - fi[..., None] * bi
    bbar_im = fr[..., None] * bi + fi[..., None] * br
    ug = uf.reshape(Bsz, L, S5_GROUPS, S5_GROUP)
    bu_re = jnp.einsum('blgc,gnc->blgn', ug, bbar_re)
    bu_im = jnp.einsum('blgc,gnc->blgn', ug, bbar_im)
    a_re = jnp.broadcast_to(abar_re, (1, L, S5_GROUPS, S5_STATE))
    a_im = jnp.broadcast_to(abar_im, (1, L, S5_GROUPS, S5_STATE))

    def combine(e1, e2):
        a1r, a1i, b1r, b1i = e1
        a2r, a2i, b2r, b2i = e2
        return (a1r * a2r - a1i * a2i, a1r * a2i + a1i * a2r,
                a2r * b1r - a2i * b1i + b2r, a2r * b1i + a2i * b1r + b2i)

    _, _, h_re, h_im = lax.associative_scan(combine, (a_re, a_im, bu_re, bu_im), axis=1)
    y = (jnp.einsum('blgn,gcn->blgc', h_re, c_re.astype(f32))
         - jnp.einsum('blgn,gcn->blgc', h_im, c_im.astype(f32)))
    y = y.reshape(Bsz, L, S5_WIDTH) + d.astype(f32) * uf
    y = jax.nn.gelu(y)
    y = y * jax.nn.sigmoid(y @ w_glu.astype(f32) + b_glu.astype(f32))
    return y.astype(u.dtype)


def token_mixer(h, w_in, dn_conv_w, dn_a_log, dn_dt_bias, dn_norm, gla_w2, gla_b2, gla_norm,
                s5_lam_re, s5_lam_im, s5_log_step, s5_b_re, s5_b_im, s5_c_re, s5_c_im, s5_d,
                s5_w_glu, s5_b_glu, w_gate, b_gate, w_branch, w_o):
    Bsz, L, _ = h.shape
    z = h @ w_in
    (dn_q, dn_k, dn_v, dn_b, dn_a, dn_g,
     gl_q, gl_k, gl_v, gl_lr, gl_r, s5_u) = split_columns(z, IN_SPLITS)
    y_a = deltanet_branch(dn_q, dn_k, dn_v, dn_b, dn_a, dn_g, dn_conv_w, dn_a_log, dn_dt_bias, dn_norm)
    y_b = gla_branch(gl_q, gl_k, gl_v, gl_lr, gl_r, gla_w2, gla_b2, gla_norm)
    y_c = s5_branch(s5_u, s5_lam_re, s5_lam_im, s5_log_step, s5_b_re, s5_b_im,
                    s5_c_re, s5_c_im, s5_d, s5_w_glu, s5_b_glu)
    br = jnp.stack([y_a, y_b, y_c], axis=2).astype(h.dtype)
    proj = jnp.einsum('blgc,gcd->blgd', br, w_branch)
    gates = jax.nn.sigmoid((h @ w_gate + b_gate).reshape(Bsz, L, N_BRANCH, D_MODEL))
    merged = jnp.sum(gates * proj, axis=2)
    return merged @ w_o


def conv_glu(h, w_up, conv_w, conv_b, w_down):
    g, u = jnp.split(h @ w_up, 2, axis=-1)
    g = causal_dwconv(g, conv_w) + conv_b
    return (jax.nn.gelu(g) * u) @ w_down


def setup_inputs(seed: int = 0) -> dict:
    key = jax.random.key(seed)
    ks = iter(jax.random.split(key, 48))
    f32 = jnp.float32

    def nrm(shape, scale):
        return scale * jax.random.normal(next(ks), shape, f32)

    def gain(shape):
        return 1.0 + 0.02 * jax.random.normal(next(ks), shape, f32)

    G, N = S5_GROUPS, S5_STATE
    dt = jnp.exp(jax.random.uniform(next(ks), (DEPTH, DN_HEADS), f32, math.log(1e-3), math.log(1e-1)))
    n_idx = jnp.arange(N, dtype=f32)
    return {
        "x": nrm((BATCH, SEQ, D_MODEL), 1.0),
        "p": nrm((DEPTH, BATCH, SEQ, P_DIM), 1.0),
        "attn_norm": gain((DEPTH, D_MODEL)),
        "w_in": nrm((DEPTH, D_MODEL, D_IN), D_MODEL ** -0.5),
        "dn_conv_w": nrm((DEPTH, DN_CONV, DN_QK * 2 + DN_V), DN_CONV ** -0.5),
        "dn_a_log": jnp.log(jax.random.uniform(next(ks), (DEPTH, DN_HEADS), f32, 1.0, 16.0)),
        "dn_dt_bias": dt + jnp.log(-jnp.expm1(-dt)),
        "dn_norm": gain((DEPTH, DN_DV)),
        "gla_w2": nrm((DEPTH, GLA_RANK, GLA_QK), GLA_RANK ** -0.5),
        "gla_b2": nrm((DEPTH, GLA_QK), 0.1),
        "gla_norm": gain((DEPTH, GLA_DV)),
        "s5_lam_re": -0.5 + nrm((DEPTH, G, N), 0.01),
        "s5_lam_im": math.pi * n_idx + nrm((DEPTH, G, N), 0.01),
        "s5_log_step": jax.random.uniform(next(ks), (DEPTH, G), f32, math.log(1e-3), math.log(1e-1)),
        "s5_b_re": nrm((DEPTH, G, N, S5_GROUP), (2 * S5_GROUP) ** -0.5),
        "s5_b_im": nrm((DEPTH, G, N, S5_GROUP), (2 * S5_GROUP) ** -0.5),
        "s5_c_re": nrm((DEPTH, G, S5_GROUP, N), N ** -0.5),
        "s5_c_im": nrm((DEPTH, G, S5_GROUP, N), N ** -0.5),
        "s5_d": nrm((DEPTH, S5_WIDTH), 1.0),
        "s5_w_glu": nrm((DEPTH, S5_WIDTH, S5_WIDTH), S5_WIDTH ** -0.5),
        "s5_b_glu": nrm((DEPTH, S5_WIDTH), 0.02),
        "w_gate": nrm((DEPTH, D_MODEL, N_BRANCH * D_MODEL), D_MODEL ** -0.5),
        "b_gate": nrm((DEPTH, N_BRANCH * D_MODEL), 0.02),
        "w_branch": nrm((DEPTH, N_BRANCH, BRANCH_WIDTH, D_MODEL), BRANCH_WIDTH ** -0.5),
        "w_o": nrm((DEPTH, D_MODEL, D_MODEL), D_MODEL ** -0.5),
        "ffn_norm": gain((DEPTH, D_MODEL)),
        "w_up": nrm((DEPTH, D_MODEL, 2 * D_FF), D_MODEL ** -0.5),
        "ffn_conv_w": nrm((DEPTH, FFN_CONV, D_FF), FFN_CONV ** -0.5),
        "ffn_conv_b": nrm((DEPTH, D_FF), 0.02),
        "w_down": nrm((DEPTH, D_FF, D_MODEL), D_FF ** -0.5),
        "ple_norm": gain((DEPTH, D_MODEL)),
        "w_ple_gate": nrm((DEPTH, D_MODEL, D_MODEL), D_MODEL ** -0.5),
        "w_ple_proj": nrm((DEPTH, P_DIM, D_MODEL), P_DIM ** -0.5),
        "final_norm": gain((D_MODEL,)),
    }


def reference(x, p, attn_norm, w_in, dn_conv_w, dn_a_log, dn_dt_bias, dn_norm, gla_w2, gla_b2,
              gla_norm, s5_lam_re, s5_lam_im, s5_log_step, s5_b_re, s5_b_im, s5_c_re, s5_c_im,
              s5_d, s5_w_glu, s5_b_glu, w_gate, b_gate, w_branch, w_o, ffn_norm, w_up,
              ffn_conv_w, ffn_conv_b, w_down, ple_norm, w_ple_gate, w_ple_proj, final_norm):
    for i in range(DEPTH):
        h = rmsnorm(x, attn_norm[i])
        x = x + token_mixer(h, w_in[i], dn_conv_w[i], dn_a_log[i], dn_dt_bias[i], dn_norm[i],
                            gla_w2[i], gla_b2[i], gla_norm[i], s5_lam_re[i], s5_lam_im[i],
                            s5_log_step[i], s5_b_re[i], s5_b_im[i], s5_c_re[i], s5_c_im[i],
                            s5_d[i], s5_w_glu[i], s5_b_glu[i], w_gate[i], b_gate[i],
                            w_branch[i], w_o[i])
        h = rmsnorm(x, ffn_norm[i])
        x = x + conv_glu(h, w_up[i], ffn_conv_w[i], ffn_conv_b[i], w_down[i])
        ple_gate = jax.nn.sigmoid(rmsnorm(x, ple_norm[i]) @ w_ple_gate[i])
        x = x + ple_gate * (p[i] @ w_ple_proj[i])
    return rmsnorm(x, final_norm)
```

```python
import numpy as np
from contextlib import ExitStack
import concourse.bass as bass
import concourse.mybir as mybir
from concourse.bass_utils import run_bass_kernel_spmd

F32 = mybir.dt.float32
BF16 = mybir.dt.bfloat16
AF = mybir.ActivationFunctionType
ALU = mybir.AluOpType
AX = mybir.AxisListType

ENGS = ("pe", "act", "dve", "pool", "sp")
EPOCH = 16000
RING = 12


class Buf:
    __slots__ = ("ap", "w", "r", "name", "psum")

    def __init__(self, ap, name=""):
        self.ap = ap
        self.psum = False
        self.w = None
        self.r = []
        self.name = name

    def __getitem__(self, k):
        return self.ap[k]


class Prog:
    def __init__(self, nc, self_sync=True, prefix=""):
        self.nc = nc
        self.prefix = prefix
        self.es = ExitStack()
        self.es_sem = ExitStack()
        self.phase_finals = []
        self.items = {e: [] for e in ENGS}
        self.count = {e: 0 for e in ENGS}
        self.waited = {e: {} for e in ENGS}
        self.self_sync = self_sync
        self.semh = {}
        self.dma_n = {"sp": 0, "pool": 0, "act": 0}
        self.dma_last = {}
        self.nbuf = 0

    def sem(self, key):
        if key not in self.semh:
            nm = self.prefix + "s_" + "_".join(str(k) for k in key)
            self.semh[key] = self.es_sem.enter_context(self.nc.semaphore(nm))
        return self.semh[key]

    def sb(self, shape, dtype=F32, name=None):
        self.nbuf += 1
        name = self.prefix + (name or f"sb{self.nbuf}")
        t = self.es.enter_context(self.nc.sbuf_tensor(name, list(shape), dtype))
        return t

    def ps(self, shape, dtype=F32, name=None):
        self.nbuf += 1
        name = self.prefix + (name or f"ps{self.nbuf}")
        t = self.es.enter_context(self.nc.psum_tensor(name, list(shape), dtype))
        return t

    def buf(self, ap, name=""):
        return Buf(ap, name)

    def sbuf(self, shape, dtype=F32, name=None):
        t = self.sb(shape, dtype, name)
        return Buf(t[:], name or "")

    def psbuf(self, shape, dtype=F32, name=None):
        t = self.ps(shape, dtype, name)
        b = Buf(t[:], name or "")
        b.psum = True
        return b

    def _deps(self, reads, writes):
        deps = []
        for b in reads:
            if b.w is not None:
                deps.append(b.w)
        for b in writes:
            if b.w is not None:
                deps.append(b.w)
            deps.extend(b.r)
        return deps

    def _waits(self, eng, deps, own_key_prefix):
        waits = []
        wd = self.waited[eng]
        for (key, val) in deps:
            if key[0] == own_key_prefix and key[0] != "dma":
                if eng == "pe" or not self.self_sync:
                    continue
            if wd.get(key, 0) >= val:
                continue
            wd[key] = val
            waits.append((key, val))
        return waits

    def op(self, eng, fn, reads=(), writes=()):
        pr = [b for b in reads if b.psum]
        if pr:
            reads = [b for b in reads if not b.psum]
            writes = list(writes) + pr
        deps = self._deps(reads, writes)
        waits = self._waits(eng, deps, eng)
        self.count[eng] += 1
        c = self.count[eng]
        key = (eng, (c - 1) // EPOCH)
        val = (c - 1) % EPOCH + 1
        ev = (key, val)
        self.items[eng].append((waits, fn, ev, 1))
        for b in reads:
            b.r.append(ev)
        for b in writes:
            b.w = ev
            b.r = []
        return ev

    def dma(self, out_ap, in_ap, reads=(), writes=(), q="sp", **kw):
        deps = self._deps(reads, writes)
        j = self.dma_n[q]
        self.dma_n[q] += 1
        slot = j % RING
        key = ("dma", q, slot)
        if j >= RING:
            deps.append((key, 16 * (j // RING)))
        waits = self._waits(q, deps, "dma")
        val = 16 * (j // RING + 1)
        ev = (key, val)
        self.dma_last[key] = val

        def fn(e, out_ap=out_ap, in_ap=in_ap, kw=kw):
            o = out_ap(e) if callable(out_ap) else out_ap
            i = in_ap(e) if callable(in_ap) else in_ap
            try:
                return e.dma_start(out=o, in_=i, **kw)
            except Exception:
                print('DMA FAIL out', o, 'in', i, flush=True)
                raise
        self.items[q].append((waits, fn, ev, 16))
        for b in reads:
            b.r.append(ev)
        for b in writes:
            b.w = ev
            b.r = []
        return ev

    def _last_events(self):
        finals = []
        for key, val in self.dma_last.items():
            finals.append((key, val))
        for e in ("pe", "act", "dve", "pool"):
            c = self.count[e]
            if c > 0:
                finals.append(((e, (c - 1) // EPOCH), (c - 1) % EPOCH + 1))
        return finals

    def begin_phase(self):
        finals = self._last_events()
        for e in ENGS:
            waits = self._waits(e, finals, "__none__")
            if waits:
                self.items[e].append((waits, None, None, 0))

    def end_phase(self, final=False):
        nc = self.nc
        for e in ENGS:
            for (waits, fn, ev, inc) in self.items[e]:
                if ev is not None:
                    self.sem(ev[0])
                for (k, v) in waits:
                    self.sem(k)
        final_waits = self._last_events() if final else []
        for (k, v) in final_waits:
            self.sem(k)
        items = self.items
        semh = self.semh

        def run(e, lst, fin=False):
            for (waits, fn, ev, inc) in lst:
                for (k, v) in waits:
                    e.wait_ge(semh[k], v)
                if fn is None:
                    continue
                ins = fn(e)
                if ins is not None and ev is not None:
                    ins.then_inc(semh[ev[0]], inc)
            if fin:
                for (k, v) in final_waits:
                    e.wait_ge(semh[k], v)

        with nc.Block() as block:
            @block.sync
            def _(e):
                run(e, items["sp"], fin=final)

            @block.tensor
            def _(e):
                run(e, items["pe"])

            @block.scalar
            def _(e):
                run(e, items["act"])

            @block.vector
            def _(e):
                run(e, items["dve"])

            @block.gpsimd
            def _(e):
                run(e, items["pool"])
        self.items = {e: [] for e in ENGS}
        self.es.close()
        self.es = ExitStack()

    def emit(self):
        self.end_phase(final=True)

    def close(self):
        self.es.close()
        self.es_sem.close()

    def mm(self, out, lhsT, rhs, start=True, stop=True, reads=(), writes=()):
        return self.op("pe", lambda e: e.matmul(out, lhsT, rhs, start=start, stop=stop),
                       reads, writes)


class View:
    __slots__ = ("buf", "ap")

    def __init__(self, buf, ap):
        self.buf = buf
        self.ap = ap

    def __getitem__(self, k):
        return View(self.buf, self.ap[k])


def V(buf, *k):
    if not k:
        return View(buf, buf.ap)
    return View(buf, buf.ap[k if len(k) > 1 else k[0]])


def _ap(x):
    return x.ap if isinstance(x, View) else x


def _bufs(*xs):
    return [x.buf for x in xs if isinstance(x, View)]


class Ops:
    def __init__(self, P):
        self.P = P

    def mm(self, out, lhsT, rhs, start=True, stop=True):
        return self.P.op("pe", lambda e: e.matmul(out.ap, lhsT.ap, rhs.ap, start=start, stop=stop),
                         _bufs(lhsT, rhs), _bufs(out))

    def tr(self, out, in_, ident):
        return self.P.op("pe", lambda e: e.transpose(out.ap, in_.ap, ident.ap), _bufs(in_, ident), _bufs(out))

    def act(self, out, in_, func, bias=None, scale=None, accum=None, eng="act"):
        kw = {}
        if bias is not None:
            kw["bias"] = _ap(bias)
        if scale is not None:
            kw["scale"] = _ap(scale)
        if accum is not None:
            kw["accum_out"] = _ap(accum)
        return self.P.op("act", lambda e: e.activation(out.ap, in_.ap, func, **kw),
                         _bufs(in_, bias, scale), _bufs(out, accum))

    def tt(self, eng, out, in0, in1, op):
        return self.P.op(eng, lambda e: e.tensor_tensor(out.ap, in0.ap, in1.ap, op=op), _bufs(in0, in1), _bufs(out))

    def ts(self, eng, out, in0, s1, op0, s2=None, op1=None):
        if op1 is None:
            return self.P.op(eng, lambda e: e.tensor_scalar(out.ap, in0.ap, _ap(s1), None, op0=op0),
                             _bufs(in0, s1), _bufs(out))
        return self.P.op(eng, lambda e: e.tensor_scalar(out.ap, in0.ap, _ap(s1), _ap(s2), op0=op0, op1=op1),
                         _bufs(in0, s1, s2), _bufs(out))

    def stt(self, eng, out, in0, scalar, in1, op0, op1):
        return self.P.op(eng, lambda e: e.scalar_tensor_tensor(out.ap, in0.ap, _ap(scalar), in1.ap, op0=op0, op1=op1),
                         _bufs(in0, scalar, in1), _bufs(out))

    def copy(self, eng, out, in_):
        if eng == "act":
            return self.act(out, in_, AF.Identity)
        return self.P.op(eng, lambda e: e.tensor_copy(out.ap, in_.ap), _bufs(in_), _bufs(out))

    def recip(self, out, in_):
        return self.P.op("dve", lambda e: e.reciprocal(out.ap, in_.ap), _bufs(in_), _bufs(out))

    def scan(self, eng, out, d0, d1, init, op0=None, op1=None):
        op0 = op0 or ALU.mult
        op1 = op1 or ALU.add
        return self.P.op(eng, lambda e: e.tensor_tensor_scan(out.ap, d0.ap, d1.ap, _ap(init), op0=op0, op1=op1),
                         _bufs(d0, d1, init), _bufs(out))

    def memset(self, eng, out, val):
        return self.P.op(eng, lambda e: e.memset(out.ap, val), [], _bufs(out))

    def aselect(self, out, in_, pattern, cmp, fill, base=0, cm=1):
        return self.P.op("pool", lambda e: e.affine_select(out.ap, in_.ap, pattern=pattern, compare_op=cmp, fill=fill,
                                                          base=base, channel_multiplier=cm),
                         _bufs(in_), _bufs(out))

    def dma(self, out, in_, q="sp"):
        return self.P.dma(_ap(out), _ap(in_), reads=_bufs(in_), writes=_bufs(out), q=q)


def _prog_coll(self, kind, in_ap, out_ap, groups, reads=(), writes=()):
    q = "pool"
    deps = self._deps(reads, writes)
    self.ncoll = getattr(self, "ncoll", 0) + 1
    key = ("cc", self.ncoll)
    waits = self._waits(q, deps, "dma")
    ev = (key, 1)
    self.dma_last[key] = 1

    def fn(e):
        return e.collective_compute(kind, ALU.bypass, groups, [in_ap.opt()], [out_ap.opt()])
    self.items[q].append((waits, fn, ev, 1))
    for b in reads:
        b.r.append(ev)
    for b in writes:
        b.w = ev
        b.r = []
    return ev


Prog.coll = _prog_coll

import math

D = 1024
DFF = 2816
NF = 22
EPS = 1e-6
NEG = -1.0e30
TB = 512
NB = 4
NWF = 656
NWT = 386
SEQ = 16384
NCORES = 8
GROUPS = [[0, 1, 2, 3], [4, 5, 6, 7]]


class PsPool:
    def __init__(self, P, n=8, name="psp"):
        self.bufs = [P.psbuf([128, 512], F32, f"{name}{i}") for i in range(n)]
        self.i = 0

    def get(self):
        b = self.bufs[self.i % len(self.bufs)]
        self.i += 1
        return b

def emit_mixer(nc, P, O, L, sfx, xsrc, yout, after_sb):
    def din(name, shape):
        return nc.dram_tensor(name + sfx, list(shape), F32, kind="ExternalInput").ap()
    anorm = din("anorm", [128, 8])
    wf_d = din("wf", [D, NWF])
    wt_d = din("wt", [D, NWT])
    dn_sc_d = din("dn_sc", [128, 2])
    dn_cw_d = din("dn_cw", [128, 12])
    dn_ng_d = din("dn_ng", [128, 128])
    gla_ng_d = din("gla_ng", [128, 128])
    gla_w2_d = din("gla_w2", [16, 64])
    gla_b2_d = din("gla_b2", [64, 1])
    s5_lam_d = din("s5_lam", [128, 12])
    s5_b_d = din("s5_b", [2, 4, 128, 16])
    s5_c_d = din("s5_c", [2, 4, 128, 16])
    s5_d_d = din("s5_d", [128, 1])
    psp = PsPool(P, n=7)
    yps_ded = P.psbuf([128, 512], F32, "yps_ded")

    def S(shape, dt=F32, name=None):
        return P.sbuf(shape, dt, name)

    ones = S([128, 512], F32, "ones")
    O.memset("pool", V(ones), 1.0)
    ident = S([128, 128], F32, "ident")
    O.memset("pool", V(ident), 1.0)
    O.aselect(V(ident), V(ident), [[-1, 128]], ALU.is_equal, 0.0, cm=1)
    U = S([128, 128], F32, "U")
    O.memset("pool", V(U), 1.0)
    O.aselect(V(U), V(U), [[1, 128]], ALU.is_ge, 0.0, cm=-1)
    mneg = S([128, 128], F32, "mneg")
    O.memset("pool", V(mneg), 0.0)
    O.aselect(V(mneg), V(mneg), [[-1, 128]], ALU.is_ge, NEG, cm=1)
    mnegT = S([128, 128], F32, "mnegT")
    O.memset("pool", V(mnegT), 0.0)
    O.aselect(V(mnegT), V(mnegT), [[1, 128]], ALU.is_ge, NEG, cm=-1)
    nLs = S([128, 128], F32, "nLs")
    O.memset("pool", V(nLs), -1.0)
    O.aselect(V(nLs), V(nLs), [[-1, 128]], ALU.is_gt, 0.0, cm=1)

    c_anorm = S([128, 8], F32, "c_anorm")
    O.dma(V(c_anorm), anorm)
    wf = S([128, 8, NWF], BF16, "wf_s")
    wt = S([128, 8, NWT], BF16, "wt_s")
    O.dma(V(wf), wf_d.rearrange("(k p) n -> p k n", p=128), q="pool")
    O.dma(V(wt), wt_d.rearrange("(k p) n -> p k n", p=128), q="pool")
    dn_sc = S([128, 2], F32, "dn_sc_s"); O.dma(V(dn_sc), dn_sc_d)
    dn_cw = S([128, 12], F32, "dn_cw_s"); O.dma(V(dn_cw), dn_cw_d)
    dn_ng = S([128, 128], F32, "dn_ng_s"); O.dma(V(dn_ng), dn_ng_d)
    gla_ng = S([128, 128], F32, "gla_ng_s"); O.dma(V(gla_ng), gla_ng_d)
    gla_w2 = S([16, 64], F32, "gla_w2_s"); O.dma(V(gla_w2), gla_w2_d)
    gla_b2 = S([64, 1], F32, "gla_b2_s"); O.dma(V(gla_b2), gla_b2_d)
    nb2 = S([64, 1], F32, "nb2")
    O.ts("dve", V(nb2), V(gla_b2), -1.0, ALU.mult)
    negA = S([128, 1], F32, "negA")
    O.act(V(negA), V(dn_sc, slice(None), slice(0, 1)), AF.Exp)
    O.ts("dve", V(negA), V(negA), -1.0, ALU.mult)
    s5d = S([128, 1], F32, "s5d"); O.dma(V(s5d), s5_d_d)

    lam = S([128, 12], F32, "lam"); O.dma(V(lam), s5_lam_d)
    lr_, li_, ls_ = (V(lam, slice(None), slice(0, 4)), V(lam, slice(None), slice(4, 8)),
                     V(lam, slice(None), slice(8, 12)))
    pp = S([128, 64], F32, "s5pp")

    def col(i):
        return V(pp, slice(None), slice(4 * i, 4 * i + 4))
    step, lrs, th, mag, c8, s8, t0, t1, cr, ci, den, nr, fr, fi, t2, t3 = [col(i) for i in range(16)]
    O.act(step, ls_, AF.Exp)
    O.tt("dve", lrs, lr_, step, ALU.mult)
    O.tt("dve", th, li_, step, ALU.mult)
    O.act(mag, lrs, AF.Exp)
    halfpi = S([128, 1], F32, "halfpi"); O.memset("pool", V(halfpi), math.pi / 2)
    O.act(s8, th, AF.Sin, scale=0.125)
    O.act(c8, th, AF.Sin, scale=-0.125, bias=V(halfpi))
    for _ in range(3):
        O.tt("dve", t0, c8, c8, ALU.mult)
        O.tt("dve", t1, s8, s8, ALU.mult)
        O.tt("dve", t2, c8, s8, ALU.mult)
        O.tt("dve", c8, t0, t1, ALU.subtract)
        O.ts("dve", s8, t2, 2.0, ALU.mult)
    O.tt("dve", cr, mag, c8, ALU.mult)
    O.tt("dve", ci, mag, s8, ALU.mult)
    O.tt("dve", t0, lr_, lr_, ALU.mult)
    O.tt("dve", t1, li_, li_, ALU.mult)
    O.tt("dve", den, t0, t1, ALU.add)
    O.recip(den, den)
    O.ts("dve", nr, cr, -1.0, ALU.add)
    O.tt("dve", t0, nr, lr_, ALU.mult)
    O.tt("dve", t1, ci, li_, ALU.mult)
    O.tt("dve", t0, t0, t1, ALU.add)
    O.tt("dve", fr, t0, den, ALU.mult)
    O.tt("dve", t0, ci, lr_, ALU.mult)
    O.tt("dve", t1, nr, li_, ALU.mult)
    O.tt("dve", t0, t0, t1, ALU.subtract)
    O.tt("dve", fi, t0, den, ALU.mult)

    Ct = [S([128, TB], F32, f"Ct{j}") for j in range(4)]
    St = [S([128, TB], F32, f"St{j}") for j in range(4)]
    Mg = [S([128, TB], F32, f"Mg{j}") for j in range(4)]
    rq = S([128, 16], F32, "rq")
    r512 = S([128, 8], F32, "r512")
    tmpT = S([128, TB // 2], F32, "tmpT")
    for j in range(4):
        cj, sj, ta, tb_ = [V(rq, slice(None), slice(4 * j + i, 4 * j + i + 1)) for i in range(4)]
        O.copy("dve", cj, V(pp, slice(None), slice(4 * 4 + j, 4 * 4 + j + 1)))
        O.copy("dve", sj, V(pp, slice(None), slice(4 * 5 + j, 4 * 5 + j + 1)))
        O.memset("pool", V(Ct[j], slice(None), slice(0, 1)), 1.0)
        O.memset("pool", V(St[j], slice(None), slice(0, 1)), 0.0)
        n = 1
        while n < TB:
            lo_c, lo_s = V(Ct[j], slice(None), slice(0, n)), V(St[j], slice(None), slice(0, n))
            hi_c, hi_s = V(Ct[j], slice(None), slice(n, 2 * n)), V(St[j], slice(None), slice(n, 2 * n))
            tm = V(tmpT, slice(None), slice(0, n))
            O.ts("dve", tm, lo_s, sj, ALU.mult)
            O.stt("dve", hi_c, lo_c, cj, tm, ALU.mult, ALU.subtract)
            O.ts("dve", tm, lo_c, sj, ALU.mult)
            O.stt("dve", hi_s, lo_s, cj, tm, ALU.mult, ALU.add)
            O.tt("dve", ta, cj, cj, ALU.mult)
            O.tt("dve", tb_, sj, sj, ALU.mult)
            O.tt("dve", sj, cj, sj, ALU.mult)
            O.ts("dve", sj, sj, 2.0, ALU.mult)
            O.tt("dve", cj, ta, tb_, ALU.subtract)
            n *= 2
        O.copy("dve", V(r512, slice(None), slice(2 * j, 2 * j + 1)), cj)
        O.copy("dve", V(r512, slice(None), slice(2 * j + 1, 2 * j + 2)), sj)
        O.ts("dve", V(Mg[j]), V(ones), V(pp, slice(None), slice(4 * 3 + j, 4 * 3 + j + 1)), ALU.mult)

    BreT = [S([128, 128], F32, f"BreT{j}") for j in range(4)]
    BimT = [S([128, 128], F32, f"BimT{j}") for j in range(4)]
    Cre = [S([128, 128], F32, f"Cre{j}") for j in range(4)]
    Cim = [S([128, 128], F32, f"Cim{j}") for j in range(4)]
    bst = S([128, 2, 16], F32, "bst")
    padr = S([128, 128], F32, "padr")
    padi = S([128, 128], F32, "padi")
    for j in range(4):
        O.dma(V(bst, slice(None), 0), s5_b_d[0, j])
        O.dma(V(bst, slice(None), 1), s5_b_d[1, j])
        O.memset("pool", V(padr), 0.0)
        O.memset("pool", V(padi), 0.0)
        O.memset("pool", V(Cre[j]), 0.0)
        O.memset("pool", V(Cim[j]), 0.0)
        frj = V(pp, slice(None), slice(4 * 12 + j, 4 * 12 + j + 1))
        fij = V(pp, slice(None), slice(4 * 13 + j, 4 * 13 + j + 1))
        for hf in range(2):
            ps_ = slice(64 * hf, 64 * hf + 64)
            cs_ = slice((2 * j + hf) * 16, (2 * j + hf) * 16 + 16)
            bre, bim = V(bst, ps_, 0), V(bst, ps_, 1)
            tm = V(tmpT, ps_, slice(0, 16))
            O.ts("dve", tm, bim, fij[ps_], ALU.mult)
            O.stt("dve", V(padr, ps_, cs_), bre, frj[ps_], tm, ALU.mult, ALU.subtract)
            O.ts("dve", tm, bre, fij[ps_], ALU.mult)
            O.stt("dve", V(padi, ps_, cs_), bim, frj[ps_], tm, ALU.mult, ALU.add)
            O.dma(V(Cre[j], ps_, cs_), s5_c_d[0, j, 64 * hf:64 * hf + 64, :])
            O.dma(V(Cim[j], ps_, cs_), s5_c_d[1, j, 64 * hf:64 * hf + 64, :])
        for (pad, dst) in ((padr, BreT[j]), (padi, BimT[j])):
            pt = psp.get()
            O.tr(V(pt, slice(None), slice(0, 128)), V(pad), V(ident))
            O.copy("act", V(dst), V(pt, slice(None), slice(0, 128)))
        O.ts("dve", V(Cim[j]), V(Cim[j]), -1.0, ALU.mult)

    Sdn = [S([128, 128], F32, f"Sdn{i}") for i in range(2)]
    Sgl = [S([64, 128], F32, f"Sgl{i}") for i in range(2)]
    O.memset("pool", V(Sdn[0]), 0.0)
    O.memset("pool", V(Sgl[0]), 0.0)
    s5c = [S([128, 2], F32, f"s5c{j}") for j in range(4)]
    for j in range(4):
        O.memset("pool", V(s5c[j]), 0.0)
    s5i = [S([128, 4], F32, f"s5i{j}") for j in range(4)]
    chist = [S([128, 3], F32, f"chist{c}") for c in range(3)]
    for c in range(3):
        O.memset("pool", V(chist[c]), 0.0)

    x_t = P.sb([128, 8, TB], F32, "x_t")
    xb = [Buf(x_t[:, k, :], f"x{k}") for k in range(8)]
    h_t = P.sb([128, 8, TB], BF16, "h_t")
    hb = [Buf(h_t[:, k, :], f"h{k}") for k in range(8)]
    sq = [S([128, TB], F32, f"sq{i}") for i in range(2)]
    rstd = S([128, TB], F32, "rstd")
    cbuf = [S([128, TB + 3], F32, f"cbuf{c}") for c in range(3)]
    cacc = [S([128, TB], F32, f"cacc{c}") for c in range(3)]
    qn = S([128, TB], F32, "qn")
    kn = S([128, TB], F32, "kn")
    tok = [S([128, NWT], F32, f"tok{b}") for b in range(NB)]
    uT = S([128, TB], F32, "uT")
    lrs_b = S([16, TB], F32, "lrs_b")
    gls = S([64, TB], F32, "gls")
    gc = S([64, TB], F32, "gc")
    gEQ = S([64, TB], F32, "gEQ")
    gEK = S([64, TB], F32, "gEK")
    gqe = S([64, TB], F32, "gqe")
    gke = S([64, TB], F32, "gke")
    gkr = S([64, TB], F32, "gkr")
    gqr = S([64, TB], F32, "gqr")
    s5w = [S([128, TB], F32, f"s5w{i}") for i in range(6)]
    yc_sb = S([128, TB], F32, "yc_sb")
    ys_t = P.sb([128, 3, TB], BF16, "ys_t")
    ys = [Buf(ys_t[:, i3, :], f"ys{i3}") for i3 in range(3)]

    def blkbufs(name, shape, n=NB):
        return [S(shape, F32, f"{name}{b}") for b in range(n)]
    sc = [[S([128, 1], F32, f"sc{b}_{i}") for i in range(12)] for b in range(NB)]
    Ug = blkbufs("Ug", [128, 128])
    Eb = blkbufs("Eb", [128, 128])
    ETb = blkbufs("ETb", [128, 128])
    qd = blkbufs("qd", [128, 128])
    bk = blkbufs("bk", [128, 128])
    kdec = blkbufs("kdec", [128, 128])
    bv = blkbufs("bv", [128, 128])
    M = blkbufs("M", [128, 256])
    X = blkbufs("X", [128, 128])
    attnT = blkbufs("attnT", [128, 128])
    un = blkbufs("un", [128, 128])
    wT = blkbufs("wT", [128, 128])
    ub = blkbufs("ub", [128, 128])
    sg = blkbufs("sg", [128, 128])
    yo = blkbufs("yo", [128, 128])
    junk = blkbufs("junk", [128, 128], 2)
    gjunk = blkbufs("gjunk", [128, 128], 2)
    gsc = [[S([64, 1], F32, f"gsc{b}_{i}") for i in range(2)] for b in range(NB)]
    gkdT = blkbufs("gkdT", [64, 128])
    gkd = blkbufs("gkd", [128, 64])
    gaT = blkbufs("gaT", [128, 128])
    gsg = blkbufs("gsg", [128, 128])
    gyo = blkbufs("gyo", [128, 128])
    gss = [[S([128, 1], F32, f"gss{b}_{i}") for i in range(2)] for b in range(NB)]
    ones64 = S([64, 128], F32, "ones64")
    O.memset("pool", V(ones64), 1.0)

    A = slice(None)

    def bsl(b):
        return slice(128 * b, 128 * b + 128)

    def out_norm(o_ps, ss_b, ng, sgate, ydst, jk):
        ssum, rs = ss_b
        O.memset("pool", V(ssum), 0.0)
        O.act(V(jk), o_ps, AF.Square, accum=V(ssum))
        O.act(V(rs), V(ssum), AF.Sqrt, bias=EPS, scale=1.0 / 128)
        O.recip(V(rs), V(rs))
        O.stt("dve", V(ydst), o_ps, V(rs), V(ng), ALU.mult, ALU.mult)
        O.tt("pool", V(ydst), V(ydst), V(sgate), ALU.mult)

    nsb = L // TB
    sdn_i = 0
    sgl_i = 0
    def gen_front(t0f):
        for k in range(8):
            O.dma(V(xb[k]), xsrc(k, t0f))
        ps = psp.get()
        for k in range(8):
            s = sq[k % 2]
            O.act(V(s), V(xb[k]), AF.Square)
            O.mm(V(ps), V(ones, A, slice(0, 128)), V(s), start=(k == 0), stop=(k == 7))
        O.act(V(rstd), V(ps), AF.Sqrt, bias=EPS, scale=1.0 / D)
        O.recip(V(rstd), V(rstd))
        for k in range(8):
            O.stt("dve", V(hb[k]), V(xb[k]), V(c_anorm, A, slice(k, k + 1)), V(rstd), ALU.mult, ALU.mult)
        yield
        for c in range(3):
            ps = psp.get()
            for k in range(8):
                O.mm(V(ps), V(wf, A, k, slice(c * 128, (c + 1) * 128)), V(hb[k]), start=(k == 0), stop=(k == 7))
            O.copy("pool", V(cbuf[c], A, slice(0, 3)), V(chist[c]))
            O.copy("act", V(cbuf[c], A, slice(3, TB + 3)), V(ps))
            O.copy("pool", V(chist[c]), V(cbuf[c], A, slice(TB, TB + 3)))
            yield
        ps = psp.get()
        for k in range(8):
            O.mm(V(ps), V(wf, A, k, slice(384, 512)), V(hb[k]), start=(k == 0), stop=(k == 7))
        O.copy("act", V(uT), V(ps))
        yield
        ps_gq = psp.get()
        for k in range(8):
            O.mm(V(ps_gq, slice(0, 64)), V(wf, A, k, slice(512, 576)), V(hb[k]), start=(k == 0), stop=(k == 7))
        O.copy("act", V(gqr), V(ps_gq, slice(0, 64)))
        ps_gk = psp.get()
        for k in range(8):
            O.mm(V(ps_gk, slice(0, 64)), V(wf, A, k, slice(576, 640)), V(hb[k]), start=(k == 0), stop=(k == 7))
        O.copy("act", V(gkr), V(ps_gk, slice(0, 64)))
        yield
        ps = psp.get()
        for k in range(8):
            O.mm(V(ps, slice(0, 16)), V(wf, A, k, slice(640, 656)), V(hb[k]), start=(k == 0), stop=(k == 7))
        O.copy("act", V(lrs_b), V(ps, slice(0, 16)))
        yield
        for b in range(NB):
            ps = psp.get()
            for k in range(8):
                O.mm(V(ps, A, slice(0, NWT)), V(hb[k], A, bsl(b)), V(wt, A, k), start=(k == 0), stop=(k == 7))
            O.copy("act", V(tok[b]), V(ps, A, slice(0, NWT)))
            yield
        for c in range(3):
            O.ts("dve", V(cacc[c]), V(cbuf[c], A, slice(0, TB)), V(dn_cw, A, slice(4 * c, 4 * c + 1)), ALU.mult)
            for t in range(1, 4):
                O.stt("dve", V(cacc[c]), V(cbuf[c], A, slice(t, t + TB)),
                      V(dn_cw, A, slice(4 * c + t, 4 * c + t + 1)), V(cacc[c]), ALU.mult, ALU.add)
            O.act(V(cacc[c]), V(cacc[c]), AF.Silu)
            yield
        for c, dst, scl in ((0, qn, 128 ** -0.5), (1, kn, 1.0)):
            s = sq[c]
            O.act(V(s), V(cacc[c]), AF.Square)
            ps = psp.get()
            O.mm(V(ps), V(ones, A, slice(0, 128)), V(s))
            O.act(V(s), V(ps), AF.Sqrt, bias=EPS, scale=1.0)
            O.recip(V(s), V(s))
            O.stt("dve", V(dst), V(cacc[c]), scl, V(s), ALU.mult, ALU.mult)
            yield

    for _ in gen_front(0):
        pass
    for sb_i in range(nsb):
        t0_ = sb_i * TB
        dn_flag = [False]
        vs = cacc[2]

        def gen_s5():
            yps = yps_ded
            for j in range(4):
                pr = psp.get()
                pi = psp.get()
                O.mm(V(pr), V(BreT[j]), V(uT))
                O.mm(V(pi), V(BimT[j]), V(uT))
                w0, w1, w2, w3, w4, w5 = [V(s5w[i]) for i in range(6)]
                O.tt("dve", w0, V(pr), V(Ct[j]), ALU.mult)
                O.tt("dve", w1, V(pi), V(St[j]), ALU.mult)
                O.tt("pool", w0, w0, w1, ALU.add)
                O.tt("dve", w2, V(pi), V(Ct[j]), ALU.mult)
                O.tt("dve", w3, V(pr), V(St[j]), ALU.mult)
                O.tt("pool", w2, w2, w3, ALU.subtract)
                cr_, ci_ = V(s5c[j], A, slice(0, 1)), V(s5c[j], A, slice(1, 2))
                c5, s5_ = V(r512, A, slice(2 * j, 2 * j + 1)), V(r512, A, slice(2 * j + 1, 2 * j + 2))
                ir, ii, ta, tb_ = [V(s5i[j], A, slice(i, i + 1)) for i in range(4)]
                O.tt("dve", ta, ci_, s5_, ALU.mult)
                O.stt("dve", ir, cr_, c5, ta, ALU.mult, ALU.subtract)
                O.tt("dve", tb_, cr_, s5_, ALU.mult)
                O.stt("dve", ii, ci_, c5, tb_, ALU.mult, ALU.add)
                O.scan("dve", w4, V(Mg[j]), w0, ir)
                O.scan("dve", w5, V(Mg[j]), w2, ii)
                O.copy("pool", cr_, V(s5w[4], A, slice(TB - 1, TB)))
                O.copy("pool", ci_, V(s5w[5], A, slice(TB - 1, TB)))
                O.tt("dve", w0, w4, V(Ct[j]), ALU.mult)
                O.tt("pool", w1, w5, V(St[j]), ALU.mult)
                O.tt("dve", w0, w0, w1, ALU.subtract)
                O.tt("pool", w2, w5, V(Ct[j]), ALU.mult)
                O.tt("dve", w3, w4, V(St[j]), ALU.mult)
                O.tt("pool", w2, w2, w3, ALU.add)
                O.mm(V(yps), V(Cre[j]), w0, start=(j == 0), stop=False)
                O.mm(V(yps), V(Cim[j]), w2, start=False, stop=(j == 3))
                yield
            O.stt("dve", V(yc_sb), V(uT), V(s5d), V(yps), ALU.mult, ALU.add)
            O.act(V(ys[2]), V(yc_sb), AF.Gelu_apprx_tanh)

        def gen_gla():
            nonlocal sgl_i
            ps = psp.get()
            O.mm(V(ps, slice(0, 64)), V(gla_w2), V(lrs_b))
            O.act(V(gls), V(ps, slice(0, 64)), AF.Exp, scale=-1.0, bias=V(nb2))
            O.act(V(gls), V(gls), AF.Ln, bias=1.0)
            for b in range(NB):
                O.scan("dve", V(gc, A, bsl(b)), V(ones64), V(gls, A, bsl(b)), 0.0)
            O.act(V(gEQ), V(gc), AF.Exp, scale=-1.0 / 16, bias=math.log(1.0 / 8))
            O.act(V(gEK), V(gc), AF.Exp, scale=1.0 / 16)
            O.tt("dve", V(gqe), V(gqr), V(gEQ), ALU.mult)
            O.tt("dve", V(gke), V(gkr), V(gEK), ALU.mult)
            yield
            for b in range(NB):
                nbl, gend = V(gsc[b][0]), V(gsc[b][1])
                O.ts("dve", nbl, V(gc, A, slice(128 * b + 127, 128 * b + 128)), -1.0 / 16, ALU.mult)
                O.act(gend, nbl, AF.Exp)
                O.act(V(gkdT[b]), V(gc, A, bsl(b)), AF.Exp, scale=1.0 / 16, bias=nbl)
                O.tt("dve", V(gkdT[b]), V(gkdT[b]), V(gkr, A, bsl(b)), ALU.mult)
            yield
            for b in range(NB):
                pt = psp.get()
                O.tr(V(pt, A, slice(0, 64)), V(gkdT[b]), V(ident, slice(0, 64), slice(0, 64)))
                O.copy("act", V(gkd[b]), V(pt, A, slice(0, 64)))
                pa = psp.get()
                O.mm(V(pa, A, slice(0, 128)), V(gke, A, bsl(b)), V(gqe, A, bsl(b)))
                O.tt("dve", V(gaT[b]), V(pa, A, slice(0, 128)), V(U), ALU.mult)
                O.act(V(gsg[b]), V(tok[b], A, slice(258, 386)), AF.Silu)
                yield
            for b in range(NB):
                gv = V(tok[b], A, slice(130, 258))
                So, Sn = Sgl[sgl_i % 2], Sgl[(sgl_i + 1) % 2]
                sgl_i += 1
                po = psp.get()
                O.mm(V(po, A, slice(0, 128)), V(gqe, A, bsl(b)), V(So), start=True, stop=False)
                O.mm(V(po, A, slice(0, 128)), V(gaT[b]), gv, start=False, stop=True)
                pd = psp.get()
                O.mm(V(pd, slice(0, 64), slice(0, 128)), V(gkd[b]), gv)
                O.stt("dve", V(Sn), V(So), V(gsc[b][1]), V(pd, slice(0, 64), slice(0, 128)), ALU.mult, ALU.add)
                out_norm(V(po, A, slice(0, 128)), gss[b], gla_ng, gsg[b], gyo[b], gjunk[b % 2])
                ptr = psp.get()
                O.tr(V(ptr, A, slice(0, 128)), V(gyo[b]), V(ident))
                O.copy("act", V(ys[1], A, bsl(b)), V(ptr, A, slice(0, 128)))
                yield

        def gen_dn():
            nonlocal sdn_i
            pAs = []
            for b in range(NB):
                beta, glog, gam, glast, ngam, egam, bg, edl, gend, tmp = [V(sc[b][i]) for i in range(10)]
                O.act(beta, V(tok[b], A, slice(128, 129)), AF.Sigmoid)
                O.act(tmp, V(tok[b], A, slice(129, 130)), AF.Exp, bias=V(dn_sc, A, slice(1, 2)))
                O.act(tmp, tmp, AF.Ln, bias=1.0)
                O.tt("dve", glog, tmp, V(negA), ALU.mult)
                O.ts("dve", V(Ug[b]), V(U), glog, ALU.mult)
                pA = psp.get()
                pAs.append(pA)
                O.mm(V(pA, A, slice(0, 128)), V(ones, A, slice(0, 128)), V(Ug[b]))
                O.mm(V(pA, A, slice(128, 129)), V(U), glog)
                O.mm(V(pA, A, slice(129, 130)), V(ones, A, slice(0, 128)), glog)
                O.copy("dve", gam, V(pA, A, slice(128, 129)))
                O.copy("dve", glast, V(pA, A, slice(129, 130)))
                O.ts("dve", ngam, gam, -1.0, ALU.mult)
                O.act(egam, gam, AF.Exp)
                O.tt("dve", bg, beta, egam, ALU.mult)
                O.act(edl, gam, AF.Exp, scale=-1.0, bias=glast)
                O.act(gend, glast, AF.Exp)
                gbc = V(pA, A, slice(0, 128))
                O.stt("dve", V(Eb[b]), gbc, -1.0, V(mneg), ALU.mult, ALU.add)
                O.act(V(Eb[b]), V(Eb[b]), AF.Exp, bias=gam)
                O.tt("dve", V(ETb[b]), gbc, V(mnegT), ALU.add)
                O.act(V(ETb[b]), V(ETb[b]), AF.Exp, bias=ngam)
                O.act(V(qd[b]), gbc, AF.Exp)
                O.tt("dve", V(qd[b]), V(qd[b]), V(qn, A, bsl(b)), ALU.mult)
                O.act(V(sg[b]), V(tok[b], A, slice(0, 128)), AF.Silu)
                yield
            for b in range(NB):
                beta, bg, edl = V(sc[b][0]), V(sc[b][6]), V(sc[b][7])
                pt = psp.get()
                O.tr(V(pt, A, slice(0, 128)), V(kn, A, bsl(b)), V(ident))
                O.tr(V(pt, A, slice(128, 256)), V(vs, A, bsl(b)), V(ident))
                O.ts("dve", V(bk[b]), V(pt, A, slice(0, 128)), bg, ALU.mult)
                O.act(V(kdec[b]), V(pt, A, slice(0, 128)), AF.Identity, scale=edl)
                O.ts("dve", V(bv[b]), V(pt, A, slice(128, 256)), beta, ALU.mult)
            yield
            for b in range(NB):
                beta = V(sc[b][0])
                pk = psp.get()
                O.mm(V(pk, A, slice(0, 128)), V(kn, A, bsl(b)), V(kn, A, bsl(b)))
                O.mm(V(pk, A, slice(128, 256)), V(kn, A, bsl(b)), V(qn, A, bsl(b)))
                O.stt("dve", V(M[b], A, slice(0, 128)), V(pk, A, slice(0, 128)), beta, V(Eb[b]), ALU.mult, ALU.mult)
                O.tt("pool", V(M[b], A, slice(0, 128)), V(M[b], A, slice(0, 128)), V(nLs), ALU.mult)
                O.tt("dve", V(attnT[b]), V(pk, A, slice(128, 256)), V(ETb[b]), ALU.mult)
            yield
            for b in range(NB):
                pt = psp.get()
                O.tr(V(pt, A, slice(0, 128)), V(M[b], A, slice(0, 128)), V(ident))
                O.copy("act", V(M[b], A, slice(128, 256)), V(pt, A, slice(0, 128)))
                O.tt("dve", V(X[b]), V(pt, A, slice(0, 128)), V(ident), ALU.add)
            dn_flag[0] = True
            yield
            for lev in range(1, 8):
                pls = []
                for b in range(NB):
                    pl = psp.get()
                    pls.append(pl)
                    Mv, MTv = V(M[b], A, slice(0, 128)), V(M[b], A, slice(128, 256))
                    if lev <= 6:
                        O.mm(V(pl, A, slice(0, 128)), MTv, Mv)
                        O.mm(V(pl, A, slice(128, 256)), Mv, MTv)
                    if lev >= 2:
                        O.mm(V(pl, A, slice(256, 384)), Mv, V(X[b]))
                for b in range(NB):
                    pl = pls[b]
                    if lev <= 6:
                        O.copy("act", V(M[b]), V(pl, A, slice(0, 256)))
                    if lev >= 2:
                        O.tt("dve", V(X[b]), V(X[b]), V(pl, A, slice(256, 384)), ALU.add)
                yield
            for b in range(NB):
                pe_ = psp.get()
                O.mm(V(pe_, A, slice(0, 128)), V(X[b]), V(bv[b]))
                O.mm(V(pe_, A, slice(128, 256)), V(bk[b]), V(X[b]))
                O.copy("act", V(un[b]), V(pe_, A, slice(0, 128)))
                O.copy("act", V(wT[b]), V(pe_, A, slice(128, 256)))
            yield
            for b in range(NB):
                So, Sn = Sdn[sdn_i % 2], Sdn[(sdn_i + 1) % 2]
                sdn_i += 1
                pw = psp.get()
                O.mm(V(pw, A, slice(0, 128)), V(wT[b]), V(So))
                O.tt("dve", V(ub[b]), V(un[b]), V(pw, A, slice(0, 128)), ALU.subtract)
                po = psp.get()
                O.mm(V(po, A, slice(0, 128)), V(qd[b]), V(So), start=True, stop=False)
                O.mm(V(po, A, slice(0, 128)), V(attnT[b]), V(ub[b]), start=False, stop=True)
                pd = psp.get()
                O.mm(V(pd, A, slice(0, 128)), V(kdec[b]), V(ub[b]))
                O.stt("dve", V(Sn), V(So), V(sc[b][8]), V(pd, A, slice(0, 128)), ALU.mult, ALU.add)
                out_norm(V(po, A, slice(0, 128)), (sc[b][10], sc[b][11]), dn_ng, sg[b], yo[b], junk[b % 2])
                ptr = psp.get()
                O.tr(V(ptr, A, slice(0, 128)), V(yo[b]), V(ident))
                O.copy("act", V(ys[0], A, bsl(b)), V(ptr, A, slice(0, 128)))
                yield
        g_dn, g_s5, g_gla = gen_dn(), gen_s5(), gen_gla()
        tasks = [g_dn, g_s5, g_gla, g_gla]
        nf = gen_front((sb_i + 1) * TB) if sb_i + 1 < nsb else None
        started = False
        while tasks:
            if nf is not None and not started and dn_flag[0] and g_s5 not in tasks and g_gla not in tasks:
                tasks.append(nf)
                started = True
            for g in list(tasks):
                if g not in tasks:
                    continue
                try:
                    next(g)
                except StopIteration:
                    while g in tasks:
                        tasks.remove(g)
        if nf is not None and not started:
            for _ in nf:
                pass
        for i3 in range(3):
            yout(i3, t0_, ys[i3])
        after_sb(t0_)


def emit_dense(nc, P, TOK, last, sfx, src, NT=512):
    def din(name, shape):
        return nc.dram_tensor(name + sfx, list(shape), F32, kind="ExternalInput").ap()
    HT = TOK + 2
    pT = din("pT", [256, TOK])
    w_gate = din("w_gate", [D, 3 * D])
    b_gate = din("b_gate", [128, 24])
    w_branch = din("w_branch", [1536, D])
    w_o = din("w_o", [D, D])
    w_glu = din("w_glu", [512, 512])
    b_glu = din("b_glu", [128, 4])
    norms = din("norms", [128, 32])
    w_up = din("w_up", [D, 2 * DFF])
    cw = din("cw", [128, NF * 3])
    cb = din("cb", [128, NF])
    w_down = din("w_down", [DFF, D])
    w_pg = din("w_pg", [D, D])
    w_pp = din("w_pp", [256, D])
    psp = PsPool(P)
    ones = P.sbuf([128, 128], F32, "ones")
    P.op("pool", lambda e: e.memset(ones.ap, 1.0), writes=[ones])
    c_bg = P.sbuf([128, 24], F32, "c_bg")
    c_bglu = P.sbuf([128, 4], F32, "c_bglu")
    c_norm = P.sbuf([128, 32], F32, "c_norm")
    c_cw = P.sbuf([128, NF * 3], F32, "c_cw")
    c_cb = P.sbuf([128, NF], F32, "c_cb")
    hmask = P.sbuf([128, 1], F32, "hmask")
    for t, srcd in ((c_bg, b_gate), (c_bglu, b_glu), (c_norm, norms), (c_cw, cw), (c_cb, cb), (hmask, src["hmask"])):
        P.dma(t.ap, srcd, writes=[t])

    NSLAB = 5
    slab_t = [P.sb([128, 8, 1024], BF16, f"slab{i}") for i in range(NSLAB)]
    slabs = [Buf(t[:], f"slab{i}") for i, t in enumerate(slab_t)]
    slab_i = [0]

    def load_w(src, r0, nk, c0, ncol):
        s = slabs[slab_i[0] % NSLAB]
        slab_i[0] += 1
        view = s.ap[:, 0:nk, 0:ncol]
        srcv = src[r0:r0 + nk * 128, c0:c0 + ncol].rearrange("(k p) n -> p k n", p=128)
        P.dma(view, srcv, writes=[s], q="pool")
        return s

    x_t = P.sb([128, 8, NT], F32, "x_t")
    xb = [Buf(x_t[:, k, :], f"x{k}") for k in range(8)]
    h_t = P.sb([128, 8, NT], BF16, "h_t")
    hb = [Buf(h_t[:, k, :], f"h{k}") for k in range(8)]
    y_t = P.sb([128, 12, NT], BF16, "y_t")
    yb = [Buf(y_t[:, k, :], f"y{k}") for k in range(12)]
    yc_t = P.sb([128, 4, NT], BF16, "yc_t")
    ycb = [Buf(yc_t[:, k, :], f"yc{k}") for k in range(4)]
    m_t = P.sb([128, 8, NT], F32, "m_t")
    mb = [Buf(m_t[:, k, :], f"m{k}") for k in range(8)]
    mbf_t = P.sb([128, 8, NT], BF16, "mbf_t")
    mbfb = [Buf(mbf_t[:, k, :], f"mbf{k}") for k in range(8)]
    a_t = P.sb([128, NF, NT], BF16, "a_t")
    ab = [Buf(a_t[:, k, :], f"a{k}") for k in range(NF)]
    p_t = P.sb([128, 2, NT], BF16, "p_t")
    pb = [Buf(p_t[:, k, :], f"p{k}") for k in range(2)]
    hist_t = P.sb([128, NF, 2], F32, "hist_t")
    histb = [Buf(hist_t[:, k, :], f"hist{k}") for k in range(NF)]
    sq_t = P.sb([128, 2, NT], F32, "sq_t")
    sqb = [Buf(sq_t[:, k, :], f"sq{k}") for k in range(2)]
    rstd = P.sbuf([128, NT], F32, "rstd")
    NG = 3
    g_t = P.sb([128, NG, NT], F32, "g_t")
    gtb = [Buf(g_t[:, k, :], f"gt{k}") for k in range(NG)]
    gi = [0]
    gb_t = P.sb([128, 2, NT + 2], F32, "gb_t")
    gbb = [Buf(gb_t[:, k, :], f"gb{k}") for k in range(2)]
    cv_t = P.sb([128, 2, NT], F32, "cv_t")
    cvb = [Buf(cv_t[:, k, :], f"cv{k}") for k in range(2)]
    ob = mb

    def rmsnorm(w, ncol, outs):
        ps = psp.get()
        for k in range(8):
            s = sqb[k % 2]
            P.op("act", lambda e, s=s, k=k: e.activation(s.ap[:, :w], xb[k].ap[:, :w], AF.Square),
                 reads=[xb[k]], writes=[s])
            P.mm(ps.ap[:, :w], ones.ap, s.ap[:, :w], start=(k == 0), stop=(k == 7),
                 reads=[ones, s], writes=[ps])
        P.op("act", lambda e: e.activation(rstd.ap[:, :w], ps.ap[:, :w], AF.Sqrt, bias=EPS, scale=1.0 / D),
             reads=[ps], writes=[rstd])
        P.op("dve", lambda e: e.reciprocal(rstd.ap[:, :w], rstd.ap[:, :w]), reads=[rstd], writes=[rstd])
        for k in range(8):
            P.op("dve", lambda e, k=k: e.scalar_tensor_tensor(
                outs[k].ap[:, :w], xb[k].ap[:, :w], c_norm.ap[:, ncol + k:ncol + k + 1], rstd.ap[:, :w],
                op0=ALU.mult, op1=ALU.mult), reads=[xb[k], c_norm, rstd], writes=[outs[k]])

    def tile(c0, w, halo):
        for k in range(8):
            src["x"](P, xb[k], k, c0, w, halo)
        for k in range(12):
            src["y"](P, yb[k], k, c0, w, halo)
        if halo:
            for k in range(8):
                P.op("dve", lambda e, k=k: e.tensor_scalar(xb[k].ap[:, :w], xb[k].ap[:, :w], hmask.ap[:, 0:1], None,
                                                           op0=ALU.mult), reads=[xb[k], hmask], writes=[xb[k]])
            for k in range(12):
                P.op("dve", lambda e, k=k: e.tensor_scalar(yb[k].ap[:, :w], yb[k].ap[:, :w], hmask.ap[:, 0:1], None,
                                                           op0=ALU.mult), reads=[yb[k], hmask], writes=[yb[k]])
        if not halo:
            for k in range(2):
                P.dma(pb[k].ap[:, :w], pT[k * 128:(k + 1) * 128, c0 - 2:c0 - 2 + w], writes=[pb[k]], q="pool")
        rmsnorm(w, 0, hb)
        U = load_w(w_glu, 0, 4, 0, 512)
        for m in range(4):
            ps = psp.get()
            for k in range(4):
                P.mm(ps.ap[:, :w], U.ap[:, k, m * 128:(m + 1) * 128], yb[8 + k].ap[:, :w],
                     start=(k == 0), stop=(k == 3), reads=[U, yb[8 + k]], writes=[ps])
            g = gtb[gi[0] % NG]; gi[0] += 1
            P.op("act", lambda e, g=g, ps=ps, m=m: e.activation(g.ap[:, :w], ps.ap[:, :w], AF.Sigmoid,
                                                             bias=c_bglu.ap[:, m:m + 1]),
                 reads=[ps, c_bglu], writes=[g])
            P.op("dve", lambda e, g=g, m=m: e.tensor_tensor(ycb[m].ap[:, :w], yb[8 + m].ap[:, :w], g.ap[:, :w],
                                                          op=ALU.mult),
                 reads=[yb[8 + m], g], writes=[ycb[m]])
        for i in range(3):
            G = load_w(w_gate, 0, 8, i * D, D)
            B = load_w(w_branch, i * 512, 4, 0, D)
            ysrc = [yb[0], yb[1], yb[2], yb[3]] if i == 0 else ([yb[4], yb[5], yb[6], yb[7]] if i == 1 else ycb)
            for m in range(8):
                pg = psp.get()
                for k in range(8):
                    P.mm(pg.ap[:, :w], G.ap[:, k, m * 128:(m + 1) * 128], hb[k].ap[:, :w],
                         start=(k == 0), stop=(k == 7), reads=[G, hb[k]], writes=[pg])
                g = gtb[gi[0] % NG]; gi[0] += 1
                P.op("act", lambda e, g=g, pg=pg, i=i, m=m: e.activation(
                    g.ap[:, :w], pg.ap[:, :w], AF.Sigmoid, bias=c_bg.ap[:, i * 8 + m:i * 8 + m + 1]),
                    reads=[pg, c_bg], writes=[g])
                pp = psp.get()
                for k in range(4):
                    P.mm(pp.ap[:, :w], B.ap[:, k, m * 128:(m + 1) * 128], ysrc[k].ap[:, :w],
                         start=(k == 0), stop=(k == 3), reads=[B, ysrc[k]], writes=[pp])
                if i == 0:
                    P.op("dve", lambda e, g=g, pp=pp, m=m: e.tensor_tensor(
                        mb[m].ap[:, :w], pp.ap[:, :w], g.ap[:, :w], op=ALU.mult),
                        reads=[pp, g], writes=[mb[m]])
                else:
                    P.op("dve", lambda e, g=g, pp=pp, m=m: e.tensor_tensor(
                        g.ap[:, :w], pp.ap[:, :w], g.ap[:, :w], op=ALU.mult),
                        reads=[pp, g], writes=[g])
                    dst = mb[m] if i == 1 else mbfb[m]
                    P.op("pool", lambda e, g=g, m=m, dst=dst: e.tensor_tensor(
                        dst.ap[:, :w], mb[m].ap[:, :w], g.ap[:, :w], op=ALU.add),
                        reads=[mb[m], g], writes=[dst])
        O = load_w(w_o, 0, 8, 0, D)
        for m in range(8):
            ps = psp.get()
            for k in range(8):
                P.mm(ps.ap[:, :w], O.ap[:, k, m * 128:(m + 1) * 128], mbfb[k].ap[:, :w],
                     start=(k == 0), stop=(k == 7), reads=[O, mbfb[k]], writes=[ps])
            P.op("dve", lambda e, ps=ps, m=m: e.tensor_tensor(xb[m].ap[:, :w], xb[m].ap[:, :w], ps.ap[:, :w],
                                                            op=ALU.add),
                 reads=[xb[m], ps], writes=[xb[m]])
        rmsnorm(w, 8, hb)
        for j0 in range(0, NF, 8):
            nj = min(8, NF - j0)
            Wg = load_w(w_up, 0, 8, j0 * 128, nj * 128)
            Wu = None if halo else load_w(w_up, 0, 8, DFF + j0 * 128, nj * 128)
            for jj in range(nj):
                j = j0 + jj
                pg = psp.get()
                for k in range(8):
                    P.mm(pg.ap[:, :w], Wg.ap[:, k, jj * 128:(jj + 1) * 128], hb[k].ap[:, :w],
                         start=(k == 0), stop=(k == 7), reads=[Wg, hb[k]], writes=[pg])
                if halo:
                    P.op("act", lambda e, pg=pg, j=j: e.activation(histb[j].ap, pg.ap[:, 0:2], AF.Identity),
                         reads=[pg], writes=[histb[j]])
                    continue
                pu = psp.get()
                for k in range(8):
                    P.mm(pu.ap[:, :w], Wu.ap[:, k, jj * 128:(jj + 1) * 128], hb[k].ap[:, :w],
                         start=(k == 0), stop=(k == 7), reads=[Wu, hb[k]], writes=[pu])
                gb = gbb[j % 2]
                cv = cvb[j % 2]
                P.op("pool", lambda e, gb=gb, j=j: e.tensor_copy(gb.ap[:, 0:2], histb[j].ap),
                     reads=[histb[j]], writes=[gb])
                P.op("act", lambda e, gb=gb, pg=pg: e.activation(gb.ap[:, 2:2 + w], pg.ap[:, :w], AF.Identity),
                     reads=[pg], writes=[gb])
                P.op("pool", lambda e, gb=gb, j=j: e.tensor_copy(histb[j].ap, gb.ap[:, w:w + 2]),
                     reads=[gb], writes=[histb[j]])
                P.op("dve", lambda e, gb=gb, cv=cv, j=j: e.tensor_scalar(
                    cv.ap[:, :w], gb.ap[:, 0:w], c_cw.ap[:, 3 * j:3 * j + 1], c_cb.ap[:, j:j + 1],
                    op0=ALU.mult, op1=ALU.add), reads=[gb, c_cw, c_cb], writes=[cv])
                for t in (1, 2):
                    P.op("dve", lambda e, gb=gb, cv=cv, j=j, t=t: e.scalar_tensor_tensor(
                        cv.ap[:, :w], gb.ap[:, t:t + w], c_cw.ap[:, 3 * j + t:3 * j + t + 1], cv.ap[:, :w],
                        op0=ALU.mult, op1=ALU.add), reads=[gb, c_cw, cv], writes=[cv])
                P.op("act", lambda e, cv=cv: e.activation(cv.ap[:, :w], cv.ap[:, :w], AF.Gelu_apprx_tanh),
                     reads=[cv], writes=[cv])
                P.op("dve", lambda e, cv=cv, pu=pu, j=j: e.tensor_tensor(
                    ab[j].ap[:, :w], pu.ap[:, :w], cv.ap[:, :w], op=ALU.mult),
                    reads=[pu, cv], writes=[ab[j]])
        if halo:
            return
        Ds = [load_w(w_down, j0 * 128, min(8, NF - j0), 0, D) for j0 in range(0, NF, 8)]
        for m in range(8):
            ps = psp.get()
            for j in range(NF):
                Dj = Ds[j // 8]
                P.mm(ps.ap[:, :w], Dj.ap[:, j % 8, m * 128:(m + 1) * 128], ab[j].ap[:, :w],
                     start=(j == 0), stop=(j == NF - 1), reads=[Dj, ab[j]], writes=[ps])
            P.op("dve", lambda e, ps=ps, m=m: e.tensor_tensor(xb[m].ap[:, :w], xb[m].ap[:, :w], ps.ap[:, :w],
                                                            op=ALU.add),
                 reads=[xb[m], ps], writes=[xb[m]])
        rmsnorm(w, 16, hb)
        PG = load_w(w_pg, 0, 8, 0, D)
        PP = load_w(w_pp, 0, 2, 0, D)
        for m in range(8):
            pg = psp.get()
            for k in range(8):
                P.mm(pg.ap[:, :w], PG.ap[:, k, m * 128:(m + 1) * 128], hb[k].ap[:, :w],
                     start=(k == 0), stop=(k == 7), reads=[PG, hb[k]], writes=[pg])
            g = gtb[gi[0] % NG]; gi[0] += 1
            P.op("act", lambda e, g=g, pg=pg: e.activation(g.ap[:, :w], pg.ap[:, :w], AF.Sigmoid),
                 reads=[pg], writes=[g])
            pp = psp.get()
            for k in range(2):
                P.mm(pp.ap[:, :w], PP.ap[:, k, m * 128:(m + 1) * 128], pb[k].ap[:, :w],
                     start=(k == 0), stop=(k == 1), reads=[PP, pb[k]], writes=[pp])
            P.op("dve", lambda e, g=g, pp=pp: e.tensor_tensor(g.ap[:, :w], pp.ap[:, :w], g.ap[:, :w], op=ALU.mult),
                 reads=[pp, g], writes=[g])
            P.op("pool", lambda e, g=g, m=m: e.tensor_tensor(xb[m].ap[:, :w], xb[m].ap[:, :w], g.ap[:, :w],
                                                           op=ALU.add),
                 reads=[xb[m], g], writes=[xb[m]])
        if last:
            rmsnorm(w, 24, ob)
            src_o = ob
        else:
            src_o = xb
        for m in range(8):
            src["out"](P, src_o[m], m, c0 - 2, w)
        src["after_tile"](P, c0 - 2)

    tile(0, 2, True)
    for t0 in range(0, TOK, NT):
        tile(2 + t0, min(NT, TOK - t0), False)


IN_OFF = dict(q=0, k=512, v=1024, b=1536, a=1540, g=1544, gq=2056, gk=2312, gv=2568, lr=3080, gr=3096, s5=3608)


def mixer_inputs(layer, hd, x_b, W):
    i = layer
    w_in = W["w_in"][i]
    o = IN_OFF
    cols_f = np.concatenate([
        np.arange(o["q"] + hd * 128, o["q"] + hd * 128 + 128),
        np.arange(o["k"] + hd * 128, o["k"] + hd * 128 + 128),
        np.arange(o["v"] + hd * 128, o["v"] + hd * 128 + 128),
        np.arange(o["s5"] + hd * 128, o["s5"] + hd * 128 + 128),
        np.arange(o["gq"] + hd * 64, o["gq"] + hd * 64 + 64),
        np.arange(o["gk"] + hd * 64, o["gk"] + hd * 64 + 64),
        np.arange(o["lr"], o["lr"] + 16)])
    cols_t = np.concatenate([
        np.arange(o["g"] + hd * 128, o["g"] + hd * 128 + 128),
        [o["b"] + hd], [o["a"] + hd],
        np.arange(o["gv"] + hd * 128, o["gv"] + hd * 128 + 128),
        np.arange(o["gr"] + hd * 128, o["gr"] + hd * 128 + 128)])
    cwv = W["dn_conv_w"][i]
    dn_cw = np.stack([cwv[:, c * 512 + hd * 128: c * 512 + hd * 128 + 128].T for c in range(3)], axis=1)
    rep = lambda v: np.ascontiguousarray(np.broadcast_to(v[None, :], (128, v.shape[0])))
    g0 = hd * 8
    lam = np.zeros((128, 12), np.float32)
    sb_ = np.zeros((2, 4, 128, 16), np.float32)
    sc_ = np.zeros((2, 4, 128, 16), np.float32)
    for j in range(4):
        for hf in range(2):
            g = g0 + 2 * j + hf
            lam[64 * hf:64 * hf + 64, j] = W["s5_lam_re"][i][g]
            lam[64 * hf:64 * hf + 64, 4 + j] = W["s5_lam_im"][i][g]
            lam[64 * hf:64 * hf + 64, 8 + j] = W["s5_log_step"][i][g]
            sb_[0, j, 64 * hf:64 * hf + 64] = W["s5_b_re"][i][g]
            sb_[1, j, 64 * hf:64 * hf + 64] = W["s5_b_im"][i][g]
            sc_[0, j, 64 * hf:64 * hf + 64] = W["s5_c_re"][i][g].T
            sc_[1, j, 64 * hf:64 * hf + 64] = W["s5_c_im"][i][g].T
    return {
        "anorm": np.ascontiguousarray(W["attn_norm"][i].reshape(8, 128).T),
        "wf": np.ascontiguousarray(w_in[:, cols_f]), "wt": np.ascontiguousarray(w_in[:, cols_t]),
        "dn_sc": np.ascontiguousarray(np.stack([np.full(128, W["dn_a_log"][i][hd], np.float32),
                                               np.full(128, W["dn_dt_bias"][i][hd], np.float32)], axis=1)),
        "dn_cw": np.ascontiguousarray(dn_cw.reshape(128, 12)),
        "dn_ng": rep(W["dn_norm"][i]), "gla_ng": rep(W["gla_norm"][i]),
        "gla_w2": np.ascontiguousarray(W["gla_w2"][i][:, hd * 64:hd * 64 + 64]),
        "gla_b2": np.ascontiguousarray(W["gla_b2"][i][hd * 64:hd * 64 + 64].reshape(64, 1)),
        "s5_lam": lam, "s5_b": sb_, "s5_c": sc_,
        "s5_d": np.ascontiguousarray(W["s5_d"][i][hd * 128:hd * 128 + 128].reshape(128, 1)),
    }


def dense_inputs(layer, x_seg, y_seg, p_seg, W):
    i = layer
    f = np.float32
    col = lambda v: np.ascontiguousarray(v.reshape(-1, 128).T)
    norms = np.concatenate([col(W["attn_norm"][i]), col(W["ffn_norm"][i]), col(W["ple_norm"][i]),
                            col(W["final_norm"])], axis=1)
    cw = np.ascontiguousarray(W["ffn_conv_w"][i].reshape(3, NF, 128).transpose(2, 1, 0).reshape(128, NF * 3))
    return {
        "pT": np.ascontiguousarray(p_seg.T),
        "w_gate": W["w_gate"][i], "b_gate": col(W["b_gate"][i]),
        "w_branch": np.ascontiguousarray(W["w_branch"][i].reshape(1536, D)),
        "w_o": W["w_o"][i], "w_glu": W["s5_w_glu"][i], "b_glu": col(W["s5_b_glu"][i]),
        "norms": np.ascontiguousarray(norms), "w_up": W["w_up"][i], "cw": cw, "cb": col(W["ffn_conv_b"][i]),
        "w_down": W["w_down"][i], "w_pg": W["w_ple_gate"][i], "w_pp": W["w_ple_proj"][i],
    }


import os
NOCOLL = os.environ.get('FUSE_NOCOLL') == '1'
NODYN = os.environ.get('FUSE_NODYN') == '1'


def build_fused(L=SEQ):
    nc = bass.Bass("TRN2", target_bir_lowering=False, num_devices=NCORES)
    TOK = L // 4
    CH = 1024
    NCH = L // CH
    CPS = TOK // CH
    NT8 = TOK // 512
    assert CPS >= 1 and TOK % 512 == 0
    xT_b = nc.dram_tensor("xT_b", [D, L], F32, kind="ExternalInput").ap()
    xs0 = nc.dram_tensor("xs0", [D, TOK + 2], F32, kind="ExternalInput").ap()
    hmask_d = nc.dram_tensor("hmask", [128, 1], F32, kind="ExternalInput").ap()
    oT = nc.dram_tensor("oT", [D, TOK], F32, kind="ExternalOutput").ap()
    yloc = [[nc.dram_tensor(f"yloc{l}_{c}", [384, CH], BF16).ap() for c in range(NCH)] for l in range(2)]
    yall_t = [nc.dram_tensor(f"yall{l}", [NCH * 1536, CH], BF16).ap() for l in range(2)]
    x1c = [[nc.dram_tensor(f"x1c_{t}_{h}", [512, 512], F32).ap() for h in range(2)] for t in range(NT8)]
    xgc = [[nc.dram_tensor(f"xgc_{t}_{h}", [4 * 512, 512], F32).ap() for h in range(2)] for t in range(NT8)]
    yseg = nc.dram_tensor("yseg", [4 * 384, TOK + 2], BF16).ap()
    xh = nc.dram_tensor("xh", [D, 2], F32).ap()
    Byloc = [[Buf(yloc[l][c]) for c in range(NCH)] for l in range(2)]
    Byall = [[Buf(yall_t[l][c * 1536:(c + 1) * 1536, :]) for c in range(NCH)] for l in range(2)]
    Bx1c = [[Buf(x1c[t][h]) for h in range(2)] for t in range(NT8)]
    Bxgc = [[Buf(xgc[t][h]) for h in range(2)] for t in range(NT8)]
    Byseg, Bxh = Buf(yseg, "yseg"), Buf(xh, "xh")
    rv = {}
    P = Prog(nc)
    O = Ops(P)

    for layer in (0, 1):
        sfx = f"_{layer}"
        P.prefix = f"m{layer}_"
        P.begin_phase()
        if layer == 0:
            def xsrc(k, t0):
                return xT_b[k * 128:(k + 1) * 128, t0:t0 + TB]
        else:
            def xsrc(k, t0):
                sg_, tt = t0 // TOK, (t0 % TOK) // 512
                r0 = sg_ * 512 + (k % 4) * 128
                return View(Bxgc[tt][k // 4], xgc[tt][k // 4][r0:r0 + 128, :])

        def yout(i3, t0, ysb, layer=layer):
            ci, off = t0 // CH, t0 % CH
            P.dma(yloc[layer][ci][i3 * 128:(i3 + 1) * 128, off:off + TB], ysb.ap, reads=[ysb], writes=[Byloc[layer][ci]])

        def after_sb(t0, layer=layer):
            ci, off = t0 // CH, t0 % CH
            if off + TB == CH and not NOCOLL:
                P.coll("AllGather", yloc[layer][ci], yall_t[layer][ci * 1536:(ci + 1) * 1536, :], GROUPS,
                       reads=[Byloc[layer][ci]], writes=[Byall[layer][ci]])
        emit_mixer(nc, P, O, L, sfx, xsrc, yout, after_sb)
        P.end_phase()
        P.prefix = f"d{layer}_"
        P.begin_phase()
        if layer == 0:
            def setup(e, rv=rv):
                r = e.snap(e.partition_id() % 4, min_val=0, max_val=3)
                rv["yrow"] = e.snap(r * (CPS * 1536), min_val=0, max_val=3 * CPS * 1536)
                rv["hrow"] = e.snap(((r * CPS + (NCH - 1)) % NCH) * 1536, min_val=0, max_val=(NCH - 1) * 1536)
                rv["prow"] = e.snap(((r + 3) % 4) * 512, min_val=0, max_val=3 * 512)
                return None
            P.items["sp"].append(([], setup, None, 0))
        ya = yall_t[layer]
        for j in range(CPS):
            P.dma(yseg[:, 2 + j * CH:2 + (j + 1) * CH],
                  (lambda e, j=j, ya=ya: ya[(slice(j * 1536, (j + 1) * 1536) if NODYN else bass.ds(rv["yrow"] + j * 1536, 1536)), :]),
                  reads=Byall[layer], writes=[Byseg])
        P.dma(yseg[:, 0:2], (lambda e, ya=ya: ya[(slice(0, 1536) if NODYN else bass.ds(rv["hrow"], 1536)), CH - 2:CH]),
              reads=Byall[layer], writes=[Byseg])
        if layer == 1:
            for h in range(2):
                P.dma(xh[h * 512:(h + 1) * 512, :],
                      (lambda e, h=h: xgc[NT8 - 1][h][(slice(0, 512) if NODYN else bass.ds(rv["prow"], 512)), 510:512]),
                      reads=[Bxgc[NT8 - 1][h]], writes=[Bxh])

        def fx(P_, buf, k, c0, w, halo, layer=layer):
            if layer == 0:
                P_.dma(buf.ap[:, :w], xs0[k * 128:(k + 1) * 128, c0:c0 + w], writes=[buf])
            elif halo:
                P_.dma(buf.ap[:, :2], xh[k * 128:(k + 1) * 128, :], reads=[Bxh], writes=[buf])
            else:
                tt = (c0 - 2) // 512
                P_.dma(buf.ap[:, :w], x1c[tt][k // 4][(k % 4) * 128:(k % 4) * 128 + 128, 0:w],
                       reads=[Bx1c[tt][k // 4]], writes=[buf])

        def fy(P_, buf, k, c0, w, halo):
            row0 = (k % 4) * 384 + (k // 4) * 128
            P_.dma(buf.ap[:, :w], yseg[row0:row0 + 128, c0:c0 + w], reads=[Byseg], writes=[buf])

        def fo(P_, buf, m_, t, w, layer=layer):
            if layer == 0:
                tt = t // 512
                P_.dma(x1c[tt][m_ // 4][(m_ % 4) * 128:(m_ % 4) * 128 + 128, 0:w], buf.ap[:, :w],
                       reads=[buf], writes=[Bx1c[tt][m_ // 4]])
            else:
                P_.dma(oT[m_ * 128:(m_ + 1) * 128, t:t + w], buf.ap[:, :w], reads=[buf])

        def after_tile(P_, t, layer=layer):
            if layer == 0 and not NOCOLL:
                tt = t // 512
                for h in range(2):
                    P_.coll("AllGather", x1c[tt][h], xgc[tt][h], GROUPS, reads=[Bx1c[tt][h]], writes=[Bxgc[tt][h]])

        emit_dense(nc, P, TOK, layer == 1, sfx, dict(hmask=hmask_d, x=fx, y=fy, out=fo, after_tile=after_tile))
        P.end_phase(final=(layer == 1))
    P.close()
    return nc


def kernel(**inputs):
    W = {k: np.asarray(v, dtype=np.float32) for k, v in inputs.items()}
    x = np.ascontiguousarray(W.pop("x"))
    p = W.pop("p")
    Bsz, L, _ = x.shape
    depth = W["w_in"].shape[0]
    assert depth == 2 and Bsz == 2
    TOK = L // 4
    nc = build_fused(L)
    xT = [np.ascontiguousarray(x[b].T) for b in range(Bsz)]
    in_maps = []
    for c in range(NCORES):
        b, r = c // 4, c % 4
        s0 = r * TOK
        xs = np.zeros((D, TOK + 2), np.float32)
        lo = max(s0 - 2, 0)
        xs[:, 2 - (s0 - lo):] = xT[b][:, lo:s0 + TOK]
        im = {"xT_b": xT[b], "xs0": xs,
              "hmask": np.full((128, 1), 0.0 if r == 0 else 1.0, np.float32)}
        for i in range(depth):
            for k, v in mixer_inputs(i, r, None, W).items():
                im[f"{k}_{i}"] = v
            for k, v in dense_inputs(i, None, None, p[i, b, s0:s0 + TOK], W).items():
                im[f"{k}_{i}"] = v
        in_maps.append(im)
    res = run_bass_kernel_spmd(nc, in_maps, core_ids=list(range(NCORES)))
    out = np.empty_like(x)
    for c in range(NCORES):
        b, r = c // 4, c % 4
        out[b, r * TOK:(r + 1) * TOK] = res.results[c]["oT"].T
    return out
```

```python
import numpy as np
from contextlib import ExitStack
import concourse.bass as bass
import concourse.mybir as mybir
from concourse.bass_utils import run_bass_kernel_spmd

F32 = mybir.dt.float32
BF16 = mybir.dt.bfloat16
AF = mybir.ActivationFunctionType
ALU = mybir.AluOpType
AX = mybir.AxisListType

ENGS = ("pe", "act", "dve", "pool", "sp")
EPOCH = 16000
RING = 12


class Buf:
    __slots__ = ("ap", "w", "r", "name", "psum")

    def __init__(self, ap, name=""):
        self.ap = ap
        self.psum = False
        self.w = None
        self.r = []
        self.name = name

    def __getitem__(self, k):
        return self.ap[k]


class Prog:
    def __init__(self, nc, self_sync=True, prefix=""):
        self.nc = nc
        self.prefix = prefix
        self.es = ExitStack()
        self.es_sem = ExitStack()
        self.phase_finals = []
        self.items = {e: [] for e in ENGS}
        self.count = {e: 0 for e in ENGS}
        self.waited = {e: {} for e in ENGS}
        self.self_sync = self_sync
        self.semh = {}
        self.dma_n = {"sp": 0, "pool": 0, "act": 0}
        self.dma_last = {}
        self.nbuf = 0

    def sem(self, key):
        if key not in self.semh:
            nm = self.prefix + "s_" + "_".join(str(k) for k in key)
            self.semh[key] = self.es_sem.enter_context(self.nc.semaphore(nm))
        return self.semh[key]

    def sb(self, shape, dtype=F32, name=None):
        self.nbuf += 1
        name = self.prefix + (name or f"sb{self.nbuf}")
        t = self.es.enter_context(self.nc.sbuf_tensor(name, list(shape), dtype))
        return t

    def ps(self, shape, dtype=F32, name=None):
        self.nbuf += 1
        name = self.prefix + (name or f"ps{self.nbuf}")
        t = self.es.enter_context(self.nc.psum_tensor(name, list(shape), dtype))
        return t

    def buf(self, ap, name=""):
        return Buf(ap, name)

    def sbuf(self, shape, dtype=F32, name=None):
        t = self.sb(shape, dtype, name)
        return Buf(t[:], name or "")

    def psbuf(self, shape, dtype=F32, name=None):
        t = self.ps(shape, dtype, name)
        b = Buf(t[:], name or "")
        b.psum = True
        return b

    def _deps(self, reads, writes):
        deps = []
        for b in reads:
            if b.w is not None:
                deps.append(b.w)
        for b in writes:
            if b.w is not None:
                deps.append(b.w)
            deps.extend(b.r)
        return deps

    def _waits(self, eng, deps, own_key_prefix):
        waits = []
        wd = self.waited[eng]
        for (key, val) in deps:
            if key[0] == own_key_prefix and key[0] != "dma":
                if eng == "pe" or not self.self_sync:
                    continue
            if wd.get(key, 0) >= val:
                continue
            wd[key] = val
            waits.append((key, val))
        return waits

    def op(self, eng, fn, reads=(), writes=()):
        pr = [b for b in reads if b.psum]
        if pr:
            reads = [b for b in reads if not b.psum]
            writes = list(writes) + pr
        deps = self._deps(reads, writes)
        waits = self._waits(eng, deps, eng)
        self.count[eng] += 1
        c = self.count[eng]
        key = (eng, (c - 1) // EPOCH)
        val = (c - 1) % EPOCH + 1
        ev = (key, val)
        self.items[eng].append((waits, fn, ev, 1))
        for b in reads:
            b.r.append(ev)
        for b in writes:
            b.w = ev
            b.r = []
        return ev

    def dma(self, out_ap, in_ap, reads=(), writes=(), q="sp", **kw):
        deps = self._deps(reads, writes)
        j = self.dma_n[q]
        self.dma_n[q] += 1
        slot = j % RING
        key = ("dma", q, slot)
        if j >= RING:
            deps.append((key, 16 * (j // RING)))
        waits = self._waits(q, deps, "dma")
        val = 16 * (j // RING + 1)
        ev = (key, val)
        self.dma_last[key] = val

        def fn(e, out_ap=out_ap, in_ap=in_ap, kw=kw):
            o = out_ap(e) if callable(out_ap) else out_ap
            i = in_ap(e) if callable(in_ap) else in_ap
            try:
                return e.dma_start(out=o, in_=i, **kw)
            except Exception:
                print('DMA FAIL out', o, 'in', i, flush=True)
                raise
        self.items[q].append((waits, fn, ev, 16))
        for b in reads:
            b.r.append(ev)
        for b in writes:
            b.w = ev
            b.r = []
        return ev

    def _last_events(self):
        finals = []
        for key, val in self.dma_last.items():
            finals.append((key, val))
        for e in ("pe", "act", "dve", "pool"):
            c = self.count[e]
            if c > 0:
                finals.append(((e, (c - 1) // EPOCH), (c - 1) % EPOCH + 1))
        return finals

    def begin_phase(self):
        finals = self._last_events()
        for e in ENGS:
            waits = self._waits(e, finals, "__none__")
            if waits:
                self.items[e].append((waits, None, None, 0))

    def end_phase(self, final=False):
        nc = self.nc
        for e in ENGS:
            for (waits, fn, ev, inc) in self.items[e]:
                if ev is not None:
                    self.sem(ev[0])
                for (k, v) in waits:
                    self.sem(k)
        final_waits = self._last_events() if final else []
        for (k, v) in final_waits:
            self.sem(k)
        items = self.items
        semh = self.semh

        def run(e, lst, fin=False):
            for (waits, fn, ev, inc) in lst:
                for (k, v) in waits:
                    e.wait_ge(semh[k], v)
                if fn is None:
                    continue
                ins = fn(e)
                if ins is not None and ev is not None:
                    ins.then_inc(semh[ev[0]], inc)
            if fin:
                for (k, v) in final_waits:
                    e.wait_ge(semh[k], v)

        with nc.Block() as block:
            @block.sync
            def _(e):
                run(e, items["sp"], fin=final)

            @block.tensor
            def _(e):
                run(e, items["pe"])

            @block.scalar
            def _(e):
                run(e, items["act"])

            @block.vector
            def _(e):
                run(e, items["dve"])

            @block.gpsimd
            def _(e):
                run(e, items["pool"])
        self.items = {e: [] for e in ENGS}
        self.es.close()
        self.es = ExitStack()

    def emit(self):
        self.end_phase(final=True)

    def close(self):
        self.es.close()
        self.es_sem.close()

    def mm(self, out, lhsT, rhs, start=True, stop=True, reads=(), writes=()):
        return self.op("pe", lambda e: e.matmul(out, lhsT, rhs, start=start, stop=stop),
                       reads, writes)


class View:
    __slots__ = ("buf", "ap")

    def __init__(self, buf, ap):
        self.buf = buf
        self.ap = ap

    def __getitem__(self, k):
        return View(self.buf, self.ap[k])


def V(buf, *k):
    if not k:
        return View(buf, buf.ap)
    return View(buf, buf.ap[k if len(k) > 1 else k[0]])


def _ap(x):
    return x.ap if isinstance(x, View) else x


def _bufs(*xs):
    return [x.buf for x in xs if isinstance(x, View)]


class Ops:
    def __init__(self, P):
        self.P = P

    def mm(self, out, lhsT, rhs, start=True, stop=True):
        return self.P.op("pe", lambda e: e.matmul(out.ap, lhsT.ap, rhs.ap, start=start, stop=stop),
                         _bufs(lhsT, rhs), _bufs(out))

    def tr(self, out, in_, ident):
        return self.P.op("pe", lambda e: e.transpose(out.ap, in_.ap, ident.ap), _bufs(in_, ident), _bufs(out))

    def act(self, out, in_, func, bias=None, scale=None, accum=None, eng="act"):
        kw = {}
        if bias is not None:
            kw["bias"] = _ap(bias)
        if scale is not None:
            kw["scale"] = _ap(scale)
        if accum is not None:
            kw["accum_out"] = _ap(accum)
        return self.P.op("act", lambda e: e.activation(out.ap, in_.ap, func, **kw),
                         _bufs(in_, bias, scale), _bufs(out, accum))

    def tt(self, eng, out, in0, in1, op):
        return self.P.op(eng, lambda e: e.tensor_tensor(out.ap, in0.ap, in1.ap, op=op), _bufs(in0, in1), _bufs(out))

    def ts(self, eng, out, in0, s1, op0, s2=None, op1=None):
        if op1 is None:
            return self.P.op(eng, lambda e: e.tensor_scalar(out.ap, in0.ap, _ap(s1), None, op0=op0),
                             _bufs(in0, s1), _bufs(out))
        return self.P.op(eng, lambda e: e.tensor_scalar(out.ap, in0.ap, _ap(s1), _ap(s2), op0=op0, op1=op1),
                         _bufs(in0, s1, s2), _bufs(out))

    def stt(self, eng, out, in0, scalar, in1, op0, op1):
        return self.P.op(eng, lambda e: e.scalar_tensor_tensor(out.ap, in0.ap, _ap(scalar), in1.ap, op0=op0, op1=op1),
                         _bufs(in0, scalar, in1), _bufs(out))

    def copy(self, eng, out, in_):
        if eng == "act":
            return self.act(out, in_, AF.Identity)
        return self.P.op(eng, lambda e: e.tensor_copy(out.ap, in_.ap), _bufs(in_), _bufs(out))

    def recip(self, out, in_):
        return self.P.op("dve", lambda e: e.reciprocal(out.ap, in_.ap), _bufs(in_), _bufs(out))

    def scan(self, eng, out, d0, d1, init, op0=None, op1=None):
        op0 = op0 or ALU.mult
        op1 = op1 or ALU.add
        return self.P.op(eng, lambda e: e.tensor_tensor_scan(out.ap, d0.ap, d1.ap, _ap(init), op0=op0, op1=op1),
                         _bufs(d0, d1, init), _bufs(out))

    def memset(self, eng, out, val):
        return self.P.op(eng, lambda e: e.memset(out.ap, val), [], _bufs(out))

    def aselect(self, out, in_, pattern, cmp, fill, base=0, cm=1):
        return self.P.op("pool", lambda e: e.affine_select(out.ap, in_.ap, pattern=pattern, compare_op=cmp, fill=fill,
                                                          base=base, channel_multiplier=cm),
                         _bufs(in_), _bufs(out))

    def dma(self, out, in_, q="sp"):
        return self.P.dma(_ap(out), _ap(in_), reads=_bufs(in_), writes=_bufs(out), q=q)


def _prog_coll(self, kind, in_ap, out_ap, groups, reads=(), writes=()):
    q = "pool"
    deps = self._deps(reads, writes)
    self.ncoll = getattr(self, "ncoll", 0) + 1
    key = ("cc", self.ncoll)
    waits = self._waits(q, deps, "dma")
    ev = (key, 1)
    self.dma_last[key] = 1

    def fn(e):
        return e.collective_compute(kind, ALU.bypass, groups, [in_ap.opt()], [out_ap.opt()])
    self.items[q].append((waits, fn, ev, 1))
    for b in reads:
        b.r.append(ev)
    for b in writes:
        b.w = ev
        b.r = []
    return ev


Prog.coll = _prog_coll

import math

D = 1024
DFF = 2816
NF = 22
EPS = 1e-6
NEG = -1.0e30
TB = 512
NB = 4
NWF = 656
NWT = 386
SEQ = 16384
NCORES = 8
GROUPS = [[0, 1, 2, 3], [4, 5, 6, 7]]


class PsPool:
    def __init__(self, P, n=8, name="psp"):
        self.bufs = [P.psbuf([128, 512], F32, f"{name}{i}") for i in range(n)]
        self.i = 0

    def get(self):
        b = self.bufs[self.i % len(self.bufs)]
        self.i += 1
        return b

def emit_mixer(nc, P, O, L, sfx, xsrc, yout, after_sb):
    def din(name, shape):
        return nc.dram_tensor(name + sfx, list(shape), F32, kind="ExternalInput").ap()
    anorm = din("anorm", [128, 8])
    wf_d = din("wf", [D, NWF])
    wt_d = din("wt", [D, NWT])
    dn_sc_d = din("dn_sc", [128, 2])
    dn_cw_d = din("dn_cw", [128, 12])
    dn_ng_d = din("dn_ng", [128, 128])
    gla_ng_d = din("gla_ng", [128, 128])
    gla_w2_d = din("gla_w2", [16, 64])
    gla_b2_d = din("gla_b2", [64, 1])
    s5_lam_d = din("s5_lam", [128, 12])
    s5_b_d = din("s5_b", [2, 4, 128, 16])
    s5_c_d = din("s5_c", [2, 4, 128, 16])
    s5_d_d = din("s5_d", [128, 1])
    psp = PsPool(P, n=7)
    yps_ded = P.psbuf([128, 512], F32, "yps_ded")

    def S(shape, dt=F32, name=None):
        return P.sbuf(shape, dt, name)

    ones = S([128, 512], F32, "ones")
    O.memset("pool", V(ones), 1.0)
    ident = S([128, 128], F32, "ident")
    O.memset("pool", V(ident), 1.0)
    O.aselect(V(ident), V(ident), [[-1, 128]], ALU.is_equal, 0.0, cm=1)
    U = S([128, 128], F32, "U")
    O.memset("pool", V(U), 1.0)
    O.aselect(V(U), V(U), [[1, 128]], ALU.is_ge, 0.0, cm=-1)
    mneg = S([128, 128], F32, "mneg")
    O.memset("pool", V(mneg), 0.0)
    O.aselect(V(mneg), V(mneg), [[-1, 128]], ALU.is_ge, NEG, cm=1)
    mnegT = S([128, 128], F32, "mnegT")
    O.memset("pool", V(mnegT), 0.0)
    O.aselect(V(mnegT), V(mnegT), [[1, 128]], ALU.is_ge, NEG, cm=-1)
    nLs = S([128, 128], F32, "nLs")
    O.memset("pool", V(nLs), -1.0)
    O.aselect(V(nLs), V(nLs), [[-1, 128]], ALU.is_gt, 0.0, cm=1)

    c_anorm = S([128, 8], F32, "c_anorm")
    O.dma(V(c_anorm), anorm)
    wf = S([128, 8, NWF], BF16, "wf_s")
    wt = S([128, 8, NWT], BF16, "wt_s")
    O.dma(V(wf), wf_d.rearrange("(k p) n -> p k n", p=128), q="pool")
    O.dma(V(wt), wt_d.rearrange("(k p) n -> p k n", p=128), q="pool")
    dn_sc = S([128, 2], F32, "dn_sc_s"); O.dma(V(dn_sc), dn_sc_d)
    dn_cw = S([128, 12], F32, "dn_cw_s"); O.dma(V(dn_cw), dn_cw_d)
    dn_ng = S([128, 128], F32, "dn_ng_s"); O.dma(V(dn_ng), dn_ng_d)
    gla_ng = S([128, 128], F32, "gla_ng_s"); O.dma(V(gla_ng), gla_ng_d)
    gla_w2 = S([16, 64], F32, "gla_w2_s"); O.dma(V(gla_w2), gla_w2_d)
    gla_b2 = S([64, 1], F32, "gla_b2_s"); O.dma(V(gla_b2), gla_b2_d)
    nb2 = S([64, 1], F32, "nb2")
    O.ts("dve", V(nb2), V(gla_b2), -1.0, ALU.mult)
    negA = S([128, 1], F32, "negA")
    O.act(V(negA), V(dn_sc, slice(None), slice(0, 1)), AF.Exp)
    O.ts("dve", V(negA), V(negA), -1.0, ALU.mult)
    s5d = S([128, 1], F32, "s5d"); O.dma(V(s5d), s5_d_d)

    lam = S([128, 12], F32, "lam"); O.dma(V(lam), s5_lam_d)
    lr_, li_, ls_ = (V(lam, slice(None), slice(0, 4)), V(lam, slice(None), slice(4, 8)),
                     V(lam, slice(None), slice(8, 12)))
    pp = S([128, 64], F32, "s5pp")

    def col(i):
        return V(pp, slice(None), slice(4 * i, 4 * i + 4))
    step, lrs, th, mag, c8, s8, t0, t1, cr, ci, den, nr, fr, fi, t2, t3 = [col(i) for i in range(16)]
    O.act(step, ls_, AF.Exp)
    O.tt("dve", lrs, lr_, step, ALU.mult)
    O.tt("dve", th, li_, step, ALU.mult)
    O.act(mag, lrs, AF.Exp)
    halfpi = S([128, 1], F32, "halfpi"); O.memset("pool", V(halfpi), math.pi / 2)
    O.act(s8, th, AF.Sin, scale=0.125)
    O.act(c8, th, AF.Sin, scale=-0.125, bias=V(halfpi))
    for _ in range(3):
        O.tt("dve", t0, c8, c8, ALU.mult)
        O.tt("dve", t1, s8, s8, ALU.mult)
        O.tt("dve", t2, c8, s8, ALU.mult)
        O.tt("dve", c8, t0, t1, ALU.subtract)
        O.ts("dve", s8, t2, 2.0, ALU.mult)
    O.tt("dve", cr, mag, c8, ALU.mult)
    O.tt("dve", ci, mag, s8, ALU.mult)
    O.tt("dve", t0, lr_, lr_, ALU.mult)
    O.tt("dve", t1, li_, li_, ALU.mult)
    O.tt("dve", den, t0, t1, ALU.add)
    O.recip(den, den)
    O.ts("dve", nr, cr, -1.0, ALU.add)
    O.tt("dve", t0, nr, lr_, ALU.mult)
    O.tt("dve", t1, ci, li_, ALU.mult)
    O.tt("dve", t0, t0, t1, ALU.add)
    O.tt("dve", fr, t0, den, ALU.mult)
    O.tt("dve", t0, ci, lr_, ALU.mult)
    O.tt("dve", t1, nr, li_, ALU.mult)
    O.tt("dve", t0, t0, t1, ALU.subtract)
    O.tt("dve", fi, t0, den, ALU.mult)

    Ct = [S([128, TB], F32, f"Ct{j}") for j in range(4)]
    St = [S([128, TB], F32, f"St{j}") for j in range(4)]
    Mg = [S([128, TB], F32, f"Mg{j}") for j in range(4)]
    rq = S([128, 16], F32, "rq")
    r512 = S([128, 8], F32, "r512")
    tmpT = S([128, TB // 2], F32, "tmpT")
    for j in range(4):
        cj, sj, ta, tb_ = [V(rq, slice(None), slice(4 * j + i, 4 * j + i + 1)) for i in range(4)]
        O.copy("dve", cj, V(pp, slice(None), slice(4 * 4 + j, 4 * 4 + j + 1)))
        O.copy("dve", sj, V(pp, slice(None), slice(4 * 5 + j, 4 * 5 + j + 1)))
        O.memset("pool", V(Ct[j], slice(None), slice(0, 1)), 1.0)
        O.memset("pool", V(St[j], slice(None), slice(0, 1)), 0.0)
        n = 1
        while n < TB:
            lo_c, lo_s = V(Ct[j], slice(None), slice(0, n)), V(St[j], slice(None), slice(0, n))
            hi_c, hi_s = V(Ct[j], slice(None), slice(n, 2 * n)), V(St[j], slice(None), slice(n, 2 * n))
            tm = V(tmpT, slice(None), slice(0, n))
            O.ts("dve", tm, lo_s, sj, ALU.mult)
            O.stt("dve", hi_c, lo_c, cj, tm, ALU.mult, ALU.subtract)
            O.ts("dve", tm, lo_c, sj, ALU.mult)
            O.stt("dve", hi_s, lo_s, cj, tm, ALU.mult, ALU.add)
            O.tt("dve", ta, cj, cj, ALU.mult)
            O.tt("dve", tb_, sj, sj, ALU.mult)
            O.tt("dve", sj, cj, sj, ALU.mult)
            O.ts("dve", sj, sj, 2.0, ALU.mult)
            O.tt("dve", cj, ta, tb_, ALU.subtract)
            n *= 2
        O.copy("dve", V(r512, slice(None), slice(2 * j, 2 * j + 1)), cj)
        O.copy("dve", V(r512, slice(None), slice(2 * j + 1, 2 * j + 2)), sj)
        O.ts("dve", V(Mg[j]), V(ones), V(pp, slice(None), slice(4 * 3 + j, 4 * 3 + j + 1)), ALU.mult)

    BreT = [S([128, 128], F32, f"BreT{j}") for j in range(4)]
    BimT = [S([128, 128], F32, f"BimT{j}") for j in range(4)]
    Cre = [S([128, 128], F32, f"Cre{j}") for j in range(4)]
    Cim = [S([128, 128], F32, f"Cim{j}") for j in range(4)]
    bst = S([128, 2, 16], F32, "bst")
    padr = S([128, 128], F32, "padr")
    padi = S([128, 128], F32, "padi")
    for j in range(4):
        O.dma(V(bst, slice(None), 0), s5_b_d[0, j])
        O.dma(V(bst, slice(None), 1), s5_b_d[1, j])
        O.memset("pool", V(padr), 0.0)
        O.memset("pool", V(padi), 0.0)
        O.memset("pool", V(Cre[j]), 0.0)
        O.memset("pool", V(Cim[j]), 0.0)
        frj = V(pp, slice(None), slice(4 * 12 + j, 4 * 12 + j + 1))
        fij = V(pp, slice(None), slice(4 * 13 + j, 4 * 13 + j + 1))
        for hf in range(2):
            ps_ = slice(64 * hf, 64 * hf + 64)
            cs_ = slice((2 * j + hf) * 16, (2 * j + hf) * 16 + 16)
            bre, bim = V(bst, ps_, 0), V(bst, ps_, 1)
            tm = V(tmpT, ps_, slice(0, 16))
            O.ts("dve", tm, bim, fij[ps_], ALU.mult)
            O.stt("dve", V(padr, ps_, cs_), bre, frj[ps_], tm, ALU.mult, ALU.subtract)
            O.ts("dve", tm, bre, fij[ps_], ALU.mult)
            O.stt("dve", V(padi, ps_, cs_), bim, frj[ps_], tm, ALU.mult, ALU.add)
            O.dma(V(Cre[j], ps_, cs_), s5_c_d[0, j, 64 * hf:64 * hf + 64, :])
            O.dma(V(Cim[j], ps_, cs_), s5_c_d[1, j, 64 * hf:64 * hf + 64, :])
        for (pad, dst) in ((padr, BreT[j]), (padi, BimT[j])):
            pt = psp.get()
            O.tr(V(pt, slice(None), slice(0, 128)), V(pad), V(ident))
            O.copy("act", V(dst), V(pt, slice(None), slice(0, 128)))
        O.ts("dve", V(Cim[j]), V(Cim[j]), -1.0, ALU.mult)

    Sdn = [S([128, 128], F32, f"Sdn{i}") for i in range(2)]
    Sgl = [S([64, 128], F32, f"Sgl{i}") for i in range(2)]
    O.memset("pool", V(Sdn[0]), 0.0)
    O.memset("pool", V(Sgl[0]), 0.0)
    s5c = [S([128, 2], F32, f"s5c{j}") for j in range(4)]
    for j in range(4):
        O.memset("pool", V(s5c[j]), 0.0)
    s5i = [S([128, 4], F32, f"s5i{j}") for j in range(4)]
    chist = [S([128, 3], F32, f"chist{c}") for c in range(3)]
    for c in range(3):
        O.memset("pool", V(chist[c]), 0.0)

    x_t = P.sb([128, 8, TB], F32, "x_t")
    xb = [Buf(x_t[:, k, :], f"x{k}") for k in range(8)]
    h_t = P.sb([128, 8, TB], BF16, "h_t")
    hb = [Buf(h_t[:, k, :], f"h{k}") for k in range(8)]
    sq = [S([128, TB], F32, f"sq{i}") for i in range(2)]
    rstd = S([128, TB], F32, "rstd")
    cbuf = [S([128, TB + 3], F32, f"cbuf{c}") for c in range(3)]
    cacc = [S([128, TB], F32, f"cacc{c}") for c in range(3)]
    qn = S([128, TB], F32, "qn")
    kn = S([128, TB], F32, "kn")
    tok = [S([128, NWT], F32, f"tok{b}") for b in range(NB)]
    uT = S([128, TB], F32, "uT")
    lrs_b = S([16, TB], F32, "lrs_b")
    gls = S([64, TB], F32, "gls")
    gc = S([64, TB], F32, "gc")
    gEQ = S([64, TB], F32, "gEQ")
    gEK = S([64, TB], F32, "gEK")
    gqe = S([64, TB], F32, "gqe")
    gke = S([64, TB], F32, "gke")
    gkr = S([64, TB], F32, "gkr")
    gqr = S([64, TB], F32, "gqr")
    s5w = [S([128, TB], F32, f"s5w{i}") for i in range(6)]
    yc_sb = S([128, TB], F32, "yc_sb")
    ys_t = P.sb([128, 3, TB], BF16, "ys_t")
    ys = [Buf(ys_t[:, i3, :], f"ys{i3}") for i3 in range(3)]

    def blkbufs(name, shape, n=NB):
        return [S(shape, F32, f"{name}{b}") for b in range(n)]
    sc = [[S([128, 1], F32, f"sc{b}_{i}") for i in range(12)] for b in range(NB)]
    Ug = blkbufs("Ug", [128, 128])
    Eb = blkbufs("Eb", [128, 128])
    ETb = blkbufs("ETb", [128, 128])
    qd = blkbufs("qd", [128, 128])
    bk = blkbufs("bk", [128, 128])
    kdec = blkbufs("kdec", [128, 128])
    bv = blkbufs("bv", [128, 128])
    M = blkbufs("M", [128, 256])
    X = blkbufs("X", [128, 128])
    attnT = blkbufs("attnT", [128, 128])
    un = blkbufs("un", [128, 128])
    wT = blkbufs("wT", [128, 128])
    ub = blkbufs("ub", [128, 128])
    sg = blkbufs("sg", [128, 128])
    yo = blkbufs("yo", [128, 128])
    junk = blkbufs("junk", [128, 128], 2)
    gjunk = blkbufs("gjunk", [128, 128], 2)
    gsc = [[S([64, 1], F32, f"gsc{b}_{i}") for i in range(2)] for b in range(NB)]
    gkdT = blkbufs("gkdT", [64, 128])
    gkd = blkbufs("gkd", [128, 64])
    gaT = blkbufs("gaT", [128, 128])
    gsg = blkbufs("gsg", [128, 128])
    gyo = blkbufs("gyo", [128, 128])
    gss = [[S([128, 1], F32, f"gss{b}_{i}") for i in range(2)] for b in range(NB)]
    ones64 = S([64, 128], F32, "ones64")
    O.memset("pool", V(ones64), 1.0)

    A = slice(None)

    def bsl(b):
        return slice(128 * b, 128 * b + 128)

    def out_norm(o_ps, ss_b, ng, sgate, ydst, jk):
        ssum, rs = ss_b
        O.memset("pool", V(ssum), 0.0)
        O.act(V(jk), o_ps, AF.Square, accum=V(ssum))
        O.act(V(rs), V(ssum), AF.Sqrt, bias=EPS, scale=1.0 / 128)
        O.recip(V(rs), V(rs))
        O.stt("dve", V(ydst), o_ps, V(rs), V(ng), ALU.mult, ALU.mult)
        O.tt("pool", V(ydst), V(ydst), V(sgate), ALU.mult)

    nsb = L // TB
    sdn_i = 0
    sgl_i = 0
    def gen_front(t0f):
        for k in range(8):
            O.dma(V(xb[k]), xsrc(k, t0f))
        ps = psp.get()
        for k in range(8):
            s = sq[k % 2]
            O.act(V(s), V(xb[k]), AF.Square)
            O.mm(V(ps), V(ones, A, slice(0, 128)), V(s), start=(k == 0), stop=(k == 7))
        O.act(V(rstd), V(ps), AF.Sqrt, bias=EPS, scale=1.0 / D)
        O.recip(V(rstd), V(rstd))
        for k in range(8):
            O.stt("dve", V(hb[k]), V(xb[k]), V(c_anorm, A, slice(k, k + 1)), V(rstd), ALU.mult, ALU.mult)
        yield
        for c in range(3):
            ps = psp.get()
            for k in range(8):
                O.mm(V(ps), V(wf, A, k, slice(c * 128, (c + 1) * 128)), V(hb[k]), start=(k == 0), stop=(k == 7))
            O.copy("pool", V(cbuf[c], A, slice(0, 3)), V(chist[c]))
            O.copy("act", V(cbuf[c], A, slice(3, TB + 3)), V(ps))
            O.copy("pool", V(chist[c]), V(cbuf[c], A, slice(TB, TB + 3)))
            yield
        ps = psp.get()
        for k in range(8):
            O.mm(V(ps), V(wf, A, k, slice(384, 512)), V(hb[k]), start=(k == 0), stop=(k == 7))
        O.copy("act", V(uT), V(ps))
        yield
        ps_gq = psp.get()
        for k in range(8):
            O.mm(V(ps_gq, slice(0, 64)), V(wf, A, k, slice(512, 576)), V(hb[k]), start=(k == 0), stop=(k == 7))
        O.copy("act", V(gqr), V(ps_gq, slice(0, 64)))
        ps_gk = psp.get()
        for k in range(8):
            O.mm(V(ps_gk, slice(0, 64)), V(wf, A, k, slice(576, 640)), V(hb[k]), start=(k == 0), stop=(k == 7))
        O.copy("act", V(gkr), V(ps_gk, slice(0, 64)))
        yield
        ps = psp.get()
        for k in range(8):
            O.mm(V(ps, slice(0, 16)), V(wf, A, k, slice(640, 656)), V(hb[k]), start=(k == 0), stop=(k == 7))
        O.copy("act", V(lrs_b), V(ps, slice(0, 16)))
        yield
        for b in range(NB):
            ps = psp.get()
            for k in range(8):
                O.mm(V(ps, A, slice(0, NWT)), V(hb[k], A, bsl(b)), V(wt, A, k), start=(k == 0), stop=(k == 7))
            O.copy("act", V(tok[b]), V(ps, A, slice(0, NWT)))
            yield
        for c in range(3):
            O.ts("dve", V(cacc[c]), V(cbuf[c], A, slice(0, TB)), V(dn_cw, A, slice(4 * c, 4 * c + 1)), ALU.mult)
            for t in range(1, 4):
                O.stt("dve", V(cacc[c]), V(cbuf[c], A, slice(t, t + TB)),
                      V(dn_cw, A, slice(4 * c + t, 4 * c + t + 1)), V(cacc[c]), ALU.mult, ALU.add)
            O.act(V(cacc[c]), V(cacc[c]), AF.Silu)
            yield
        for c, dst, scl in ((0, qn, 128 ** -0.5), (1, kn, 1.0)):
            s = sq[c]
            O.act(V(s), V(cacc[c]), AF.Square)
            ps = psp.get()
            O.mm(V(ps), V(ones, A, slice(0, 128)), V(s))
            O.act(V(s), V(ps), AF.Sqrt, bias=EPS, scale=1.0)
            O.recip(V(s), V(s))
            O.stt("dve", V(dst), V(cacc[c]), scl, V(s), ALU.mult, ALU.mult)
            yield

    for _ in gen_front(0):
        pass
    for sb_i in range(nsb):
        t0_ = sb_i * TB
        dn_flag = [False]
        vs = cacc[2]

        def gen_s5():
            yps = yps_ded
            for j in range(4):
                pr = psp.get()
                pi = psp.get()
                O.mm(V(pr), V(BreT[j]), V(uT))
                O.mm(V(pi), V(BimT[j]), V(uT))
                w0, w1, w2, w3, w4, w5 = [V(s5w[i]) for i in range(6)]
                O.tt("dve", w0, V(pr), V(Ct[j]), ALU.mult)
                O.tt("dve", w1, V(pi), V(St[j]), ALU.mult)
                O.tt("pool", w0, w0, w1, ALU.add)
                O.tt("dve", w2, V(pi), V(Ct[j]), ALU.mult)
                O.tt("dve", w3, V(pr), V(St[j]), ALU.mult)
                O.tt("pool", w2, w2, w3, ALU.subtract)
                cr_, ci_ = V(s5c[j], A, slice(0, 1)), V(s5c[j], A, slice(1, 2))
                c5, s5_ = V(r512, A, slice(2 * j, 2 * j + 1)), V(r512, A, slice(2 * j + 1, 2 * j + 2))
                ir, ii, ta, tb_ = [V(s5i[j], A, slice(i, i + 1)) for i in range(4)]
                O.tt("dve", ta, ci_, s5_, ALU.mult)
                O.stt("dve", ir, cr_, c5, ta, ALU.mult, ALU.subtract)
                O.tt("dve", tb_, cr_, s5_, ALU.mult)
                O.stt("dve", ii, ci_, c5, tb_, ALU.mult, ALU.add)
                O.scan("dve", w4, V(Mg[j]), w0, ir)
                O.scan("dve", w5, V(Mg[j]), w2, ii)
                O.copy("pool", cr_, V(s5w[4], A, slice(TB - 1, TB)))
                O.copy("pool", ci_, V(s5w[5], A, slice(TB - 1, TB)))
                O.tt("dve", w0, w4, V(Ct[j]), ALU.mult)
                O.tt("pool", w1, w5, V(St[j]), ALU.mult)
                O.tt("dve", w0, w0, w1, ALU.subtract)
                O.tt("pool", w2, w5, V(Ct[j]), ALU.mult)
                O.tt("dve", w3, w4, V(St[j]), ALU.mult)
                O.tt("pool", w2, w2, w3, ALU.add)
                yield
                O.mm(V(yps), V(Cre[j]), w0, start=(j == 0), stop=False)
                O.mm(V(yps), V(Cim[j]), w2, start=False, stop=(j == 3))
                yield
            O.stt("dve", V(yc_sb), V(uT), V(s5d), V(yps), ALU.mult, ALU.add)
            O.act(V(ys[2]), V(yc_sb), AF.Gelu_apprx_tanh)

        def gen_gla():
            nonlocal sgl_i
            ps = psp.get()
            O.mm(V(ps, slice(0, 64)), V(gla_w2), V(lrs_b))
            O.act(V(gls), V(ps, slice(0, 64)), AF.Exp, scale=-1.0, bias=V(nb2))
            O.act(V(gls), V(gls), AF.Ln, bias=1.0)
            for b in range(NB):
                O.scan("dve", V(gc, A, bsl(b)), V(ones64), V(gls, A, bsl(b)), 0.0)
            O.act(V(gEQ), V(gc), AF.Exp, scale=-1.0 / 16, bias=math.log(1.0 / 8))
            O.act(V(gEK), V(gc), AF.Exp, scale=1.0 / 16)
            O.tt("dve", V(gqe), V(gqr), V(gEQ), ALU.mult)
            O.tt("dve", V(gke), V(gkr), V(gEK), ALU.mult)
            yield
            for b in range(NB):
                nbl, gend = V(gsc[b][0]), V(gsc[b][1])
                O.ts("dve", nbl, V(gc, A, slice(128 * b + 127, 128 * b + 128)), -1.0 / 16, ALU.mult)
                O.act(gend, nbl, AF.Exp)
                O.act(V(gkdT[b]), V(gc, A, bsl(b)), AF.Exp, scale=1.0 / 16, bias=nbl)
                O.tt("dve", V(gkdT[b]), V(gkdT[b]), V(gkr, A, bsl(b)), ALU.mult)
            yield
            for b in range(NB):
                pt = psp.get()
                O.tr(V(pt, A, slice(0, 64)), V(gkdT[b]), V(ident, slice(0, 64), slice(0, 64)))
                O.copy("act", V(gkd[b]), V(pt, A, slice(0, 64)))
                pa = psp.get()
                O.mm(V(pa, A, slice(0, 128)), V(gke, A, bsl(b)), V(gqe, A, bsl(b)))
                O.tt("dve", V(gaT[b]), V(pa, A, slice(0, 128)), V(U), ALU.mult)
                O.act(V(gsg[b]), V(tok[b], A, slice(258, 386)), AF.Silu)
                yield
            for b in range(NB):
                gv = V(tok[b], A, slice(130, 258))
                So, Sn = Sgl[sgl_i % 2], Sgl[(sgl_i + 1) % 2]
                sgl_i += 1
                po = psp.get()
                O.mm(V(po, A, slice(0, 128)), V(gqe, A, bsl(b)), V(So), start=True, stop=False)
                O.mm(V(po, A, slice(0, 128)), V(gaT[b]), gv, start=False, stop=True)
                pd = psp.get()
                O.mm(V(pd, slice(0, 64), slice(0, 128)), V(gkd[b]), gv)
                O.stt("dve", V(Sn), V(So), V(gsc[b][1]), V(pd, slice(0, 64), slice(0, 128)), ALU.mult, ALU.add)
                out_norm(V(po, A, slice(0, 128)), gss[b], gla_ng, gsg[b], gyo[b], gjunk[b % 2])
                yield
                ptr = psp.get()
                O.tr(V(ptr, A, slice(0, 128)), V(gyo[b]), V(ident))
                O.copy("act", V(ys[1], A, bsl(b)), V(ptr, A, slice(0, 128)))
                yield

        def gen_dn():
            nonlocal sdn_i
            pAs = []
            for b in range(NB):
                beta, glog, gam, glast, ngam, egam, bg, edl, gend, tmp = [V(sc[b][i]) for i in range(10)]
                O.act(beta, V(tok[b], A, slice(128, 129)), AF.Sigmoid)
                O.act(tmp, V(tok[b], A, slice(129, 130)), AF.Exp, bias=V(dn_sc, A, slice(1, 2)))
                O.act(tmp, tmp, AF.Ln, bias=1.0)
                O.tt("dve", glog, tmp, V(negA), ALU.mult)
                O.ts("dve", V(Ug[b]), V(U), glog, ALU.mult)
                yield
                pA = psp.get()
                pAs.append(pA)
                O.mm(V(pA, A, slice(0, 128)), V(ones, A, slice(0, 128)), V(Ug[b]))
                O.mm(V(pA, A, slice(128, 129)), V(U), glog)
                O.mm(V(pA, A, slice(129, 130)), V(ones, A, slice(0, 128)), glog)
                O.copy("dve", gam, V(pA, A, slice(128, 129)))
                O.copy("dve", glast, V(pA, A, slice(129, 130)))
                O.ts("dve", ngam, gam, -1.0, ALU.mult)
                O.act(egam, gam, AF.Exp)
                O.tt("dve", bg, beta, egam, ALU.mult)
                O.act(edl, gam, AF.Exp, scale=-1.0, bias=glast)
                O.act(gend, glast, AF.Exp)
                gbc = V(pA, A, slice(0, 128))
                O.stt("dve", V(Eb[b]), gbc, -1.0, V(mneg), ALU.mult, ALU.add)
                O.act(V(Eb[b]), V(Eb[b]), AF.Exp, bias=gam)
                O.tt("dve", V(ETb[b]), gbc, V(mnegT), ALU.add)
                O.act(V(qd[b]), gbc, AF.Exp)
                yield
                O.act(V(ETb[b]), V(ETb[b]), AF.Exp, bias=ngam)
                O.tt("dve", V(qd[b]), V(qd[b]), V(qn, A, bsl(b)), ALU.mult)
                O.act(V(sg[b]), V(tok[b], A, slice(0, 128)), AF.Silu)
                yield
            for b in range(NB):
                beta, bg, edl = V(sc[b][0]), V(sc[b][6]), V(sc[b][7])
                pt = psp.get()
                O.tr(V(pt, A, slice(0, 128)), V(kn, A, bsl(b)), V(ident))
                O.tr(V(pt, A, slice(128, 256)), V(vs, A, bsl(b)), V(ident))
                O.ts("dve", V(bk[b]), V(pt, A, slice(0, 128)), bg, ALU.mult)
                O.act(V(kdec[b]), V(pt, A, slice(0, 128)), AF.Identity, scale=edl)
                O.ts("dve", V(bv[b]), V(pt, A, slice(128, 256)), beta, ALU.mult)
            yield
            for b in range(NB):
                beta = V(sc[b][0])
                pk = psp.get()
                O.mm(V(pk, A, slice(0, 128)), V(kn, A, bsl(b)), V(kn, A, bsl(b)))
                O.mm(V(pk, A, slice(128, 256)), V(kn, A, bsl(b)), V(qn, A, bsl(b)))
                O.stt("dve", V(M[b], A, slice(0, 128)), V(pk, A, slice(0, 128)), beta, V(Eb[b]), ALU.mult, ALU.mult)
                O.tt("pool", V(M[b], A, slice(0, 128)), V(M[b], A, slice(0, 128)), V(nLs), ALU.mult)
                O.tt("dve", V(attnT[b]), V(pk, A, slice(128, 256)), V(ETb[b]), ALU.mult)
            yield
            for b in range(NB):
                pt = psp.get()
                O.tr(V(pt, A, slice(0, 128)), V(M[b], A, slice(0, 128)), V(ident))
                O.copy("act", V(M[b], A, slice(128, 256)), V(pt, A, slice(0, 128)))
                O.tt("dve", V(X[b]), V(pt, A, slice(0, 128)), V(ident), ALU.add)
            dn_flag[0] = True
            yield
            for lev in range(1, 8):
                pls = []
                for b in range(NB):
                    pl = psp.get()
                    pls.append(pl)
                    Mv, MTv = V(M[b], A, slice(0, 128)), V(M[b], A, slice(128, 256))
                    if lev <= 6:
                        O.mm(V(pl, A, slice(0, 128)), MTv, Mv)
                        O.mm(V(pl, A, slice(128, 256)), Mv, MTv)
                    if lev >= 2:
                        O.mm(V(pl, A, slice(256, 384)), Mv, V(X[b]))
                for b in range(NB):
                    pl = pls[b]
                    if lev <= 6:
                        O.copy("act", V(M[b]), V(pl, A, slice(0, 256)))
                    if lev >= 2:
                        O.tt("dve", V(X[b]), V(X[b]), V(pl, A, slice(256, 384)), ALU.add)
                yield
            for b in range(NB):
                pe_ = psp.get()
                O.mm(V(pe_, A, slice(0, 128)), V(X[b]), V(bv[b]))
                O.mm(V(pe_, A, slice(128, 256)), V(bk[b]), V(X[b]))
                O.copy("act", V(un[b]), V(pe_, A, slice(0, 128)))
                O.copy("act", V(wT[b]), V(pe_, A, slice(128, 256)))
            yield
            for b in range(NB):
                So, Sn = Sdn[sdn_i % 2], Sdn[(sdn_i + 1) % 2]
                sdn_i += 1
                pw = psp.get()
                O.mm(V(pw, A, slice(0, 128)), V(wT[b]), V(So))
                O.tt("dve", V(ub[b]), V(un[b]), V(pw, A, slice(0, 128)), ALU.subtract)
                yield
                po = psp.get()
                O.mm(V(po, A, slice(0, 128)), V(qd[b]), V(So), start=True, stop=False)
                O.mm(V(po, A, slice(0, 128)), V(attnT[b]), V(ub[b]), start=False, stop=True)
                pd = psp.get()
                O.mm(V(pd, A, slice(0, 128)), V(kdec[b]), V(ub[b]))
                O.stt("dve", V(Sn), V(So), V(sc[b][8]), V(pd, A, slice(0, 128)), ALU.mult, ALU.add)
                out_norm(V(po, A, slice(0, 128)), (sc[b][10], sc[b][11]), dn_ng, sg[b], yo[b], junk[b % 2])
                yield
                ptr = psp.get()
                O.tr(V(ptr, A, slice(0, 128)), V(yo[b]), V(ident))
                O.copy("act", V(ys[0], A, bsl(b)), V(ptr, A, slice(0, 128)))
                yield
        g_dn, g_s5, g_gla = gen_dn(), gen_s5(), gen_gla()
        tasks = [g_dn, g_s5, g_gla, g_gla]
        nf = gen_front((sb_i + 1) * TB) if sb_i + 1 < nsb else None
        started = False
        while tasks:
            if nf is not None and not started and dn_flag[0] and g_s5 not in tasks and g_gla not in tasks:
                tasks.append(nf)
                started = True
            for g in list(tasks):
                if g not in tasks:
                    continue
                try:
                    next(g)
                except StopIteration:
                    while g in tasks:
                        tasks.remove(g)
        if nf is not None and not started:
            for _ in nf:
                pass
        for i3 in range(3):
            yout(i3, t0_, ys[i3])
        after_sb(t0_)


def emit_dense(nc, P, TOK, last, sfx, src, NT=512):
    def din(name, shape):
        return nc.dram_tensor(name + sfx, list(shape), F32, kind="ExternalInput").ap()
    HT = TOK + 2
    pT = din("pT", [256, TOK])
    w_gate = din("w_gate", [D, 3 * D])
    b_gate = din("b_gate", [128, 24])
    w_branch = din("w_branch", [1536, D])
    w_o = din("w_o", [D, D])
    w_glu = din("w_glu", [512, 512])
    b_glu = din("b_glu", [128, 4])
    norms = din("norms", [128, 32])
    w_up = din("w_up", [D, 2 * DFF])
    cw = din("cw", [128, NF * 3])
    cb = din("cb", [128, NF])
    w_down = din("w_down", [DFF, D])
    w_pg = din("w_pg", [D, D])
    w_pp = din("w_pp", [256, D])
    psp = PsPool(P)
    ones = P.sbuf([128, 128], F32, "ones")
    P.op("pool", lambda e: e.memset(ones.ap, 1.0), writes=[ones])
    c_bg = P.sbuf([128, 24], F32, "c_bg")
    c_bglu = P.sbuf([128, 4], F32, "c_bglu")
    c_norm = P.sbuf([128, 32], F32, "c_norm")
    c_cw = P.sbuf([128, NF * 3], F32, "c_cw")
    c_cb = P.sbuf([128, NF], F32, "c_cb")
    hmask = P.sbuf([128, 1], F32, "hmask")
    for t, srcd in ((c_bg, b_gate), (c_bglu, b_glu), (c_norm, norms), (c_cw, cw), (c_cb, cb), (hmask, src["hmask"])):
        P.dma(t.ap, srcd, writes=[t])

    NSLAB = 5
    slab_t = [P.sb([128, 8, 1024], BF16, f"slab{i}") for i in range(NSLAB)]
    slabs = [Buf(t[:], f"slab{i}") for i, t in enumerate(slab_t)]
    slab_i = [0]

    def load_w(src, r0, nk, c0, ncol):
        s = slabs[slab_i[0] % NSLAB]
        slab_i[0] += 1
        view = s.ap[:, 0:nk, 0:ncol]
        srcv = src[r0:r0 + nk * 128, c0:c0 + ncol].rearrange("(k p) n -> p k n", p=128)
        P.dma(view, srcv, writes=[s], q="pool")
        return s

    x_t = P.sb([128, 8, NT], F32, "x_t")
    xb = [Buf(x_t[:, k, :], f"x{k}") for k in range(8)]
    h_t = P.sb([128, 8, NT], BF16, "h_t")
    hb = [Buf(h_t[:, k, :], f"h{k}") for k in range(8)]
    y_t = P.sb([128, 12, NT], BF16, "y_t")
    yb = [Buf(y_t[:, k, :], f"y{k}") for k in range(12)]
    yc_t = P.sb([128, 4, NT], BF16, "yc_t")
    ycb = [Buf(yc_t[:, k, :], f"yc{k}") for k in range(4)]
    m_t = P.sb([128, 8, NT], F32, "m_t")
    mb = [Buf(m_t[:, k, :], f"m{k}") for k in range(8)]
    mbf_t = P.sb([128, 8, NT], BF16, "mbf_t")
    mbfb = [Buf(mbf_t[:, k, :], f"mbf{k}") for k in range(8)]
    a_t = P.sb([128, NF, NT], BF16, "a_t")
    ab = [Buf(a_t[:, k, :], f"a{k}") for k in range(NF)]
    p_t = P.sb([128, 2, NT], BF16, "p_t")
    pb = [Buf(p_t[:, k, :], f"p{k}") for k in range(2)]
    hist_t = P.sb([128, NF, 2], F32, "hist_t")
    histb = [Buf(hist_t[:, k, :], f"hist{k}") for k in range(NF)]
    sq_t = P.sb([128, 2, NT], F32, "sq_t")
    sqb = [Buf(sq_t[:, k, :], f"sq{k}") for k in range(2)]
    rstd = P.sbuf([128, NT], F32, "rstd")
    NG = 3
    g_t = P.sb([128, NG, NT], F32, "g_t")
    gtb = [Buf(g_t[:, k, :], f"gt{k}") for k in range(NG)]
    gi = [0]
    gb_t = P.sb([128, 2, NT + 2], F32, "gb_t")
    gbb = [Buf(gb_t[:, k, :], f"gb{k}") for k in range(2)]
    cv_t = P.sb([128, 2, NT], F32, "cv_t")
    cvb = [Buf(cv_t[:, k, :], f"cv{k}") for k in range(2)]
    ob = mb

    def rmsnorm(w, ncol, outs):
        ps = psp.get()
        for k in range(8):
            s = sqb[k % 2]
            P.op("act", lambda e, s=s, k=k: e.activation(s.ap[:, :w], xb[k].ap[:, :w], AF.Square),
                 reads=[xb[k]], writes=[s])
            P.mm(ps.ap[:, :w], ones.ap, s.ap[:, :w], start=(k == 0), stop=(k == 7),
                 reads=[ones, s], writes=[ps])
        P.op("act", lambda e: e.activation(rstd.ap[:, :w], ps.ap[:, :w], AF.Sqrt, bias=EPS, scale=1.0 / D),
             reads=[ps], writes=[rstd])
        P.op("dve", lambda e: e.reciprocal(rstd.ap[:, :w], rstd.ap[:, :w]), reads=[rstd], writes=[rstd])
        for k in range(8):
            P.op("dve", lambda e, k=k: e.scalar_tensor_tensor(
                outs[k].ap[:, :w], xb[k].ap[:, :w], c_norm.ap[:, ncol + k:ncol + k + 1], rstd.ap[:, :w],
                op0=ALU.mult, op1=ALU.mult), reads=[xb[k], c_norm, rstd], writes=[outs[k]])

    def tile(c0, w, halo):
        for k in range(8):
            src["x"](P, xb[k], k, c0, w, halo)
        for k in range(12):
            src["y"](P, yb[k], k, c0, w, halo)
        if halo:
            for k in range(8):
                P.op("dve", lambda e, k=k: e.tensor_scalar(xb[k].ap[:, :w], xb[k].ap[:, :w], hmask.ap[:, 0:1], None,
                                                           op0=ALU.mult), reads=[xb[k], hmask], writes=[xb[k]])
            for k in range(12):
                P.op("dve", lambda e, k=k: e.tensor_scalar(yb[k].ap[:, :w], yb[k].ap[:, :w], hmask.ap[:, 0:1], None,
                                                           op0=ALU.mult), reads=[yb[k], hmask], writes=[yb[k]])
        if not halo:
            for k in range(2):
                P.dma(pb[k].ap[:, :w], pT[k * 128:(k + 1) * 128, c0 - 2:c0 - 2 + w], writes=[pb[k]], q="pool")
        rmsnorm(w, 0, hb)
        U = load_w(w_glu, 0, 4, 0, 512)
        for m in range(4):
            ps = psp.get()
            for k in range(4):
                P.mm(ps.ap[:, :w], U.ap[:, k, m * 128:(m + 1) * 128], yb[8 + k].ap[:, :w],
                     start=(k == 0), stop=(k == 3), reads=[U, yb[8 + k]], writes=[ps])
            g = gtb[gi[0] % NG]; gi[0] += 1
            P.op("act", lambda e, g=g, ps=ps, m=m: e.activation(g.ap[:, :w], ps.ap[:, :w], AF.Sigmoid,
                                                             bias=c_bglu.ap[:, m:m + 1]),
                 reads=[ps, c_bglu], writes=[g])
            P.op("dve", lambda e, g=g, m=m: e.tensor_tensor(ycb[m].ap[:, :w], yb[8 + m].ap[:, :w], g.ap[:, :w],
                                                          op=ALU.mult),
                 reads=[yb[8 + m], g], writes=[ycb[m]])
        for i in range(3):
            G = load_w(w_gate, 0, 8, i * D, D)
            B = load_w(w_branch, i * 512, 4, 0, D)
            ysrc = [yb[0], yb[1], yb[2], yb[3]] if i == 0 else ([yb[4], yb[5], yb[6], yb[7]] if i == 1 else ycb)
            for m in range(8):
                pg = psp.get()
                for k in range(8):
                    P.mm(pg.ap[:, :w], G.ap[:, k, m * 128:(m + 1) * 128], hb[k].ap[:, :w],
                         start=(k == 0), stop=(k == 7), reads=[G, hb[k]], writes=[pg])
                g = gtb[gi[0] % NG]; gi[0] += 1
                P.op("act", lambda e, g=g, pg=pg, i=i, m=m: e.activation(
                    g.ap[:, :w], pg.ap[:, :w], AF.Sigmoid, bias=c_bg.ap[:, i * 8 + m:i * 8 + m + 1]),
                    reads=[pg, c_bg], writes=[g])
                pp = psp.get()
                for k in range(4):
                    P.mm(pp.ap[:, :w], B.ap[:, k, m * 128:(m + 1) * 128], ysrc[k].ap[:, :w],
                         start=(k == 0), stop=(k == 3), reads=[B, ysrc[k]], writes=[pp])
                if i == 0:
                    P.op("dve", lambda e, g=g, pp=pp, m=m: e.tensor_tensor(
                        mb[m].ap[:, :w], pp.ap[:, :w], g.ap[:, :w], op=ALU.mult),
                        reads=[pp, g], writes=[mb[m]])
                else:
                    P.op("dve", lambda e, g=g, pp=pp, m=m: e.tensor_tensor(
                        g.ap[:, :w], pp.ap[:, :w], g.ap[:, :w], op=ALU.mult),
                        reads=[pp, g], writes=[g])
                    dst = mb[m] if i == 1 else mbfb[m]
                    P.op("pool", lambda e, g=g, m=m, dst=dst: e.tensor_tensor(
                        dst.ap[:, :w], mb[m].ap[:, :w], g.ap[:, :w], op=ALU.add),
                        reads=[mb[m], g], writes=[dst])
        O = load_w(w_o, 0, 8, 0, D)
        for m in range(8):
            ps = psp.get()
            for k in range(8):
                P.mm(ps.ap[:, :w], O.ap[:, k, m * 128:(m + 1) * 128], mbfb[k].ap[:, :w],
                     start=(k == 0), stop=(k == 7), reads=[O, mbfb[k]], writes=[ps])
            P.op("dve", lambda e, ps=ps, m=m: e.tensor_tensor(xb[m].ap[:, :w], xb[m].ap[:, :w], ps.ap[:, :w],
                                                            op=ALU.add),
                 reads=[xb[m], ps], writes=[xb[m]])
        rmsnorm(w, 8, hb)
        for j0 in range(0, NF, 8):
            nj = min(8, NF - j0)
            Wg = load_w(w_up, 0, 8, j0 * 128, nj * 128)
            Wu = None if halo else load_w(w_up, 0, 8, DFF + j0 * 128, nj * 128)
            for jj in range(nj):
                j = j0 + jj
                pg = psp.get()
                for k in range(8):
                    P.mm(pg.ap[:, :w], Wg.ap[:, k, jj * 128:(jj + 1) * 128], hb[k].ap[:, :w],
                         start=(k == 0), stop=(k == 7), reads=[Wg, hb[k]], writes=[pg])
                if halo:
                    P.op("act", lambda e, pg=pg, j=j: e.activation(histb[j].ap, pg.ap[:, 0:2], AF.Identity),
                         reads=[pg], writes=[histb[j]])
                    continue
                pu = psp.get()
                for k in range(8):
                    P.mm(pu.ap[:, :w], Wu.ap[:, k, jj * 128:(jj + 1) * 128], hb[k].ap[:, :w],
                         start=(k == 0), stop=(k == 7), reads=[Wu, hb[k]], writes=[pu])
                gb = gbb[j % 2]
                cv = cvb[j % 2]
                P.op("pool", lambda e, gb=gb, j=j: e.tensor_copy(gb.ap[:, 0:2], histb[j].ap),
                     reads=[histb[j]], writes=[gb])
                P.op("act", lambda e, gb=gb, pg=pg: e.activation(gb.ap[:, 2:2 + w], pg.ap[:, :w], AF.Identity),
                     reads=[pg], writes=[gb])
                P.op("pool", lambda e, gb=gb, j=j: e.tensor_copy(histb[j].ap, gb.ap[:, w:w + 2]),
                     reads=[gb], writes=[histb[j]])
                P.op("dve", lambda e, gb=gb, cv=cv, j=j: e.tensor_scalar(
                    cv.ap[:, :w], gb.ap[:, 0:w], c_cw.ap[:, 3 * j:3 * j + 1], c_cb.ap[:, j:j + 1],
                    op0=ALU.mult, op1=ALU.add), reads=[gb, c_cw, c_cb], writes=[cv])
                for t in (1, 2):
                    P.op("dve", lambda e, gb=gb, cv=cv, j=j, t=t: e.scalar_tensor_tensor(
                        cv.ap[:, :w], gb.ap[:, t:t + w], c_cw.ap[:, 3 * j + t:3 * j + t + 1], cv.ap[:, :w],
                        op0=ALU.mult, op1=ALU.add), reads=[gb, c_cw, cv], writes=[cv])
                P.op("act", lambda e, cv=cv: e.activation(cv.ap[:, :w], cv.ap[:, :w], AF.Gelu_apprx_tanh),
                     reads=[cv], writes=[cv])
                P.op("dve", lambda e, cv=cv, pu=pu, j=j: e.tensor_tensor(
                    ab[j].ap[:, :w], pu.ap[:, :w], cv.ap[:, :w], op=ALU.mult),
                    reads=[pu, cv], writes=[ab[j]])
        if halo:
            return
        Ds = [load_w(w_down, j0 * 128, min(8, NF - j0), 0, D) for j0 in range(0, NF, 8)]
        for m in range(8):
            ps = psp.get()
            for j in range(NF):
                Dj = Ds[j // 8]
                P.mm(ps.ap[:, :w], Dj.ap[:, j % 8, m * 128:(m + 1) * 128], ab[j].ap[:, :w],
                     start=(j == 0), stop=(j == NF - 1), reads=[Dj, ab[j]], writes=[ps])
            P.op("dve", lambda e, ps=ps, m=m: e.tensor_tensor(xb[m].ap[:, :w], xb[m].ap[:, :w], ps.ap[:, :w],
                                                            op=ALU.add),
                 reads=[xb[m], ps], writes=[xb[m]])
        rmsnorm(w, 16, hb)
        PG = load_w(w_pg, 0, 8, 0, D)
        PP = load_w(w_pp, 0, 2, 0, D)
        for m in range(8):
            pg = psp.get()
            for k in range(8):
                P.mm(pg.ap[:, :w], PG.ap[:, k, m * 128:(m + 1) * 128], hb[k].ap[:, :w],
                     start=(k == 0), stop=(k == 7), reads=[PG, hb[k]], writes=[pg])
            g = gtb[gi[0] % NG]; gi[0] += 1
            P.op("act", lambda e, g=g, pg=pg: e.activation(g.ap[:, :w], pg.ap[:, :w], AF.Sigmoid),
                 reads=[pg], writes=[g])
            pp = psp.get()
            for k in range(2):
                P.mm(pp.ap[:, :w], PP.ap[:, k, m * 128:(m + 1) * 128], pb[k].ap[:, :w],
                     start=(k == 0), stop=(k == 1), reads=[PP, pb[k]], writes=[pp])
            P.op("dve", lambda e, g=g, pp=pp: e.tensor_tensor(g.ap[:, :w], pp.ap[:, :w], g.ap[:, :w], op=ALU.mult),
                 reads=[pp, g], writes=[g])
            P.op("pool", lambda e, g=g, m=m: e.tensor_tensor(xb[m].ap[:, :w], xb[m].ap[:, :w], g.ap[:, :w],
                                                           op=ALU.add),
                 reads=[xb[m], g], writes=[xb[m]])
        if last:
            rmsnorm(w, 24, ob)
            src_o = ob
        else:
            src_o = xb
        for m in range(8):
            src["out"](P, src_o[m], m, c0 - 2, w)
        src["after_tile"](P, c0 - 2)

    tile(0, 2, True)
    for t0 in range(0, TOK, NT):
        tile(2 + t0, min(NT, TOK - t0), False)


IN_OFF = dict(q=0, k=512, v=1024, b=1536, a=1540, g=1544, gq=2056, gk=2312, gv=2568, lr=3080, gr=3096, s5=3608)


def mixer_inputs(layer, hd, x_b, W):
    i = layer
    w_in = W["w_in"][i]
    o = IN_OFF
    cols_f = np.concatenate([
        np.arange(o["q"] + hd * 128, o["q"] + hd * 128 + 128),
        np.arange(o["k"] + hd * 128, o["k"] + hd * 128 + 128),
        np.arange(o["v"] + hd * 128, o["v"] + hd * 128 + 128),
        np.arange(o["s5"] + hd * 128, o["s5"] + hd * 128 + 128),
        np.arange(o["gq"] + hd * 64, o["gq"] + hd * 64 + 64),
        np.arange(o["gk"] + hd * 64, o["gk"] + hd * 64 + 64),
        np.arange(o["lr"], o["lr"] + 16)])
    cols_t = np.concatenate([
        np.arange(o["g"] + hd * 128, o["g"] + hd * 128 + 128),
        [o["b"] + hd], [o["a"] + hd],
        np.arange(o["gv"] + hd * 128, o["gv"] + hd * 128 + 128),
        np.arange(o["gr"] + hd * 128, o["gr"] + hd * 128 + 128)])
    cwv = W["dn_conv_w"][i]
    dn_cw = np.stack([cwv[:, c * 512 + hd * 128: c * 512 + hd * 128 + 128].T for c in range(3)], axis=1)
    rep = lambda v: np.ascontiguousarray(np.broadcast_to(v[None, :], (128, v.shape[0])))
    g0 = hd * 8
    lam = np.zeros((128, 12), np.float32)
    sb_ = np.zeros((2, 4, 128, 16), np.float32)
    sc_ = np.zeros((2, 4, 128, 16), np.float32)
    for j in range(4):
        for hf in range(2):
            g = g0 + 2 * j + hf
            lam[64 * hf:64 * hf + 64, j] = W["s5_lam_re"][i][g]
            lam[64 * hf:64 * hf + 64, 4 + j] = W["s5_lam_im"][i][g]
            lam[64 * hf:64 * hf + 64, 8 + j] = W["s5_log_step"][i][g]
            sb_[0, j, 64 * hf:64 * hf + 64] = W["s5_b_re"][i][g]
            sb_[1, j, 64 * hf:64 * hf + 64] = W["s5_b_im"][i][g]
            sc_[0, j, 64 * hf:64 * hf + 64] = W["s5_c_re"][i][g].T
            sc_[1, j, 64 * hf:64 * hf + 64] = W["s5_c_im"][i][g].T
    return {
        "anorm": np.ascontiguousarray(W["attn_norm"][i].reshape(8, 128).T),
        "wf": np.ascontiguousarray(w_in[:, cols_f]), "wt": np.ascontiguousarray(w_in[:, cols_t]),
        "dn_sc": np.ascontiguousarray(np.stack([np.full(128, W["dn_a_log"][i][hd], np.float32),
                                               np.full(128, W["dn_dt_bias"][i][hd], np.float32)], axis=1)),
        "dn_cw": np.ascontiguousarray(dn_cw.reshape(128, 12)),
        "dn_ng": rep(W["dn_norm"][i]), "gla_ng": rep(W["gla_norm"][i]),
        "gla_w2": np.ascontiguousarray(W["gla_w2"][i][:, hd * 64:hd * 64 + 64]),
        "gla_b2": np.ascontiguousarray(W["gla_b2"][i][hd * 64:hd * 64 + 64].reshape(64, 1)),
        "s5_lam": lam, "s5_b": sb_, "s5_c": sc_,
        "s5_d": np.ascontiguousarray(W["s5_d"][i][hd * 128:hd * 128 + 128].reshape(128, 1)),
    }


def dense_inputs(layer, x_seg, y_seg, p_seg, W):
    i = layer
    f = np.float32
    col = lambda v: np.ascontiguousarray(v.reshape(-1, 128).T)
    norms = np.concatenate([col(W["attn_norm"][i]), col(W["ffn_norm"][i]), col(W["ple_norm"][i]),
                            col(W["final_norm"])], axis=1)
    cw = np.ascontiguousarray(W["ffn_conv_w"][i].reshape(3, NF, 128).transpose(2, 1, 0).reshape(128, NF * 3))
    return {
        "pT": np.ascontiguousarray(p_seg.T),
        "w_gate": W["w_gate"][i], "b_gate": col(W["b_gate"][i]),
        "w_branch": np.ascontiguousarray(W["w_branch"][i].reshape(1536, D)),
        "w_o": W["w_o"][i], "w_glu": W["s5_w_glu"][i], "b_glu": col(W["s5_b_glu"][i]),
        "norms": np.ascontiguousarray(norms), "w_up": W["w_up"][i], "cw": cw, "cb": col(W["ffn_conv_b"][i]),
        "w_down": W["w_down"][i], "w_pg": W["w_ple_gate"][i], "w_pp": W["w_ple_proj"][i],
    }


import os
NOCOLL = os.environ.get('FUSE_NOCOLL') == '1'
NODYN = os.environ.get('FUSE_NODYN') == '1'


def build_fused(L=SEQ):
    nc = bass.Bass("TRN2", target_bir_lowering=False, num_devices=NCORES)
    TOK = L // 4
    CH = 1024
    NCH = L // CH
    CPS = TOK // CH
    NT8 = TOK // 512
    assert CPS >= 1 and TOK % 512 == 0
    xT_b = nc.dram_tensor("xT_b", [D, L], F32, kind="ExternalInput").ap()
    xs0 = nc.dram_tensor("xs0", [D, TOK + 2], F32, kind="ExternalInput").ap()
    hmask_d = nc.dram_tensor("hmask", [128, 1], F32, kind="ExternalInput").ap()
    oT = nc.dram_tensor("oT", [D, TOK], F32, kind="ExternalOutput").ap()
    yloc = [[nc.dram_tensor(f"yloc{l}_{c}", [384, CH], BF16).ap() for c in range(NCH)] for l in range(2)]
    yall_t = [nc.dram_tensor(f"yall{l}", [NCH * 1536, CH], BF16).ap() for l in range(2)]
    x1c = [[nc.dram_tensor(f"x1c_{t}_{h}", [512, 512], F32).ap() for h in range(2)] for t in range(NT8)]
    xgc = [[nc.dram_tensor(f"xgc_{t}_{h}", [4 * 512, 512], F32).ap() for h in range(2)] for t in range(NT8)]
    yseg = nc.dram_tensor("yseg", [4 * 384, TOK + 2], BF16).ap()
    xh = nc.dram_tensor("xh", [D, 2], F32).ap()
    Byloc = [[Buf(yloc[l][c]) for c in range(NCH)] for l in range(2)]
    Byall = [[Buf(yall_t[l][c * 1536:(c + 1) * 1536, :]) for c in range(NCH)] for l in range(2)]
    Bx1c = [[Buf(x1c[t][h]) for h in range(2)] for t in range(NT8)]
    Bxgc = [[Buf(xgc[t][h]) for h in range(2)] for t in range(NT8)]
    Byseg, Bxh = Buf(yseg, "yseg"), Buf(xh, "xh")
    rv = {}
    P = Prog(nc)
    O = Ops(P)

    for layer in (0, 1):
        sfx = f"_{layer}"
        P.prefix = f"m{layer}_"
        P.begin_phase()
        if layer == 0:
            def xsrc(k, t0):
                return xT_b[k * 128:(k + 1) * 128, t0:t0 + TB]
        else:
            def xsrc(k, t0):
                sg_, tt = t0 // TOK, (t0 % TOK) // 512
                r0 = sg_ * 512 + (k % 4) * 128
                return View(Bxgc[tt][k // 4], xgc[tt][k // 4][r0:r0 + 128, :])

        def yout(i3, t0, ysb, layer=layer):
            ci, off = t0 // CH, t0 % CH
            P.dma(yloc[layer][ci][i3 * 128:(i3 + 1) * 128, off:off + TB], ysb.ap, reads=[ysb], writes=[Byloc[layer][ci]])

        def after_sb(t0, layer=layer):
            ci, off = t0 // CH, t0 % CH
            if off + TB == CH and not NOCOLL:
                P.coll("AllGather", yloc[layer][ci], yall_t[layer][ci * 1536:(ci + 1) * 1536, :], GROUPS,
                       reads=[Byloc[layer][ci]], writes=[Byall[layer][ci]])
        emit_mixer(nc, P, O, L, sfx, xsrc, yout, after_sb)
        P.end_phase()
        P.prefix = f"d{layer}_"
        P.begin_phase()
        if layer == 0:
            def setup(e, rv=rv):
                r = e.snap(e.partition_id() % 4, min_val=0, max_val=3)
                rv["yrow"] = e.snap(r * (CPS * 1536), min_val=0, max_val=3 * CPS * 1536)
                rv["hrow"] = e.snap(((r * CPS + (NCH - 1)) % NCH) * 1536, min_val=0, max_val=(NCH - 1) * 1536)
                rv["prow"] = e.snap(((r + 3) % 4) * 512, min_val=0, max_val=3 * 512)
                return None
            P.items["sp"].append(([], setup, None, 0))
        ya = yall_t[layer]
        for j in range(CPS):
            P.dma(yseg[:, 2 + j * CH:2 + (j + 1) * CH],
                  (lambda e, j=j, ya=ya: ya[(slice(j * 1536, (j + 1) * 1536) if NODYN else bass.ds(rv["yrow"] + j * 1536, 1536)), :]),
                  reads=Byall[layer], writes=[Byseg])
        P.dma(yseg[:, 0:2], (lambda e, ya=ya: ya[(slice(0, 1536) if NODYN else bass.ds(rv["hrow"], 1536)), CH - 2:CH]),
              reads=Byall[layer], writes=[Byseg])
        if layer == 1:
            for h in range(2):
                P.dma(xh[h * 512:(h + 1) * 512, :],
                      (lambda e, h=h: xgc[NT8 - 1][h][(slice(0, 512) if NODYN else bass.ds(rv["prow"], 512)), 510:512]),
                      reads=[Bxgc[NT8 - 1][h]], writes=[Bxh])

        def fx(P_, buf, k, c0, w, halo, layer=layer):
            if layer == 0:
                P_.dma(buf.ap[:, :w], xs0[k * 128:(k + 1) * 128, c0:c0 + w], writes=[buf])
            elif halo:
                P_.dma(buf.ap[:, :2], xh[k * 128:(k + 1) * 128, :], reads=[Bxh], writes=[buf])
            else:
                tt = (c0 - 2) // 512
                P_.dma(buf.ap[:, :w], x1c[tt][k // 4][(k % 4) * 128:(k % 4) * 128 + 128, 0:w],
                       reads=[Bx1c[tt][k // 4]], writes=[buf])

        def fy(P_, buf, k, c0, w, halo):
            row0 = (k % 4) * 384 + (k // 4) * 128
            P_.dma(buf.ap[:, :w], yseg[row0:row0 + 128, c0:c0 + w], reads=[Byseg], writes=[buf])

        def fo(P_, buf, m_, t, w, layer=layer):
            if layer == 0:
                tt = t // 512
                P_.dma(x1c[tt][m_ // 4][(m_ % 4) * 128:(m_ % 4) * 128 + 128, 0:w], buf.ap[:, :w],
                       reads=[buf], writes=[Bx1c[tt][m_ // 4]])
            else:
                P_.dma(oT[m_ * 128:(m_ + 1) * 128, t:t + w], buf.ap[:, :w], reads=[buf])

        def after_tile(P_, t, layer=layer):
            if layer == 0 and not NOCOLL:
                tt = t // 512
                for h in range(2):
                    P_.coll("AllGather", x1c[tt][h], xgc[tt][h], GROUPS, reads=[Bx1c[tt][h]], writes=[Bxgc[tt][h]])

        emit_dense(nc, P, TOK, layer == 1, sfx, dict(hmask=hmask_d, x=fx, y=fy, out=fo, after_tile=after_tile))
        P.end_phase(final=(layer == 1))
    P.close()
    return nc


def kernel(**inputs):
    W = {k: np.asarray(v, dtype=np.float32) for k, v in inputs.items()}
    x = np.ascontiguousarray(W.pop("x"))
    p = W.pop("p")
    Bsz, L, _ = x.shape
    depth = W["w_in"].shape[0]
    assert depth == 2 and Bsz == 2
    TOK = L // 4
    nc = build_fused(L)
    xT = [np.ascontiguousarray(x[b].T) for b in range(Bsz)]
    in_maps = []
    for c in range(NCORES):
        b, r = c // 4, c % 4
        s0 = r * TOK
        xs = np.zeros((D, TOK + 2), np.float32)
        lo = max(s0 - 2, 0)
        xs[:, 2 - (s0 - lo):] = xT[b][:, lo:s0 + TOK]
        im = {"xT_b": xT[b], "xs0": xs,
              "hmask": np.full((128, 1), 0.0 if r == 0 else 1.0, np.float32)}
        for i in range(depth):
            for k, v in mixer_inputs(i, r, None, W).items():
                im[f"{k}_{i}"] = v
            for k, v in dense_inputs(i, None, None, p[i, b, s0:s0 + TOK], W).items():
                im[f"{k}_{i}"] = v
        in_maps.append(im)
    res = run_bass_kernel_spmd(nc, in_maps, core_ids=list(range(NCORES)))
    out = np.empty_like(x)
    for c in range(NCORES):
        b, r = c // 4, c % 4
        out[b, r * TOK:(r + 1) * TOK] = res.results[c]["oT"].T
    return out
```

```python
import numpy as np
from contextlib import ExitStack
import concourse.bass as bass
import concourse.mybir as mybir
from concourse.bass_utils import run_bass_kernel_spmd

F32 = mybir.dt.float32
BF16 = mybir.dt.bfloat16
AF = mybir.ActivationFunctionType
ALU = mybir.AluOpType
AX = mybir.AxisListType

ENGS = ("pe", "act", "dve", "pool", "sp")
EPOCH = 16000
RING = 12


class Buf:
    __slots__ = ("ap", "w", "r", "name", "psum")

    def __init__(self, ap, name=""):
        self.ap = ap
        self.psum = False
        self.w = None
        self.r = []
        self.name = name

    def __getitem__(self, k):
        return self.ap[k]


class Prog:
    def __init__(self, nc, self_sync=True, prefix=""):
        self.nc = nc
        self.prefix = prefix
        self.es = ExitStack()
        self.es_sem = ExitStack()
        self.phase_finals = []
        self.items = {e: [] for e in ENGS}
        self.count = {e: 0 for e in ENGS}
        self.waited = {e: {} for e in ENGS}
        self.self_sync = self_sync
        self.semh = {}
        self.dma_n = {"sp": 0, "pool": 0, "act": 0}
        self.dma_last = {}
        self.nbuf = 0

    def sem(self, key):
        if key not in self.semh:
            nm = self.prefix + "s_" + "_".join(str(k) for k in key)
            self.semh[key] = self.es_sem.enter_context(self.nc.semaphore(nm))
        return self.semh[key]

    def sb(self, shape, dtype=F32, name=None):
        self.nbuf += 1
        name = self.prefix + (name or f"sb{self.nbuf}")
        t = self.es.enter_context(self.nc.sbuf_tensor(name, list(shape), dtype))
        return t

    def ps(self, shape, dtype=F32, name=None):
        self.nbuf += 1
        name = self.prefix + (name or f"ps{self.nbuf}")
        t = self.es.enter_context(self.nc.psum_tensor(name, list(shape), dtype))
        return t

    def buf(self, ap, name=""):
        return Buf(ap, name)

    def sbuf(self, shape, dtype=F32, name=None):
        t = self.sb(shape, dtype, name)
        return Buf(t[:], name or "")

    def psbuf(self, shape, dtype=F32, name=None):
        t = self.ps(shape, dtype, name)
        b = Buf(t[:], name or "")
        b.psum = True
        return b

    def _deps(self, reads, writes):
        deps = []
        for b in reads:
            if b.w is not None:
                deps.append(b.w)
        for b in writes:
            if b.w is not None:
                deps.append(b.w)
            deps.extend(b.r)
        return deps

    def _waits(self, eng, deps, own_key_prefix):
        waits = []
        wd = self.waited[eng]
        for (key, val) in deps:
            if key[0] == own_key_prefix and key[0] != "dma":
                if eng == "pe" or not self.self_sync:
                    continue
            if wd.get(key, 0) >= val:
                continue
            wd[key] = val
            waits.append((key, val))
        return waits

    def op(self, eng, fn, reads=(), writes=()):
        pr = [b for b in reads if b.psum]
        if pr:
            reads = [b for b in reads if not b.psum]
            writes = list(writes) + pr
        deps = self._deps(reads, writes)
        waits = self._waits(eng, deps, eng)
        self.count[eng] += 1
        c = self.count[eng]
        key = (eng, (c - 1) // EPOCH)
        val = (c - 1) % EPOCH + 1
        ev = (key, val)
        self.items[eng].append((waits, fn, ev, 1))
        for b in reads:
            b.r.append(ev)
        for b in writes:
            b.w = ev
            b.r = []
        return ev

    def dma(self, out_ap, in_ap, reads=(), writes=(), q="sp", **kw):
        deps = self._deps(reads, writes)
        j = self.dma_n[q]
        self.dma_n[q] += 1
        slot = j % RING
        key = ("dma", q, slot)
        if j >= RING:
            deps.append((key, 16 * (j // RING)))
        waits = self._waits(q, deps, "dma")
        val = 16 * (j // RING + 1)
        ev = (key, val)
        self.dma_last[key] = val

        def fn(e, out_ap=out_ap, in_ap=in_ap, kw=kw):
            o = out_ap(e) if callable(out_ap) else out_ap
            i = in_ap(e) if callable(in_ap) else in_ap
            try:
                return e.dma_start(out=o, in_=i, **kw)
            except Exception:
                print('DMA FAIL out', o, 'in', i, flush=True)
                raise
        self.items[q].append((waits, fn, ev, 16))
        for b in reads:
            b.r.append(ev)
        for b in writes:
            b.w = ev
            b.r = []
        return ev

    def _last_events(self):
        finals = []
        for key, val in self.dma_last.items():
            finals.append((key, val))
        for e in ("pe", "act", "dve", "pool"):
            c = self.count[e]
            if c > 0:
                finals.append(((e, (c - 1) // EPOCH), (c - 1) % EPOCH + 1))
        return finals

    def begin_phase(self):
        finals = self._last_events()
        for e in ENGS:
            waits = self._waits(e, finals, "__none__")
            if waits:
                self.items[e].append((waits, None, None, 0))

    def end_phase(self, final=False):
        nc = self.nc
        for e in ENGS:
            for (waits, fn, ev, inc) in self.items[e]:
                if ev is not None:
                    self.sem(ev[0])
                for (k, v) in waits:
                    self.sem(k)
        final_waits = self._last_events() if final else []
        for (k, v) in final_waits:
            self.sem(k)
        items = self.items
        semh = self.semh

        def run(e, lst, fin=False):
            for (waits, fn, ev, inc) in lst:
                for (k, v) in waits:
                    e.wait_ge(semh[k], v)
                if fn is None:
                    continue
                ins = fn(e)
                if ins is not None and ev is not None:
                    ins.then_inc(semh[ev[0]], inc)
            if fin:
                for (k, v) in final_waits:
                    e.wait_ge(semh[k], v)

        with nc.Block() as block:
            @block.sync
            def _(e):
                run(e, items["sp"], fin=final)

            @block.tensor
            def _(e):
                run(e, items["pe"])

            @block.scalar
            def _(e):
                run(e, items["act"])

            @block.vector
            def _(e):
                run(e, items["dve"])

            @block.gpsimd
            def _(e):
                run(e, items["pool"])
        self.items = {e: [] for e in ENGS}
        self.es.close()
        self.es = ExitStack()

    def emit(self):
        self.end_phase(final=True)

    def close(self):
        self.es.close()
        self.es_sem.close()

    def mm(self, out, lhsT, rhs, start=True, stop=True, reads=(), writes=()):
        return self.op("pe", lambda e: e.matmul(out, lhsT, rhs, start=start, stop=stop),
                       reads, writes)


class View:
    __slots__ = ("buf", "ap")

    def __init__(self, buf, ap):
        self.buf = buf
        self.ap = ap

    def __getitem__(self, k):
        return View(self.buf, self.ap[k])


def V(buf, *k):
    if not k:
        return View(buf, buf.ap)
    return View(buf, buf.ap[k if len(k) > 1 else k[0]])


def _ap(x):
    return x.ap if isinstance(x, View) else x


def _bufs(*xs):
    return [x.buf for x in xs if isinstance(x, View)]


class Ops:
    def __init__(self, P):
        self.P = P

    def mm(self, out, lhsT, rhs, start=True, stop=True):
        return self.P.op("pe", lambda e: e.matmul(out.ap, lhsT.ap, rhs.ap, start=start, stop=stop),
                         _bufs(lhsT, rhs), _bufs(out))

    def tr(self, out, in_, ident):
        return self.P.op("pe", lambda e: e.transpose(out.ap, in_.ap, ident.ap), _bufs(in_, ident), _bufs(out))

    def act(self, out, in_, func, bias=None, scale=None, accum=None, eng="act"):
        kw = {}
        if bias is not None:
            kw["bias"] = _ap(bias)
        if scale is not None:
            kw["scale"] = _ap(scale)
        if accum is not None:
            kw["accum_out"] = _ap(accum)
        return self.P.op("act", lambda e: e.activation(out.ap, in_.ap, func, **kw),
                         _bufs(in_, bias, scale), _bufs(out, accum))

    def tt(self, eng, out, in0, in1, op):
        return self.P.op(eng, lambda e: e.tensor_tensor(out.ap, in0.ap, in1.ap, op=op), _bufs(in0, in1), _bufs(out))

    def ts(self, eng, out, in0, s1, op0, s2=None, op1=None):
        if op1 is None:
            return self.P.op(eng, lambda e: e.tensor_scalar(out.ap, in0.ap, _ap(s1), None, op0=op0),
                             _bufs(in0, s1), _bufs(out))
        return self.P.op(eng, lambda e: e.tensor_scalar(out.ap, in0.ap, _ap(s1), _ap(s2), op0=op0, op1=op1),
                         _bufs(in0, s1, s2), _bufs(out))

    def stt(self, eng, out, in0, scalar, in1, op0, op1):
        return self.P.op(eng, lambda e: e.scalar_tensor_tensor(out.ap, in0.ap, _ap(scalar), in1.ap, op0=op0, op1=op1),
                         _bufs(in0, scalar, in1), _bufs(out))

    def copy(self, eng, out, in_):
        if eng == "act":
            return self.act(out, in_, AF.Identity)
        return self.P.op(eng, lambda e: e.tensor_copy(out.ap, in_.ap), _bufs(in_), _bufs(out))

    def recip(self, out, in_):
        return self.P.op("dve", lambda e: e.reciprocal(out.ap, in_.ap), _bufs(in_), _bufs(out))

    def scan(self, eng, out, d0, d1, init, op0=None, op1=None):
        op0 = op0 or ALU.mult
        op1 = op1 or ALU.add
        return self.P.op(eng, lambda e: e.tensor_tensor_scan(out.ap, d0.ap, d1.ap, _ap(init), op0=op0, op1=op1),
                         _bufs(d0, d1, init), _bufs(out))

    def memset(self, eng, out, val):
        return self.P.op(eng, lambda e: e.memset(out.ap, val), [], _bufs(out))

    def aselect(self, out, in_, pattern, cmp, fill, base=0, cm=1):
        return self.P.op("pool", lambda e: e.affine_select(out.ap, in_.ap, pattern=pattern, compare_op=cmp, fill=fill,
                                                          base=base, channel_multiplier=cm),
                         _bufs(in_), _bufs(out))

    def dma(self, out, in_, q="sp"):
        return self.P.dma(_ap(out), _ap(in_), reads=_bufs(in_), writes=_bufs(out), q=q)


def _prog_coll(self, kind, in_ap, out_ap, groups, reads=(), writes=()):
    q = "pool"
    deps = self._deps(reads, writes)
    self.ncoll = getattr(self, "ncoll", 0) + 1
    key = ("cc", self.ncoll)
    waits = self._waits(q, deps, "dma")
    ev = (key, 1)
    self.dma_last[key] = 1

    def fn(e):
        return e.collective_compute(kind, ALU.bypass, groups, [in_ap.opt()], [out_ap.opt()])
    self.items[q].append((waits, fn, ev, 1))
    for b in reads:
        b.r.append(ev)
    for b in writes:
        b.w = ev
        b.r = []
    return ev


Prog.coll = _prog_coll

import math

D = 1024
DFF = 2816
NF = 22
EPS = 1e-6
NEG = -1.0e30
TB = 512
NB = 4
NWF = 656
NWT = 386
SEQ = 16384
NCORES = 8
GROUPS = [[0, 1, 2, 3], [4, 5, 6, 7]]


class PsPool:
    def __init__(self, P, n=8, name="psp"):
        self.bufs = [P.psbuf([128, 512], F32, f"{name}{i}") for i in range(n)]
        self.i = 0

    def get(self):
        b = self.bufs[self.i % len(self.bufs)]
        self.i += 1
        return b

def emit_mixer(nc, P, O, L, sfx, xsrc, yout, after_sb):
    def din(name, shape):
        return nc.dram_tensor(name + sfx, list(shape), F32, kind="ExternalInput").ap()
    anorm = din("anorm", [128, 8])
    wf_d = din("wf", [D, NWF])
    wt_d = din("wt", [D, NWT])
    dn_sc_d = din("dn_sc", [128, 2])
    dn_cw_d = din("dn_cw", [128, 12])
    dn_ng_d = din("dn_ng", [128, 128])
    gla_ng_d = din("gla_ng", [128, 128])
    gla_w2_d = din("gla_w2", [16, 64])
    gla_b2_d = din("gla_b2", [64, 1])
    s5_lam_d = din("s5_lam", [128, 12])
    s5_b_d = din("s5_b", [2, 4, 128, 16])
    s5_c_d = din("s5_c", [2, 4, 128, 16])
    s5_d_d = din("s5_d", [128, 1])
    psp = PsPool(P, n=7)
    yps_ded = P.psbuf([128, 512], F32, "yps_ded")

    def S(shape, dt=F32, name=None):
        return P.sbuf(shape, dt, name)

    ones = S([128, 512], F32, "ones")
    O.memset("pool", V(ones), 1.0)
    ident = S([128, 128], F32, "ident")
    O.memset("pool", V(ident), 1.0)
    O.aselect(V(ident), V(ident), [[-1, 128]], ALU.is_equal, 0.0, cm=1)
    U = S([128, 128], F32, "U")
    O.memset("pool", V(U), 1.0)
    O.aselect(V(U), V(U), [[1, 128]], ALU.is_ge, 0.0, cm=-1)
    mneg = S([128, 128], F32, "mneg")
    O.memset("pool", V(mneg), 0.0)
    O.aselect(V(mneg), V(mneg), [[-1, 128]], ALU.is_ge, NEG, cm=1)
    mnegT = S([128, 128], F32, "mnegT")
    O.memset("pool", V(mnegT), 0.0)
    O.aselect(V(mnegT), V(mnegT), [[1, 128]], ALU.is_ge, NEG, cm=-1)
    nLs = S([128, 128], F32, "nLs")
    O.memset("pool", V(nLs), -1.0)
    O.aselect(V(nLs), V(nLs), [[-1, 128]], ALU.is_gt, 0.0, cm=1)

    c_anorm = S([128, 8], F32, "c_anorm")
    O.dma(V(c_anorm), anorm)
    wf = S([128, 8, NWF], BF16, "wf_s")
    wt = S([128, 8, NWT], BF16, "wt_s")
    O.dma(V(wf), wf_d.rearrange("(k p) n -> p k n", p=128), q="pool")
    O.dma(V(wt), wt_d.rearrange("(k p) n -> p k n", p=128), q="pool")
    dn_sc = S([128, 2], F32, "dn_sc_s"); O.dma(V(dn_sc), dn_sc_d)
    dn_cw = S([128, 12], F32, "dn_cw_s"); O.dma(V(dn_cw), dn_cw_d)
    dn_ng = S([128, 128], F32, "dn_ng_s"); O.dma(V(dn_ng), dn_ng_d)
    gla_ng = S([128, 128], F32, "gla_ng_s"); O.dma(V(gla_ng), gla_ng_d)
    gla_w2 = S([16, 64], F32, "gla_w2_s"); O.dma(V(gla_w2), gla_w2_d)
    gla_b2 = S([64, 1], F32, "gla_b2_s"); O.dma(V(gla_b2), gla_b2_d)
    nb2 = S([64, 1], F32, "nb2")
    O.ts("dve", V(nb2), V(gla_b2), -1.0, ALU.mult)
    negA = S([128, 1], F32, "negA")
    O.act(V(negA), V(dn_sc, slice(None), slice(0, 1)), AF.Exp)
    O.ts("dve", V(negA), V(negA), -1.0, ALU.mult)
    s5d = S([128, 1], F32, "s5d"); O.dma(V(s5d), s5_d_d)

    lam = S([128, 12], F32, "lam"); O.dma(V(lam), s5_lam_d)
    lr_, li_, ls_ = (V(lam, slice(None), slice(0, 4)), V(lam, slice(None), slice(4, 8)),
                     V(lam, slice(None), slice(8, 12)))
    pp = S([128, 64], F32, "s5pp")

    def col(i):
        return V(pp, slice(None), slice(4 * i, 4 * i + 4))
    step, lrs, th, mag, c8, s8, t0, t1, cr, ci, den, nr, fr, fi, t2, t3 = [col(i) for i in range(16)]
    O.act(step, ls_, AF.Exp)
    O.tt("dve", lrs, lr_, step, ALU.mult)
    O.tt("dve", th, li_, step, ALU.mult)
    O.act(mag, lrs, AF.Exp)
    halfpi = S([128, 1], F32, "halfpi"); O.memset("pool", V(halfpi), math.pi / 2)
    O.act(s8, th, AF.Sin, scale=0.125)
    O.act(c8, th, AF.Sin, scale=-0.125, bias=V(halfpi))
    for _ in range(3):
        O.tt("dve", t0, c8, c8, ALU.mult)
        O.tt("dve", t1, s8, s8, ALU.mult)
        O.tt("dve", t2, c8, s8, ALU.mult)
        O.tt("dve", c8, t0, t1, ALU.subtract)
        O.ts("dve", s8, t2, 2.0, ALU.mult)
    O.tt("dve", cr, mag, c8, ALU.mult)
    O.tt("dve", ci, mag, s8, ALU.mult)
    O.tt("dve", t0, lr_, lr_, ALU.mult)
    O.tt("dve", t1, li_, li_, ALU.mult)
    O.tt("dve", den, t0, t1, ALU.add)
    O.recip(den, den)
    O.ts("dve", nr, cr, -1.0, ALU.add)
    O.tt("dve", t0, nr, lr_, ALU.mult)
    O.tt("dve", t1, ci, li_, ALU.mult)
    O.tt("dve", t0, t0, t1, ALU.add)
    O.tt("dve", fr, t0, den, ALU.mult)
    O.tt("dve", t0, ci, lr_, ALU.mult)
    O.tt("dve", t1, nr, li_, ALU.mult)
    O.tt("dve", t0, t0, t1, ALU.subtract)
    O.tt("dve", fi, t0, den, ALU.mult)

    Ct = [S([128, TB], F32, f"Ct{j}") for j in range(4)]
    St = [S([128, TB], F32, f"St{j}") for j in range(4)]
    Mg = [S([128, TB], F32, f"Mg{j}") for j in range(4)]
    rq = S([128, 16], F32, "rq")
    r512 = S([128, 8], F32, "r512")
    tmpT = S([128, TB // 2], F32, "tmpT")
    for j in range(4):
        cj, sj, ta, tb_ = [V(rq, slice(None), slice(4 * j + i, 4 * j + i + 1)) for i in range(4)]
        O.copy("dve", cj, V(pp, slice(None), slice(4 * 4 + j, 4 * 4 + j + 1)))
        O.copy("dve", sj, V(pp, slice(None), slice(4 * 5 + j, 4 * 5 + j + 1)))
        O.memset("pool", V(Ct[j], slice(None), slice(0, 1)), 1.0)
        O.memset("pool", V(St[j], slice(None), slice(0, 1)), 0.0)
        n = 1
        while n < TB:
            lo_c, lo_s = V(Ct[j], slice(None), slice(0, n)), V(St[j], slice(None), slice(0, n))
            hi_c, hi_s = V(Ct[j], slice(None), slice(n, 2 * n)), V(St[j], slice(None), slice(n, 2 * n))
            tm = V(tmpT, slice(None), slice(0, n))
            O.ts("dve", tm, lo_s, sj, ALU.mult)
            O.stt("dve", hi_c, lo_c, cj, tm, ALU.mult, ALU.subtract)
            O.ts("dve", tm, lo_c, sj, ALU.mult)
            O.stt("dve", hi_s, lo_s, cj, tm, ALU.mult, ALU.add)
            O.tt("dve", ta, cj, cj, ALU.mult)
            O.tt("dve", tb_, sj, sj, ALU.mult)
            O.tt("dve", sj, cj, sj, ALU.mult)
            O.ts("dve", sj, sj, 2.0, ALU.mult)
            O.tt("dve", cj, ta, tb_, ALU.subtract)
            n *= 2
        O.copy("dve", V(r512, slice(None), slice(2 * j, 2 * j + 1)), cj)
        O.copy("dve", V(r512, slice(None), slice(2 * j + 1, 2 * j + 2)), sj)
        O.ts("dve", V(Mg[j]), V(ones), V(pp, slice(None), slice(4 * 3 + j, 4 * 3 + j + 1)), ALU.mult)

    BreT = [S([128, 128], F32, f"BreT{j}") for j in range(4)]
    BimT = [S([128, 128], F32, f"BimT{j}") for j in range(4)]
    Cre = [S([128, 128], F32, f"Cre{j}") for j in range(4)]
    Cim = [S([128, 128], F32, f"Cim{j}") for j in range(4)]
    bst = S([128, 2, 16], F32, "bst")
    padr = S([128, 128], F32, "padr")
    padi = S([128, 128], F32, "padi")
    for j in range(4):
        O.dma(V(bst, slice(None), 0), s5_b_d[0, j])
        O.dma(V(bst, slice(None), 1), s5_b_d[1, j])
        O.memset("pool", V(padr), 0.0)
        O.memset("pool", V(padi), 0.0)
        O.memset("pool", V(Cre[j]), 0.0)
        O.memset("pool", V(Cim[j]), 0.0)
        frj = V(pp, slice(None), slice(4 * 12 + j, 4 * 12 + j + 1))
        fij = V(pp, slice(None), slice(4 * 13 + j, 4 * 13 + j + 1))
        for hf in range(2):
            ps_ = slice(64 * hf, 64 * hf + 64)
            cs_ = slice((2 * j + hf) * 16, (2 * j + hf) * 16 + 16)
            bre, bim = V(bst, ps_, 0), V(bst, ps_, 1)
            tm = V(tmpT, ps_, slice(0, 16))
            O.ts("dve", tm, bim, fij[ps_], ALU.mult)
            O.stt("dve", V(padr, ps_, cs_), bre, frj[ps_], tm, ALU.mult, ALU.subtract)
            O.ts("dve", tm, bre, fij[ps_], ALU.mult)
            O.stt("dve", V(padi, ps_, cs_), bim, frj[ps_], tm, ALU.mult, ALU.add)
            O.dma(V(Cre[j], ps_, cs_), s5_c_d[0, j, 64 * hf:64 * hf + 64, :])
            O.dma(V(Cim[j], ps_, cs_), s5_c_d[1, j, 64 * hf:64 * hf + 64, :])
        for (pad, dst) in ((padr, BreT[j]), (padi, BimT[j])):
            pt = psp.get()
            O.tr(V(pt, slice(None), slice(0, 128)), V(pad), V(ident))
            O.copy("act", V(dst), V(pt, slice(None), slice(0, 128)))
        O.ts("dve", V(Cim[j]), V(Cim[j]), -1.0, ALU.mult)

    Sdn = [S([128, 128], F32, f"Sdn{i}") for i in range(2)]
    Sgl = [S([64, 128], F32, f"Sgl{i}") for i in range(2)]
    O.memset("pool", V(Sdn[0]), 0.0)
    O.memset("pool", V(Sgl[0]), 0.0)
    s5c = [S([128, 2], F32, f"s5c{j}") for j in range(4)]
    for j in range(4):
        O.memset("pool", V(s5c[j]), 0.0)
    s5i = [S([128, 4], F32, f"s5i{j}") for j in range(4)]
    chist = [S([128, 3], F32, f"chist{c}") for c in range(3)]
    for c in range(3):
        O.memset("pool", V(chist[c]), 0.0)

    x_t = P.sb([128, 8, TB], F32, "x_t")
    xb = [Buf(x_t[:, k, :], f"x{k}") for k in range(8)]
    h_t = P.sb([128, 8, TB], BF16, "h_t")
    hb = [Buf(h_t[:, k, :], f"h{k}") for k in range(8)]
    sq = [S([128, TB], F32, f"sq{i}") for i in range(2)]
    rstd = S([128, TB], F32, "rstd")
    cbuf = [S([128, TB + 3], F32, f"cbuf{c}") for c in range(3)]
    cacc = [S([128, TB], F32, f"cacc{c}") for c in range(3)]
    qn = S([128, TB], F32, "qn")
    kn = S([128, TB], F32, "kn")
    tok = [S([128, NWT], F32, f"tok{b}") for b in range(NB)]
    uT = S([128, TB], F32, "uT")
    lrs_b = S([16, TB], F32, "lrs_b")
    gls = S([64, TB], F32, "gls")
    gc = S([64, TB], F32, "gc")
    gEQ = S([64, TB], F32, "gEQ")
    gEK = S([64, TB], F32, "gEK")
    gqe = S([64, TB], F32, "gqe")
    gke = S([64, TB], F32, "gke")
    gkr = S([64, TB], F32, "gkr")
    gqr = S([64, TB], F32, "gqr")
    s5w = [S([128, TB], F32, f"s5w{i}") for i in range(6)]
    yc_sb = S([128, TB], F32, "yc_sb")
    ys_t = P.sb([128, 3, TB], BF16, "ys_t")
    ys = [Buf(ys_t[:, i3, :], f"ys{i3}") for i3 in range(3)]

    def blkbufs(name, shape, n=NB):
        return [S(shape, F32, f"{name}{b}") for b in range(n)]
    sc = [[S([128, 1], F32, f"sc{b}_{i}") for i in range(12)] for b in range(NB)]
    Ug = blkbufs("Ug", [128, 128])
    Eb = blkbufs("Eb", [128, 128])
    ETb = blkbufs("ETb", [128, 128])
    qd = blkbufs("qd", [128, 128])
    bk = blkbufs("bk", [128, 128])
    kdec = blkbufs("kdec", [128, 128])
    bv = blkbufs("bv", [128, 128])
    M = blkbufs("M", [128, 256])
    X = blkbufs("X", [128, 128])
    attnT = blkbufs("attnT", [128, 128])
    un = blkbufs("un", [128, 128])
    wT = blkbufs("wT", [128, 128])
    ub = blkbufs("ub", [128, 128])
    sg = blkbufs("sg", [128, 128])
    yo = blkbufs("yo", [128, 128])
    junk = blkbufs("junk", [128, 128], 2)
    gjunk = blkbufs("gjunk", [128, 128], 2)
    gsc = [[S([64, 1], F32, f"gsc{b}_{i}") for i in range(2)] for b in range(NB)]
    gkdT = blkbufs("gkdT", [64, 128])
    gkd = blkbufs("gkd", [128, 64])
    gaT = blkbufs("gaT", [128, 128])
    gsg = blkbufs("gsg", [128, 128])
    gyo = blkbufs("gyo", [128, 128])
    gss = [[S([128, 1], F32, f"gss{b}_{i}") for i in range(2)] for b in range(NB)]
    ones64 = S([64, 128], F32, "ones64")
    O.memset("pool", V(ones64), 1.0)

    A = slice(None)

    def bsl(b):
        return slice(128 * b, 128 * b + 128)

    def out_norm(o_ps, ss_b, ng, sgate, ydst, jk):
        ssum, rs = ss_b
        O.memset("pool", V(ssum), 0.0)
        O.act(V(jk), o_ps, AF.Square, accum=V(ssum))
        O.act(V(rs), V(ssum), AF.Sqrt, bias=EPS, scale=1.0 / 128)
        O.recip(V(rs), V(rs))
        O.stt("dve", V(ydst), o_ps, V(rs), V(ng), ALU.mult, ALU.mult)
        O.tt("pool", V(ydst), V(ydst), V(sgate), ALU.mult)

    nsb = L // TB
    sdn_i = 0
    sgl_i = 0
    def gen_front(t0f):
        for k in range(8):
            O.dma(V(xb[k]), xsrc(k, t0f))
        ps = psp.get()
        for k in range(8):
            s = sq[k % 2]
            O.act(V(s), V(xb[k]), AF.Square)
            O.mm(V(ps), V(ones, A, slice(0, 128)), V(s), start=(k == 0), stop=(k == 7))
        O.act(V(rstd), V(ps), AF.Sqrt, bias=EPS, scale=1.0 / D)
        O.recip(V(rstd), V(rstd))
        for k in range(8):
            O.stt("dve", V(hb[k]), V(xb[k]), V(c_anorm, A, slice(k, k + 1)), V(rstd), ALU.mult, ALU.mult)
        yield
        for c in range(3):
            ps = psp.get()
            for k in range(8):
                O.mm(V(ps), V(wf, A, k, slice(c * 128, (c + 1) * 128)), V(hb[k]), start=(k == 0), stop=(k == 7))
            O.copy("pool", V(cbuf[c], A, slice(0, 3)), V(chist[c]))
            O.copy("act", V(cbuf[c], A, slice(3, TB + 3)), V(ps))
            O.copy("pool", V(chist[c]), V(cbuf[c], A, slice(TB, TB + 3)))
            yield
        ps = psp.get()
        for k in range(8):
            O.mm(V(ps), V(wf, A, k, slice(384, 512)), V(hb[k]), start=(k == 0), stop=(k == 7))
        O.copy("act", V(uT), V(ps))
        yield
        ps_gq = psp.get()
        for k in range(8):
            O.mm(V(ps_gq, slice(0, 64)), V(wf, A, k, slice(512, 576)), V(hb[k]), start=(k == 0), stop=(k == 7))
        O.copy("act", V(gqr), V(ps_gq, slice(0, 64)))
        ps_gk = psp.get()
        for k in range(8):
            O.mm(V(ps_gk, slice(0, 64)), V(wf, A, k, slice(576, 640)), V(hb[k]), start=(k == 0), stop=(k == 7))
        O.copy("act", V(gkr), V(ps_gk, slice(0, 64)))
        yield
        ps = psp.get()
        for k in range(8):
            O.mm(V(ps, slice(0, 16)), V(wf, A, k, slice(640, 656)), V(hb[k]), start=(k == 0), stop=(k == 7))
        O.copy("act", V(lrs_b), V(ps, slice(0, 16)))
        yield
        for b in range(NB):
            ps = psp.get()
            for k in range(8):
                O.mm(V(ps, A, slice(0, NWT)), V(hb[k], A, bsl(b)), V(wt, A, k), start=(k == 0), stop=(k == 7))
            O.copy("act", V(tok[b]), V(ps, A, slice(0, NWT)))
            yield
        for c in range(3):
            O.ts("dve", V(cacc[c]), V(cbuf[c], A, slice(0, TB)), V(dn_cw, A, slice(4 * c, 4 * c + 1)), ALU.mult)
            for t in range(1, 4):
                O.stt("dve", V(cacc[c]), V(cbuf[c], A, slice(t, t + TB)),
                      V(dn_cw, A, slice(4 * c + t, 4 * c + t + 1)), V(cacc[c]), ALU.mult, ALU.add)
            O.act(V(cacc[c]), V(cacc[c]), AF.Silu)
            yield
        for c, dst, scl in ((0, qn, 128 ** -0.5), (1, kn, 1.0)):
            s = sq[c]
            O.act(V(s), V(cacc[c]), AF.Square)
            ps = psp.get()
            O.mm(V(ps), V(ones, A, slice(0, 128)), V(s))
            O.act(V(s), V(ps), AF.Sqrt, bias=EPS, scale=1.0)
            O.recip(V(s), V(s))
            O.stt("dve", V(dst), V(cacc[c]), scl, V(s), ALU.mult, ALU.mult)
            yield

    for _ in gen_front(0):
        pass
    for sb_i in range(nsb):
        t0_ = sb_i * TB
        dn_flag = [False]
        vs = cacc[2]

        def gen_s5():
            yps = yps_ded
            for j in range(4):
                pr = psp.get()
                pi = psp.get()
                O.mm(V(pr), V(BreT[j]), V(uT))
                O.mm(V(pi), V(BimT[j]), V(uT))
                w0, w1, w2, w3, w4, w5 = [V(s5w[i]) for i in range(6)]
                O.tt("dve", w0, V(pr), V(Ct[j]), ALU.mult)
                O.tt("dve", w1, V(pi), V(St[j]), ALU.mult)
                O.tt("pool", w0, w0, w1, ALU.add)
                O.tt("dve", w2, V(pi), V(Ct[j]), ALU.mult)
                O.tt("dve", w3, V(pr), V(St[j]), ALU.mult)
                O.tt("pool", w2, w2, w3, ALU.subtract)
                cr_, ci_ = V(s5c[j], A, slice(0, 1)), V(s5c[j], A, slice(1, 2))
                c5, s5_ = V(r512, A, slice(2 * j, 2 * j + 1)), V(r512, A, slice(2 * j + 1, 2 * j + 2))
                ir, ii, ta, tb_ = [V(s5i[j], A, slice(i, i + 1)) for i in range(4)]
                O.tt("dve", ta, ci_, s5_, ALU.mult)
                O.stt("dve", ir, cr_, c5, ta, ALU.mult, ALU.subtract)
                O.tt("dve", tb_, cr_, s5_, ALU.mult)
                O.stt("dve", ii, ci_, c5, tb_, ALU.mult, ALU.add)
                O.scan("dve", w4, V(Mg[j]), w0, ir)
                O.scan("dve", w5, V(Mg[j]), w2, ii)
                O.copy("pool", cr_, V(s5w[4], A, slice(TB - 1, TB)))
                O.copy("pool", ci_, V(s5w[5], A, slice(TB - 1, TB)))
                O.tt("dve", w0, w4, V(Ct[j]), ALU.mult)
                O.tt("pool", w1, w5, V(St[j]), ALU.mult)
                O.tt("dve", w0, w0, w1, ALU.subtract)
                O.tt("pool", w2, w5, V(Ct[j]), ALU.mult)
                O.tt("dve", w3, w4, V(St[j]), ALU.mult)
                O.tt("pool", w2, w2, w3, ALU.add)
                yield
                O.mm(V(yps), V(Cre[j]), w0, start=(j == 0), stop=False)
                O.mm(V(yps), V(Cim[j]), w2, start=False, stop=(j == 3))
                yield
            O.stt("dve", V(yc_sb), V(uT), V(s5d), V(yps), ALU.mult, ALU.add)
            O.act(V(ys[2]), V(yc_sb), AF.Gelu_apprx_tanh)

        def gen_gla():
            nonlocal sgl_i
            ps = psp.get()
            O.mm(V(ps, slice(0, 64)), V(gla_w2), V(lrs_b))
            O.act(V(gls), V(ps, slice(0, 64)), AF.Exp, scale=-1.0, bias=V(nb2))
            O.act(V(gls), V(gls), AF.Ln, bias=1.0)
            for b in range(NB):
                O.scan("dve", V(gc, A, bsl(b)), V(ones64), V(gls, A, bsl(b)), 0.0)
            O.act(V(gEQ), V(gc), AF.Exp, scale=-1.0 / 16, bias=math.log(1.0 / 8))
            O.act(V(gEK), V(gc), AF.Exp, scale=1.0 / 16)
            O.tt("dve", V(gqe), V(gqr), V(gEQ), ALU.mult)
            O.tt("dve", V(gke), V(gkr), V(gEK), ALU.mult)
            yield
            for b in range(NB):
                nbl, gend = V(gsc[b][0]), V(gsc[b][1])
                O.ts("dve", nbl, V(gc, A, slice(128 * b + 127, 128 * b + 128)), -1.0 / 16, ALU.mult)
                O.act(gend, nbl, AF.Exp)
                O.act(V(gkdT[b]), V(gc, A, bsl(b)), AF.Exp, scale=1.0 / 16, bias=nbl)
                O.tt("dve", V(gkdT[b]), V(gkdT[b]), V(gkr, A, bsl(b)), ALU.mult)
            yield
            for b in range(NB):
                pt = psp.get()
                O.tr(V(pt, A, slice(0, 64)), V(gkdT[b]), V(ident, slice(0, 64), slice(0, 64)))
                O.copy("act", V(gkd[b]), V(pt, A, slice(0, 64)))
                pa = psp.get()
                O.mm(V(pa, A, slice(0, 128)), V(gke, A, bsl(b)), V(gqe, A, bsl(b)))
                O.tt("dve", V(gaT[b]), V(pa, A, slice(0, 128)), V(U), ALU.mult)
                O.act(V(gsg[b]), V(tok[b], A, slice(258, 386)), AF.Silu)
                yield
            for b in range(NB):
                gv = V(tok[b], A, slice(130, 258))
                So, Sn = Sgl[sgl_i % 2], Sgl[(sgl_i + 1) % 2]
                sgl_i += 1
                po = psp.get()
                O.mm(V(po, A, slice(0, 128)), V(gqe, A, bsl(b)), V(So), start=True, stop=False)
                O.mm(V(po, A, slice(0, 128)), V(gaT[b]), gv, start=False, stop=True)
                pd = psp.get()
                O.mm(V(pd, slice(0, 64), slice(0, 128)), V(gkd[b]), gv)
                O.stt("dve", V(Sn), V(So), V(gsc[b][1]), V(pd, slice(0, 64), slice(0, 128)), ALU.mult, ALU.add)
                out_norm(V(po, A, slice(0, 128)), gss[b], gla_ng, gsg[b], gyo[b], gjunk[b % 2])
                yield
                ptr = psp.get()
                O.tr(V(ptr, A, slice(0, 128)), V(gyo[b]), V(ident))
                O.copy("act", V(ys[1], A, bsl(b)), V(ptr, A, slice(0, 128)))
                yield

        def gen_dn():
            nonlocal sdn_i
            pAs = []
            for b in range(NB):
                beta, glog, gam, glast, ngam, egam, bg, edl, gend, tmp = [V(sc[b][i]) for i in range(10)]
                O.act(beta, V(tok[b], A, slice(128, 129)), AF.Sigmoid)
                O.act(tmp, V(tok[b], A, slice(129, 130)), AF.Exp, bias=V(dn_sc, A, slice(1, 2)))
                O.act(tmp, tmp, AF.Ln, bias=1.0)
                O.tt("dve", glog, tmp, V(negA), ALU.mult)
                O.ts("dve", V(Ug[b]), V(U), glog, ALU.mult)
                yield
                pA = psp.get()
                pAs.append(pA)
                O.mm(V(pA, A, slice(0, 128)), V(ones, A, slice(0, 128)), V(Ug[b]))
                O.mm(V(pA, A, slice(128, 129)), V(U), glog)
                O.mm(V(pA, A, slice(129, 130)), V(ones, A, slice(0, 128)), glog)
                O.copy("dve", gam, V(pA, A, slice(128, 129)))
                O.copy("dve", glast, V(pA, A, slice(129, 130)))
                O.ts("dve", ngam, gam, -1.0, ALU.mult)
                O.act(egam, gam, AF.Exp)
                O.tt("dve", bg, beta, egam, ALU.mult)
                O.act(edl, gam, AF.Exp, scale=-1.0, bias=glast)
                O.act(gend, glast, AF.Exp)
                gbc = V(pA, A, slice(0, 128))
                O.stt("dve", V(Eb[b]), gbc, -1.0, V(mneg), ALU.mult, ALU.add)
                O.act(V(Eb[b]), V(Eb[b]), AF.Exp, bias=gam)
                O.tt("dve", V(ETb[b]), gbc, V(mnegT), ALU.add)
                O.act(V(qd[b]), gbc, AF.Exp)
                yield
                O.act(V(ETb[b]), V(ETb[b]), AF.Exp, bias=ngam)
                O.tt("dve", V(qd[b]), V(qd[b]), V(qn, A, bsl(b)), ALU.mult)
                O.act(V(sg[b]), V(tok[b], A, slice(0, 128)), AF.Silu)
                yield
            for b in range(NB):
                beta, bg, edl = V(sc[b][0]), V(sc[b][6]), V(sc[b][7])
                pt = psp.get()
                O.tr(V(pt, A, slice(0, 128)), V(kn, A, bsl(b)), V(ident))
                O.tr(V(pt, A, slice(128, 256)), V(vs, A, bsl(b)), V(ident))
                O.ts("dve", V(bk[b]), V(pt, A, slice(0, 128)), bg, ALU.mult)
                O.act(V(kdec[b]), V(pt, A, slice(0, 128)), AF.Identity, scale=edl)
                O.ts("dve", V(bv[b]), V(pt, A, slice(128, 256)), beta, ALU.mult)
            yield
            for b in range(NB):
                beta = V(sc[b][0])
                pk = psp.get()
                O.mm(V(pk, A, slice(0, 128)), V(kn, A, bsl(b)), V(kn, A, bsl(b)))
                O.mm(V(pk, A, slice(128, 256)), V(kn, A, bsl(b)), V(qn, A, bsl(b)))
                O.stt("dve", V(M[b], A, slice(0, 128)), V(pk, A, slice(0, 128)), beta, V(Eb[b]), ALU.mult, ALU.mult)
                O.tt("pool", V(M[b], A, slice(0, 128)), V(M[b], A, slice(0, 128)), V(nLs), ALU.mult)
                O.tt("dve", V(attnT[b]), V(pk, A, slice(128, 256)), V(ETb[b]), ALU.mult)
            yield
            for b in range(NB):
                pt = psp.get()
                O.tr(V(pt, A, slice(0, 128)), V(M[b], A, slice(0, 128)), V(ident))
                O.copy("act", V(M[b], A, slice(128, 256)), V(pt, A, slice(0, 128)))
                O.tt("dve", V(X[b]), V(pt, A, slice(0, 128)), V(ident), ALU.add)
            dn_flag[0] = True
            yield
            for lev in range(1, 8):
                pls = []
                for b in range(NB):
                    pl = psp.get()
                    pls.append(pl)
                    Mv, MTv = V(M[b], A, slice(0, 128)), V(M[b], A, slice(128, 256))
                    if lev <= 6:
                        O.mm(V(pl, A, slice(0, 128)), MTv, Mv)
                        O.mm(V(pl, A, slice(128, 256)), Mv, MTv)
                    if lev >= 2:
                        O.mm(V(pl, A, slice(256, 384)), Mv, V(X[b]))
                for b in range(NB):
                    pl = pls[b]
                    if lev <= 6:
                        O.copy("act", V(M[b]), V(pl, A, slice(0, 256)))
                    if lev >= 2:
                        O.tt("dve", V(X[b]), V(X[b]), V(pl, A, slice(256, 384)), ALU.add)
                yield
            for b in range(NB):
                pe_ = psp.get()
                O.mm(V(pe_, A, slice(0, 128)), V(X[b]), V(bv[b]))
                O.mm(V(pe_, A, slice(128, 256)), V(bk[b]), V(X[b]))
                O.copy("act", V(un[b]), V(pe_, A, slice(0, 128)))
                O.copy("act", V(wT[b]), V(pe_, A, slice(128, 256)))
            yield
            for b in range(NB):
                So, Sn = Sdn[sdn_i % 2], Sdn[(sdn_i + 1) % 2]
                sdn_i += 1
                pw = psp.get()
                O.mm(V(pw, A, slice(0, 128)), V(wT[b]), V(So))
                O.tt("dve", V(ub[b]), V(un[b]), V(pw, A, slice(0, 128)), ALU.subtract)
                yield
                po = psp.get()
                O.mm(V(po, A, slice(0, 128)), V(qd[b]), V(So), start=True, stop=False)
                O.mm(V(po, A, slice(0, 128)), V(attnT[b]), V(ub[b]), start=False, stop=True)
                pd = psp.get()
                O.mm(V(pd, A, slice(0, 128)), V(kdec[b]), V(ub[b]))
                O.stt("dve", V(Sn), V(So), V(sc[b][8]), V(pd, A, slice(0, 128)), ALU.mult, ALU.add)
                out_norm(V(po, A, slice(0, 128)), (sc[b][10], sc[b][11]), dn_ng, sg[b], yo[b], junk[b % 2])
                yield
                ptr = psp.get()
                O.tr(V(ptr, A, slice(0, 128)), V(yo[b]), V(ident))
                O.copy("act", V(ys[0], A, bsl(b)), V(ptr, A, slice(0, 128)))
                yield
        g_dn, g_s5, g_gla = gen_dn(), gen_s5(), gen_gla()
        tasks = [g_dn, g_s5, g_gla]
        rnd = 0
        nf = gen_front((sb_i + 1) * TB) if sb_i + 1 < nsb else None
        started = False
        while tasks:
            if nf is not None and not started and dn_flag[0] and g_s5 not in tasks and g_gla not in tasks:
                tasks.append(nf)
                started = True
            rnd += 1
            for g in list(tasks):
                if g not in tasks:
                    continue
                if g is g_s5 and rnd % 2 == 0:
                    continue
                try:
                    next(g)
                except StopIteration:
                    while g in tasks:
                        tasks.remove(g)
        if nf is not None and not started:
            for _ in nf:
                pass
        for i3 in range(3):
            yout(i3, t0_, ys[i3])
        after_sb(t0_)


def emit_dense(nc, P, TOK, last, sfx, src, NT=512):
    def din(name, shape):
        return nc.dram_tensor(name + sfx, list(shape), F32, kind="ExternalInput").ap()
    HT = TOK + 2
    pT = din("pT", [256, TOK])
    w_gate = din("w_gate", [D, 3 * D])
    b_gate = din("b_gate", [128, 24])
    w_branch = din("w_branch", [1536, D])
    w_o = din("w_o", [D, D])
    w_glu = din("w_glu", [512, 512])
    b_glu = din("b_glu", [128, 4])
    norms = din("norms", [128, 32])
    w_up = din("w_up", [D, 2 * DFF])
    cw = din("cw", [128, NF * 3])
    cb = din("cb", [128, NF])
    w_down = din("w_down", [DFF, D])
    w_pg = din("w_pg", [D, D])
    w_pp = din("w_pp", [256, D])
    psp = PsPool(P)
    ones = P.sbuf([128, 128], F32, "ones")
    P.op("pool", lambda e: e.memset(ones.ap, 1.0), writes=[ones])
    c_bg = P.sbuf([128, 24], F32, "c_bg")
    c_bglu = P.sbuf([128, 4], F32, "c_bglu")
    c_norm = P.sbuf([128, 32], F32, "c_norm")
    c_cw = P.sbuf([128, NF * 3], F32, "c_cw")
    c_cb = P.sbuf([128, NF], F32, "c_cb")
    hmask = P.sbuf([128, 1], F32, "hmask")
    for t, srcd in ((c_bg, b_gate), (c_bglu, b_glu), (c_norm, norms), (c_cw, cw), (c_cb, cb), (hmask, src["hmask"])):
        P.dma(t.ap, srcd, writes=[t])

    NSLAB = 5
    slab_t = [P.sb([128, 8, 1024], BF16, f"slab{i}") for i in range(NSLAB)]
    slabs = [Buf(t[:], f"slab{i}") for i, t in enumerate(slab_t)]
    slab_i = [0]

    def load_w(src, r0, nk, c0, ncol):
        s = slabs[slab_i[0] % NSLAB]
        slab_i[0] += 1
        view = s.ap[:, 0:nk, 0:ncol]
        srcv = src[r0:r0 + nk * 128, c0:c0 + ncol].rearrange("(k p) n -> p k n", p=128)
        P.dma(view, srcv, writes=[s], q="pool")
        return s

    x_t = P.sb([128, 8, NT], F32, "x_t")
    xb = [Buf(x_t[:, k, :], f"x{k}") for k in range(8)]
    h_t = P.sb([128, 8, NT], BF16, "h_t")
    hb = [Buf(h_t[:, k, :], f"h{k}") for k in range(8)]
    y_t = P.sb([128, 12, NT], BF16, "y_t")
    yb = [Buf(y_t[:, k, :], f"y{k}") for k in range(12)]
    yc_t = P.sb([128, 4, NT], BF16, "yc_t")
    ycb = [Buf(yc_t[:, k, :], f"yc{k}") for k in range(4)]
    m_t = P.sb([128, 8, NT], F32, "m_t")
    mb = [Buf(m_t[:, k, :], f"m{k}") for k in range(8)]
    mbf_t = P.sb([128, 8, NT], BF16, "mbf_t")
    mbfb = [Buf(mbf_t[:, k, :], f"mbf{k}") for k in range(8)]
    a_t = P.sb([128, NF, NT], BF16, "a_t")
    ab = [Buf(a_t[:, k, :], f"a{k}") for k in range(NF)]
    p_t = P.sb([128, 2, NT], BF16, "p_t")
    pb = [Buf(p_t[:, k, :], f"p{k}") for k in range(2)]
    hist_t = P.sb([128, NF, 2], F32, "hist_t")
    histb = [Buf(hist_t[:, k, :], f"hist{k}") for k in range(NF)]
    sq_t = P.sb([128, 2, NT], F32, "sq_t")
    sqb = [Buf(sq_t[:, k, :], f"sq{k}") for k in range(2)]
    rstd = P.sbuf([128, NT], F32, "rstd")
    NG = 3
    g_t = P.sb([128, NG, NT], F32, "g_t")
    gtb = [Buf(g_t[:, k, :], f"gt{k}") for k in range(NG)]
    gi = [0]
    gb_t = P.sb([128, 2, NT + 2], F32, "gb_t")
    gbb = [Buf(gb_t[:, k, :], f"gb{k}") for k in range(2)]
    cv_t = P.sb([128, 2, NT], F32, "cv_t")
    cvb = [Buf(cv_t[:, k, :], f"cv{k}") for k in range(2)]
    ob = mb

    def rmsnorm(w, ncol, outs):
        ps = psp.get()
        for k in range(8):
            s = sqb[k % 2]
            P.op("act", lambda e, s=s, k=k: e.activation(s.ap[:, :w], xb[k].ap[:, :w], AF.Square),
                 reads=[xb[k]], writes=[s])
            P.mm(ps.ap[:, :w], ones.ap, s.ap[:, :w], start=(k == 0), stop=(k == 7),
                 reads=[ones, s], writes=[ps])
        P.op("act", lambda e: e.activation(rstd.ap[:, :w], ps.ap[:, :w], AF.Sqrt, bias=EPS, scale=1.0 / D),
             reads=[ps], writes=[rstd])
        P.op("dve", lambda e: e.reciprocal(rstd.ap[:, :w], rstd.ap[:, :w]), reads=[rstd], writes=[rstd])
        for k in range(8):
            P.op("dve", lambda e, k=k: e.scalar_tensor_tensor(
                outs[k].ap[:, :w], xb[k].ap[:, :w], c_norm.ap[:, ncol + k:ncol + k + 1], rstd.ap[:, :w],
                op0=ALU.mult, op1=ALU.mult), reads=[xb[k], c_norm, rstd], writes=[outs[k]])

    def tile(c0, w, halo):
        for k in range(8):
            src["x"](P, xb[k], k, c0, w, halo)
        for k in range(12):
            src["y"](P, yb[k], k, c0, w, halo)
        if halo:
            for k in range(8):
                P.op("dve", lambda e, k=k: e.tensor_scalar(xb[k].ap[:, :w], xb[k].ap[:, :w], hmask.ap[:, 0:1], None,
                                                           op0=ALU.mult), reads=[xb[k], hmask], writes=[xb[k]])
            for k in range(12):
                P.op("dve", lambda e, k=k: e.tensor_scalar(yb[k].ap[:, :w], yb[k].ap[:, :w], hmask.ap[:, 0:1], None,
                                                           op0=ALU.mult), reads=[yb[k], hmask], writes=[yb[k]])
        if not halo:
            for k in range(2):
                P.dma(pb[k].ap[:, :w], pT[k * 128:(k + 1) * 128, c0 - 2:c0 - 2 + w], writes=[pb[k]], q="pool")
        rmsnorm(w, 0, hb)
        U = load_w(w_glu, 0, 4, 0, 512)
        for m in range(4):
            ps = psp.get()
            for k in range(4):
                P.mm(ps.ap[:, :w], U.ap[:, k, m * 128:(m + 1) * 128], yb[8 + k].ap[:, :w],
                     start=(k == 0), stop=(k == 3), reads=[U, yb[8 + k]], writes=[ps])
            g = gtb[gi[0] % NG]; gi[0] += 1
            P.op("act", lambda e, g=g, ps=ps, m=m: e.activation(g.ap[:, :w], ps.ap[:, :w], AF.Sigmoid,
                                                             bias=c_bglu.ap[:, m:m + 1]),
                 reads=[ps, c_bglu], writes=[g])
            P.op("dve", lambda e, g=g, m=m: e.tensor_tensor(ycb[m].ap[:, :w], yb[8 + m].ap[:, :w], g.ap[:, :w],
                                                          op=ALU.mult),
                 reads=[yb[8 + m], g], writes=[ycb[m]])
        for i in range(3):
            G = load_w(w_gate, 0, 8, i * D, D)
            B = load_w(w_branch, i * 512, 4, 0, D)
            ysrc = [yb[0], yb[1], yb[2], yb[3]] if i == 0 else ([yb[4], yb[5], yb[6], yb[7]] if i == 1 else ycb)
            for m in range(8):
                pg = psp.get()
                for k in range(8):
                    P.mm(pg.ap[:, :w], G.ap[:, k, m * 128:(m + 1) * 128], hb[k].ap[:, :w],
                         start=(k == 0), stop=(k == 7), reads=[G, hb[k]], writes=[pg])
                g = gtb[gi[0] % NG]; gi[0] += 1
                P.op("act", lambda e, g=g, pg=pg, i=i, m=m: e.activation(
                    g.ap[:, :w], pg.ap[:, :w], AF.Sigmoid, bias=c_bg.ap[:, i * 8 + m:i * 8 + m + 1]),
                    reads=[pg, c_bg], writes=[g])
                pp = psp.get()
                for k in range(4):
                    P.mm(pp.ap[:, :w], B.ap[:, k, m * 128:(m + 1) * 128], ysrc[k].ap[:, :w],
                         start=(k == 0), stop=(k == 3), reads=[B, ysrc[k]], writes=[pp])
                if i == 0:
                    P.op("dve", lambda e, g=g, pp=pp, m=m: e.tensor_tensor(
                        mb[m].ap[:, :w], pp.ap[:, :w], g.ap[:, :w], op=ALU.mult),
                        reads=[pp, g], writes=[mb[m]])
                else:
                    P.op("dve", lambda e, g=g, pp=pp, m=m: e.tensor_tensor(
                        g.ap[:, :w], pp.ap[:, :w], g.ap[:, :w], op=ALU.mult),
                        reads=[pp, g], writes=[g])
                    dst = mb[m] if i == 1 else mbfb[m]
                    P.op("pool", lambda e, g=g, m=m, dst=dst: e.tensor_tensor(
                        dst.ap[:, :w], mb[m].ap[:, :w], g.ap[:, :w], op=ALU.add),
                        reads=[mb[m], g], writes=[dst])
        O = load_w(w_o, 0, 8, 0, D)
        for m in range(8):
            ps = psp.get()
            for k in range(8):
                P.mm(ps.ap[:, :w], O.ap[:, k, m * 128:(m + 1) * 128], mbfb[k].ap[:, :w],
                     start=(k == 0), stop=(k == 7), reads=[O, mbfb[k]], writes=[ps])
            P.op("dve", lambda e, ps=ps, m=m: e.tensor_tensor(xb[m].ap[:, :w], xb[m].ap[:, :w], ps.ap[:, :w],
                                                            op=ALU.add),
                 reads=[xb[m], ps], writes=[xb[m]])
        rmsnorm(w, 8, hb)
        for j0 in range(0, NF, 8):
            nj = min(8, NF - j0)
            Wg = load_w(w_up, 0, 8, j0 * 128, nj * 128)
            Wu = None if halo else load_w(w_up, 0, 8, DFF + j0 * 128, nj * 128)
            for jj in range(nj):
                j = j0 + jj
                pg = psp.get()
                for k in range(8):
                    P.mm(pg.ap[:, :w], Wg.ap[:, k, jj * 128:(jj + 1) * 128], hb[k].ap[:, :w],
                         start=(k == 0), stop=(k == 7), reads=[Wg, hb[k]], writes=[pg])
                if halo:
                    P.op("act", lambda e, pg=pg, j=j: e.activation(histb[j].ap, pg.ap[:, 0:2], AF.Identity),
                         reads=[pg], writes=[histb[j]])
                    continue
                pu = psp.get()
                for k in range(8):
                    P.mm(pu.ap[:, :w], Wu.ap[:, k, jj * 128:(jj + 1) * 128], hb[k].ap[:, :w],
                         start=(k == 0), stop=(k == 7), reads=[Wu, hb[k]], writes=[pu])
                gb = gbb[j % 2]
                cv = cvb[j % 2]
                P.op("pool", lambda e, gb=gb, j=j: e.tensor_copy(gb.ap[:, 0:2], histb[j].ap),
                     reads=[histb[j]], writes=[gb])
                P.op("act", lambda e, gb=gb, pg=pg: e.activation(gb.ap[:, 2:2 + w], pg.ap[:, :w], AF.Identity),
                     reads=[pg], writes=[gb])
                P.op("pool", lambda e, gb=gb, j=j: e.tensor_copy(histb[j].ap, gb.ap[:, w:w + 2]),
                     reads=[gb], writes=[histb[j]])
                P.op("dve", lambda e, gb=gb, cv=cv, j=j: e.tensor_scalar(
                    cv.ap[:, :w], gb.ap[:, 0:w], c_cw.ap[:, 3 * j:3 * j + 1], c_cb.ap[:, j:j + 1],
                    op0=ALU.mult, op1=ALU.add), reads=[gb, c_cw, c_cb], writes=[cv])
                for t in (1, 2):
                    P.op("dve", lambda e, gb=gb, cv=cv, j=j, t=t: e.scalar_tensor_tensor(
                        cv.ap[:, :w], gb.ap[:, t:t + w], c_cw.ap[:, 3 * j + t:3 * j + t + 1], cv.ap[:, :w],
                        op0=ALU.mult, op1=ALU.add), reads=[gb, c_cw, cv], writes=[cv])
                P.op("act", lambda e, cv=cv: e.activation(cv.ap[:, :w], cv.ap[:, :w], AF.Gelu_apprx_tanh),
                     reads=[cv], writes=[cv])
                P.op("dve", lambda e, cv=cv, pu=pu, j=j: e.tensor_tensor(
                    ab[j].ap[:, :w], pu.ap[:, :w], cv.ap[:, :w], op=ALU.mult),
                    reads=[pu, cv], writes=[ab[j]])
        if halo:
            return
        Ds = [load_w(w_down, j0 * 128, min(8, NF - j0), 0, D) for j0 in range(0, NF, 8)]
        for m in range(8):
            ps = psp.get()
            for j in range(NF):
                Dj = Ds[j // 8]
                P.mm(ps.ap[:, :w], Dj.ap[:, j % 8, m * 128:(m + 1) * 128], ab[j].ap[:, :w],
                     start=(j == 0), stop=(j == NF - 1), reads=[Dj, ab[j]], writes=[ps])
            P.op("dve", lambda e, ps=ps, m=m: e.tensor_tensor(xb[m].ap[:, :w], xb[m].ap[:, :w], ps.ap[:, :w],
                                                            op=ALU.add),
                 reads=[xb[m], ps], writes=[xb[m]])
        rmsnorm(w, 16, hb)
        PG = load_w(w_pg, 0, 8, 0, D)
        PP = load_w(w_pp, 0, 2, 0, D)
        for m in range(8):
            pg = psp.get()
            for k in range(8):
                P.mm(pg.ap[:, :w], PG.ap[:, k, m * 128:(m + 1) * 128], hb[k].ap[:, :w],
                     start=(k == 0), stop=(k == 7), reads=[PG, hb[k]], writes=[pg])
            g = gtb[gi[0] % NG]; gi[0] += 1
            P.op("act", lambda e, g=g, pg=pg: e.activation(g.ap[:, :w], pg.ap[:, :w], AF.Sigmoid),
                 reads=[pg], writes=[g])
            pp = psp.get()
            for k in range(2):
                P.mm(pp.ap[:, :w], PP.ap[:, k, m * 128:(m + 1) * 128], pb[k].ap[:, :w],
                     start=(k == 0), stop=(k == 1), reads=[PP, pb[k]], writes=[pp])
            P.op("dve", lambda e, g=g, pp=pp: e.tensor_tensor(g.ap[:, :w], pp.ap[:, :w], g.ap[:, :w], op=ALU.mult),
                 reads=[pp, g], writes=[g])
            P.op("pool", lambda e, g=g, m=m: e.tensor_tensor(xb[m].ap[:, :w], xb[m].ap[:, :w], g.ap[:, :w],
                                                           op=ALU.add),
                 reads=[xb[m], g], writes=[xb[m]])
        if last:
            rmsnorm(w, 24, ob)
            src_o = ob
        else:
            src_o = xb
        for m in range(8):
            src["out"](P, src_o[m], m, c0 - 2, w)
        src["after_tile"](P, c0 - 2)

    tile(0, 2, True)
    for t0 in range(0, TOK, NT):
        tile(2 + t0, min(NT, TOK - t0), False)


IN_OFF = dict(q=0, k=512, v=1024, b=1536, a=1540, g=1544, gq=2056, gk=2312, gv=2568, lr=3080, gr=3096, s5=3608)


def mixer_inputs(layer, hd, x_b, W):
    i = layer
    w_in = W["w_in"][i]
    o = IN_OFF
    cols_f = np.concatenate([
        np.arange(o["q"] + hd * 128, o["q"] + hd * 128 + 128),
        np.arange(o["k"] + hd * 128, o["k"] + hd * 128 + 128),
        np.arange(o["v"] + hd * 128, o["v"] + hd * 128 + 128),
        np.arange(o["s5"] + hd * 128, o["s5"] + hd * 128 + 128),
        np.arange(o["gq"] + hd * 64, o["gq"] + hd * 64 + 64),
        np.arange(o["gk"] + hd * 64, o["gk"] + hd * 64 + 64),
        np.arange(o["lr"], o["lr"] + 16)])
    cols_t = np.concatenate([
        np.arange(o["g"] + hd * 128, o["g"] + hd * 128 + 128),
        [o["b"] + hd], [o["a"] + hd],
        np.arange(o["gv"] + hd * 128, o["gv"] + hd * 128 + 128),
        np.arange(o["gr"] + hd * 128, o["gr"] + hd * 128 + 128)])
    cwv = W["dn_conv_w"][i]
    dn_cw = np.stack([cwv[:, c * 512 + hd * 128: c * 512 + hd * 128 + 128].T for c in range(3)], axis=1)
    rep = lambda v: np.ascontiguousarray(np.broadcast_to(v[None, :], (128, v.shape[0])))
    g0 = hd * 8
    lam = np.zeros((128, 12), np.float32)
    sb_ = np.zeros((2, 4, 128, 16), np.float32)
    sc_ = np.zeros((2, 4, 128, 16), np.float32)
    for j in range(4):
        for hf in range(2):
            g = g0 + 2 * j + hf
            lam[64 * hf:64 * hf + 64, j] = W["s5_lam_re"][i][g]
            lam[64 * hf:64 * hf + 64, 4 + j] = W["s5_lam_im"][i][g]
            lam[64 * hf:64 * hf + 64, 8 + j] = W["s5_log_step"][i][g]
            sb_[0, j, 64 * hf:64 * hf + 64] = W["s5_b_re"][i][g]
            sb_[1, j, 64 * hf:64 * hf + 64] = W["s5_b_im"][i][g]
            sc_[0, j, 64 * hf:64 * hf + 64] = W["s5_c_re"][i][g].T
            sc_[1, j, 64 * hf:64 * hf + 64] = W["s5_c_im"][i][g].T
    return {
        "anorm": np.ascontiguousarray(W["attn_norm"][i].reshape(8, 128).T),
        "wf": np.ascontiguousarray(w_in[:, cols_f]), "wt": np.ascontiguousarray(w_in[:, cols_t]),
        "dn_sc": np.ascontiguousarray(np.stack([np.full(128, W["dn_a_log"][i][hd], np.float32),
                                               np.full(128, W["dn_dt_bias"][i][hd], np.float32)], axis=1)),
        "dn_cw": np.ascontiguousarray(dn_cw.reshape(128, 12)),
        "dn_ng": rep(W["dn_norm"][i]), "gla_ng": rep(W["gla_norm"][i]),
        "gla_w2": np.ascontiguousarray(W["gla_w2"][i][:, hd * 64:hd * 64 + 64]),
        "gla_b2": np.ascontiguousarray(W["gla_b2"][i][hd * 64:hd * 64 + 64].reshape(64, 1)),
        "s5_lam": lam, "s5_b": sb_, "s5_c": sc_,
        "s5_d": np.ascontiguousarray(W["s5_d"][i][hd * 128:hd * 128 + 128].reshape(128, 1)),
    }


def dense_inputs(layer, x_seg, y_seg, p_seg, W):
    i = layer
    f = np.float32
    col = lambda v: np.ascontiguousarray(v.reshape(-1, 128).T)
    norms = np.concatenate([col(W["attn_norm"][i]), col(W["ffn_norm"][i]), col(W["ple_norm"][i]),
                            col(W["final_norm"])], axis=1)
    cw = np.ascontiguousarray(W["ffn_conv_w"][i].reshape(3, NF, 128).transpose(2, 1, 0).reshape(128, NF * 3))
    return {
        "pT": np.ascontiguousarray(p_seg.T),
        "w_gate": W["w_gate"][i], "b_gate": col(W["b_gate"][i]),
        "w_branch": np.ascontiguousarray(W["w_branch"][i].reshape(1536, D)),
        "w_o": W["w_o"][i], "w_glu": W["s5_w_glu"][i], "b_glu": col(W["s5_b_glu"][i]),
        "norms": np.ascontiguousarray(norms), "w_up": W["w_up"][i], "cw": cw, "cb": col(W["ffn_conv_b"][i]),
        "w_down": W["w_down"][i], "w_pg": W["w_ple_gate"][i], "w_pp": W["w_ple_proj"][i],
    }


import os
NOCOLL = os.environ.get('FUSE_NOCOLL') == '1'
NODYN = os.environ.get('FUSE_NODYN') == '1'


def build_fused(L=SEQ):
    nc = bass.Bass("TRN2", target_bir_lowering=False, num_devices=NCORES)
    TOK = L // 4
    CH = 1024
    NCH = L // CH
    CPS = TOK // CH
    NT8 = TOK // 512
    assert CPS >= 1 and TOK % 512 == 0
    xT_b = nc.dram_tensor("xT_b", [D, L], F32, kind="ExternalInput").ap()
    xs0 = nc.dram_tensor("xs0", [D, TOK + 2], F32, kind="ExternalInput").ap()
    hmask_d = nc.dram_tensor("hmask", [128, 1], F32, kind="ExternalInput").ap()
    oT = nc.dram_tensor("oT", [D, TOK], F32, kind="ExternalOutput").ap()
    yloc = [[nc.dram_tensor(f"yloc{l}_{c}", [384, CH], BF16).ap() for c in range(NCH)] for l in range(2)]
    yall_t = [nc.dram_tensor(f"yall{l}", [NCH * 1536, CH], BF16).ap() for l in range(2)]
    x1c = [[nc.dram_tensor(f"x1c_{t}_{h}", [512, 512], F32).ap() for h in range(2)] for t in range(NT8)]
    xgc = [[nc.dram_tensor(f"xgc_{t}_{h}", [4 * 512, 512], F32).ap() for h in range(2)] for t in range(NT8)]
    yseg = nc.dram_tensor("yseg", [4 * 384, TOK + 2], BF16).ap()
    xh = nc.dram_tensor("xh", [D, 2], F32).ap()
    Byloc = [[Buf(yloc[l][c]) for c in range(NCH)] for l in range(2)]
    Byall = [[Buf(yall_t[l][c * 1536:(c + 1) * 1536, :]) for c in range(NCH)] for l in range(2)]
    Bx1c = [[Buf(x1c[t][h]) for h in range(2)] for t in range(NT8)]
    Bxgc = [[Buf(xgc[t][h]) for h in range(2)] for t in range(NT8)]
    Byseg, Bxh = Buf(yseg, "yseg"), Buf(xh, "xh")
    rv = {}
    P = Prog(nc)
    O = Ops(P)

    for layer in (0, 1):
        sfx = f"_{layer}"
        P.prefix = f"m{layer}_"
        P.begin_phase()
        if layer == 0:
            def xsrc(k, t0):
                return xT_b[k * 128:(k + 1) * 128, t0:t0 + TB]
        else:
            def xsrc(k, t0):
                sg_, tt = t0 // TOK, (t0 % TOK) // 512
                r0 = sg_ * 512 + (k % 4) * 128
                return View(Bxgc[tt][k // 4], xgc[tt][k // 4][r0:r0 + 128, :])

        def yout(i3, t0, ysb, layer=layer):
            ci, off = t0 // CH, t0 % CH
            P.dma(yloc[layer][ci][i3 * 128:(i3 + 1) * 128, off:off + TB], ysb.ap, reads=[ysb], writes=[Byloc[layer][ci]])

        def after_sb(t0, layer=layer):
            ci, off = t0 // CH, t0 % CH
            if off + TB == CH and not NOCOLL:
                P.coll("AllGather", yloc[layer][ci], yall_t[layer][ci * 1536:(ci + 1) * 1536, :], GROUPS,
                       reads=[Byloc[layer][ci]], writes=[Byall[layer][ci]])
        emit_mixer(nc, P, O, L, sfx, xsrc, yout, after_sb)
        P.end_phase()
        P.prefix = f"d{layer}_"
        P.begin_phase()
        if layer == 0:
            def setup(e, rv=rv):
                r = e.snap(e.partition_id() % 4, min_val=0, max_val=3)
                rv["yrow"] = e.snap(r * (CPS * 1536), min_val=0, max_val=3 * CPS * 1536)
                rv["hrow"] = e.snap(((r * CPS + (NCH - 1)) % NCH) * 1536, min_val=0, max_val=(NCH - 1) * 1536)
                rv["prow"] = e.snap(((r + 3) % 4) * 512, min_val=0, max_val=3 * 512)
                return None
            P.items["sp"].append(([], setup, None, 0))
        ya = yall_t[layer]
        for j in range(CPS):
            P.dma(yseg[:, 2 + j * CH:2 + (j + 1) * CH],
                  (lambda e, j=j, ya=ya: ya[(slice(j * 1536, (j + 1) * 1536) if NODYN else bass.ds(rv["yrow"] + j * 1536, 1536)), :]),
                  reads=Byall[layer], writes=[Byseg])
        P.dma(yseg[:, 0:2], (lambda e, ya=ya: ya[(slice(0, 1536) if NODYN else bass.ds(rv["hrow"], 1536)), CH - 2:CH]),
              reads=Byall[layer], writes=[Byseg])
        if layer == 1:
            for h in range(2):
                P.dma(xh[h * 512:(h + 1) * 512, :],
                      (lambda e, h=h: xgc[NT8 - 1][h][(slice(0, 512) if NODYN else bass.ds(rv["prow"], 512)), 510:512]),
                      reads=[Bxgc[NT8 - 1][h]], writes=[Bxh])

        def fx(P_, buf, k, c0, w, halo, layer=layer):
            if layer == 0:
                P_.dma(buf.ap[:, :w], xs0[k * 128:(k + 1) * 128, c0:c0 + w], writes=[buf])
            elif halo:
                P_.dma(buf.ap[:, :2], xh[k * 128:(k + 1) * 128, :], reads=[Bxh], writes=[buf])
            else:
                tt = (c0 - 2) // 512
                P_.dma(buf.ap[:, :w], x1c[tt][k // 4][(k % 4) * 128:(k % 4) * 128 + 128, 0:w],
                       reads=[Bx1c[tt][k // 4]], writes=[buf])

        def fy(P_, buf, k, c0, w, halo):
            row0 = (k % 4) * 384 + (k // 4) * 128
            P_.dma(buf.ap[:, :w], yseg[row0:row0 + 128, c0:c0 + w], reads=[Byseg], writes=[buf])

        def fo(P_, buf, m_, t, w, layer=layer):
            if layer == 0:
                tt = t // 512
                P_.dma(x1c[tt][m_ // 4][(m_ % 4) * 128:(m_ % 4) * 128 + 128, 0:w], buf.ap[:, :w],
                       reads=[buf], writes=[Bx1c[tt][m_ // 4]])
            else:
                P_.dma(oT[m_ * 128:(m_ + 1) * 128, t:t + w], buf.ap[:, :w], reads=[buf])

        def after_tile(P_, t, layer=layer):
            if layer == 0 and not NOCOLL:
                tt = t // 512
                for h in range(2):
                    P_.coll("AllGather", x1c[tt][h], xgc[tt][h], GROUPS, reads=[Bx1c[tt][h]], writes=[Bxgc[tt][h]])

        emit_dense(nc, P, TOK, layer == 1, sfx, dict(hmask=hmask_d, x=fx, y=fy, out=fo, after_tile=after_tile))
        P.end_phase(final=(layer == 1))
    P.close()
    return nc


def kernel(**inputs):
    W = {k: np.asarray(v, dtype=np.float32) for k, v in inputs.items()}
    x = np.ascontiguousarray(W.pop("x"))
    p = W.pop("p")
    Bsz, L, _ = x.shape
    depth = W["w_in"].shape[0]
    assert depth == 2 and Bsz == 2
    TOK = L // 4
    nc = build_fused(L)
    xT = [np.ascontiguousarray(x[b].T) for b in range(Bsz)]
    in_maps = []
    for c in range(NCORES):
        b, r = c // 4, c % 4
        s0 = r * TOK
        xs = np.zeros((D, TOK + 2), np.float32)
        lo = max(s0 - 2, 0)
        xs[:, 2 - (s0 - lo):] = xT[b][:, lo:s0 + TOK]
        im = {"xT_b": xT[b], "xs0": xs,
              "hmask": np.full((128, 1), 0.0 if r == 0 else 1.0, np.float32)}
        for i in range(depth):
            for k, v in mixer_inputs(i, r, None, W).items():
                im[f"{k}_{i}"] = v
            for k, v in dense_inputs(i, None, None, p[i, b, s0:s0 + TOK], W).items():
                im[f"{k}_{i}"] = v
        in_maps.append(im)
    res = run_bass_kernel_spmd(nc, in_maps, core_ids=list(range(NCORES)))
    out = np.empty_like(x)
    for c in range(NCORES):
        b, r = c // 4, c % 4
        out[b, r * TOK:(r + 1) * TOK] = res.results[c]["oT"].T
    return out
```

```python
import numpy as np
from contextlib import ExitStack
import concourse.bass as bass
import concourse.mybir as mybir
from concourse.bass_utils import run_bass_kernel_spmd

F32 = mybir.dt.float32
BF16 = mybir.dt.bfloat16
AF = mybir.ActivationFunctionType
ALU = mybir.AluOpType
AX = mybir.AxisListType

ENGS = ("pe", "act", "dve", "pool", "sp")
EPOCH = 16000
RING = 12


class Buf:
    __slots__ = ("ap", "w", "r", "name", "psum")

    def __init__(self, ap, name=""):
        self.ap = ap
        self.psum = False
        self.w = None
        self.r = []
        self.name = name

    def __getitem__(self, k):
        return self.ap[k]


class Prog:
    def __init__(self, nc, self_sync=True, prefix=""):
        self.nc = nc
        self.prefix = prefix
        self.es = ExitStack()
        self.es_sem = ExitStack()
        self.phase_finals = []
        self.items = {e: [] for e in ENGS}
        self.count = {e: 0 for e in ENGS}
        self.waited = {e: {} for e in ENGS}
        self.self_sync = self_sync
        self.semh = {}
        self.dma_n = {"sp": 0, "pool": 0, "act": 0}
        self.dma_last = {}
        self.nbuf = 0

    def sem(self, key):
        if key not in self.semh:
            nm = self.prefix + "s_" + "_".join(str(k) for k in key)
            self.semh[key] = self.es_sem.enter_context(self.nc.semaphore(nm))
        return self.semh[key]

    def sb(self, shape, dtype=F32, name=None):
        self.nbuf += 1
        name = self.prefix + (name or f"sb{self.nbuf}")
        t = self.es.enter_context(self.nc.sbuf_tensor(name, list(shape), dtype))
        return t

    def ps(self, shape, dtype=F32, name=None):
        self.nbuf += 1
        name = self.prefix + (name or f"ps{self.nbuf}")
        t = self.es.enter_context(self.nc.psum_tensor(name, list(shape), dtype))
        return t

    def buf(self, ap, name=""):
        return Buf(ap, name)

    def sbuf(self, shape, dtype=F32, name=None):
        t = self.sb(shape, dtype, name)
        return Buf(t[:], name or "")

    def psbuf(self, shape, dtype=F32, name=None):
        t = self.ps(shape, dtype, name)
        b = Buf(t[:], name or "")
        b.psum = True
        return b

    def _deps(self, reads, writes):
        deps = []
        for b in reads:
            if b.w is not None:
                deps.append(b.w)
        for b in writes:
            if b.w is not None:
                deps.append(b.w)
            deps.extend(b.r)
        return deps

    def _waits(self, eng, deps, own_key_prefix):
        waits = []
        wd = self.waited[eng]
        for (key, val) in deps:
            if key[0] == own_key_prefix and key[0] != "dma":
                if eng == "pe" or not self.self_sync:
                    continue
            if wd.get(key, 0) >= val:
                continue
            wd[key] = val
            waits.append((key, val))
        return waits

    def op(self, eng, fn, reads=(), writes=()):
        pr = [b for b in reads if b.psum]
        if pr:
            reads = [b for b in reads if not b.psum]
            writes = list(writes) + pr
        deps = self._deps(reads, writes)
        waits = self._waits(eng, deps, eng)
        self.count[eng] += 1
        c = self.count[eng]
        key = (eng, (c - 1) // EPOCH)
        val = (c - 1) % EPOCH + 1
        ev = (key, val)
        self.items[eng].append((waits, fn, ev, 1))
        for b in reads:
            b.r.append(ev)
        for b in writes:
            b.w = ev
            b.r = []
        return ev

    def dma(self, out_ap, in_ap, reads=(), writes=(), q="sp", **kw):
        deps = self._deps(reads, writes)
        j = self.dma_n[q]
        self.dma_n[q] += 1
        slot = j % RING
        key = ("dma", q, slot)
        if j >= RING:
            deps.append((key, 16 * (j // RING)))
        waits = self._waits(q, deps, "dma")
        val = 16 * (j // RING + 1)
        ev = (key, val)
        self.dma_last[key] = val

        def fn(e, out_ap=out_ap, in_ap=in_ap, kw=kw):
            o = out_ap(e) if callable(out_ap) else out_ap
            i = in_ap(e) if callable(in_ap) else in_ap
            try:
                return e.dma_start(out=o, in_=i, **kw)
            except Exception:
                print('DMA FAIL out', o, 'in', i, flush=True)
                raise
        self.items[q].append((waits, fn, ev, 16))
        for b in reads:
            b.r.append(ev)
        for b in writes:
            b.w = ev
            b.r = []
        return ev

    def _last_events(self):
        finals = []
        for key, val in self.dma_last.items():
            finals.append((key, val))
        for e in ("pe", "act", "dve", "pool"):
            c = self.count[e]
            if c > 0:
                finals.append(((e, (c - 1) // EPOCH), (c - 1) % EPOCH + 1))
        return finals

    def begin_phase(self):
        finals = self._last_events()
        for e in ENGS:
            waits = self._waits(e, finals, "__none__")
            if waits:
                self.items[e].append((waits, None, None, 0))

    def end_phase(self, final=False):
        nc = self.nc
        for e in ENGS:
            for (waits, fn, ev, inc) in self.items[e]:
                if ev is not None:
                    self.sem(ev[0])
                for (k, v) in waits:
                    self.sem(k)
        final_waits = self._last_events() if final else []
        for (k, v) in final_waits:
            self.sem(k)
        items = self.items
        semh = self.semh

        def run(e, lst, fin=False):
            for (waits, fn, ev, inc) in lst:
                for (k, v) in waits:
                    e.wait_ge(semh[k], v)
                if fn is None:
                    continue
                ins = fn(e)
                if ins is not None and ev is not None:
                    ins.then_inc(semh[ev[0]], inc)
            if fin:
                for (k, v) in final_waits:
                    e.wait_ge(semh[k], v)

        with nc.Block() as block:
            @block.sync
            def _(e):
                run(e, items["sp"], fin=final)

            @block.tensor
            def _(e):
                run(e, items["pe"])

            @block.scalar
            def _(e):
                run(e, items["act"])

            @block.vector
            def _(e):
                run(e, items["dve"])

            @block.gpsimd
            def _(e):
                run(e, items["pool"])
        self.items = {e: [] for e in ENGS}
        self.es.close()
        self.es = ExitStack()

    def emit(self):
        self.end_phase(final=True)

    def close(self):
        self.es.close()
        self.es_sem.close()

    def mm(self, out, lhsT, rhs, start=True, stop=True, reads=(), writes=()):
        return self.op("pe", lambda e: e.matmul(out, lhsT, rhs, start=start, stop=stop),
                       reads, writes)


class View:
    __slots__ = ("buf", "ap")

    def __init__(self, buf, ap):
        self.buf = buf
        self.ap = ap

    def __getitem__(self, k):
        return View(self.buf, self.ap[k])


def V(buf, *k):
    if not k:
        return View(buf, buf.ap)
    return View(buf, buf.ap[k if len(k) > 1 else k[0]])


def _ap(x):
    return x.ap if isinstance(x, View) else x


def _bufs(*xs):
    return [x.buf for x in xs if isinstance(x, View)]


class Ops:
    def __init__(self, P):
        self.P = P

    def mm(self, out, lhsT, rhs, start=True, stop=True):
        return self.P.op("pe", lambda e: e.matmul(out.ap, lhsT.ap, rhs.ap, start=start, stop=stop),
                         _bufs(lhsT, rhs), _bufs(out))

    def tr(self, out, in_, ident):
        return self.P.op("pe", lambda e: e.transpose(out.ap, in_.ap, ident.ap), _bufs(in_, ident), _bufs(out))

    def act(self, out, in_, func, bias=None, scale=None, accum=None, eng="act"):
        kw = {}
        if bias is not None:
            kw["bias"] = _ap(bias)
        if scale is not None:
            kw["scale"] = _ap(scale)
        if accum is not None:
            kw["accum_out"] = _ap(accum)
        return self.P.op("act", lambda e: e.activation(out.ap, in_.ap, func, **kw),
                         _bufs(in_, bias, scale), _bufs(out, accum))

    def tt(self, eng, out, in0, in1, op):
        return self.P.op(eng, lambda e: e.tensor_tensor(out.ap, in0.ap, in1.ap, op=op), _bufs(in0, in1), _bufs(out))

    def ts(self, eng, out, in0, s1, op0, s2=None, op1=None):
        if op1 is None:
            return self.P.op(eng, lambda e: e.tensor_scalar(out.ap, in0.ap, _ap(s1), None, op0=op0),
                             _bufs(in0, s1), _bufs(out))
        return self.P.op(eng, lambda e: e.tensor_scalar(out.ap, in0.ap, _ap(s1), _ap(s2), op0=op0, op1=op1),
                         _bufs(in0, s1, s2), _bufs(out))

    def stt(self, eng, out, in0, scalar, in1, op0, op1):
        return self.P.op(eng, lambda e: e.scalar_tensor_tensor(out.ap, in0.ap, _ap(scalar), in1.ap, op0=op0, op1=op1),
                         _bufs(in0, scalar, in1), _bufs(out))

    def copy(self, eng, out, in_):
        if eng == "act":
            return self.act(out, in_, AF.Identity)
        return self.P.op(eng, lambda e: e.tensor_copy(out.ap, in_.ap), _bufs(in_), _bufs(out))

    def recip(self, out, in_):
        return self.P.op("dve", lambda e: e.reciprocal(out.ap, in_.ap), _bufs(in_), _bufs(out))

    def scan(self, eng, out, d0, d1, init, op0=None, op1=None):
        op0 = op0 or ALU.mult
        op1 = op1 or ALU.add
        return self.P.op(eng, lambda e: e.tensor_tensor_scan(out.ap, d0.ap, d1.ap, _ap(init), op0=op0, op1=op1),
                         _bufs(d0, d1, init), _bufs(out))

    def memset(self, eng, out, val):
        return self.P.op(eng, lambda e: e.memset(out.ap, val), [], _bufs(out))

    def aselect(self, out, in_, pattern, cmp, fill, base=0, cm=1):
        return self.P.op("pool", lambda e: e.affine_select(out.ap, in_.ap, pattern=pattern, compare_op=cmp, fill=fill,
                                                          base=base, channel_multiplier=cm),
                         _bufs(in_), _bufs(out))

    def dma(self, out, in_, q="sp"):
        return self.P.dma(_ap(out), _ap(in_), reads=_bufs(in_), writes=_bufs(out), q=q)


def _prog_coll(self, kind, in_ap, out_ap, groups, reads=(), writes=()):
    q = "pool"
    deps = self._deps(reads, writes)
    self.ncoll = getattr(self, "ncoll", 0) + 1
    key = ("cc", self.ncoll)
    waits = self._waits(q, deps, "dma")
    ev = (key, 1)
    self.dma_last[key] = 1

    def fn(e):
        return e.collective_compute(kind, ALU.bypass, groups, [in_ap.opt()], [out_ap.opt()])
    self.items[q].append((waits, fn, ev, 1))
    for b in reads:
        b.r.append(ev)
    for b in writes:
        b.w = ev
        b.r = []
    return ev


Prog.coll = _prog_coll

import math

D = 1024
DFF = 2816
NF = 22
EPS = 1e-6
NEG = -1.0e30
TB = 512
NB = 4
NWF = 656
NWT = 386
SEQ = 16384
NCORES = 8
GROUPS = [[0, 1, 2, 3], [4, 5, 6, 7]]


class PsPool:
    def __init__(self, P, n=8, name="psp"):
        self.bufs = [P.psbuf([128, 512], F32, f"{name}{i}") for i in range(n)]
        self.i = 0

    def get(self):
        b = self.bufs[self.i % len(self.bufs)]
        self.i += 1
        return b

def emit_mixer(nc, P, O, L, sfx, xsrc, yout, after_sb):
    def din(name, shape):
        return nc.dram_tensor(name + sfx, list(shape), F32, kind="ExternalInput").ap()
    anorm = din("anorm", [128, 8])
    wf_d = din("wf", [D, NWF])
    wt_d = din("wt", [D, NWT])
    dn_sc_d = din("dn_sc", [128, 2])
    dn_cw_d = din("dn_cw", [128, 12])
    dn_ng_d = din("dn_ng", [128, 128])
    gla_ng_d = din("gla_ng", [128, 128])
    gla_w2_d = din("gla_w2", [16, 64])
    gla_b2_d = din("gla_b2", [64, 1])
    s5_lam_d = din("s5_lam", [128, 12])
    s5_b_d = din("s5_b", [2, 4, 128, 16])
    s5_c_d = din("s5_c", [2, 4, 128, 16])
    s5_d_d = din("s5_d", [128, 1])
    psp = PsPool(P, n=7)
    yps_ded = P.psbuf([128, 512], F32, "yps_ded")

    def S(shape, dt=F32, name=None):
        return P.sbuf(shape, dt, name)

    ones = S([128, 512], F32, "ones")
    O.memset("pool", V(ones), 1.0)
    ident = S([128, 128], F32, "ident")
    O.memset("pool", V(ident), 1.0)
    O.aselect(V(ident), V(ident), [[-1, 128]], ALU.is_equal, 0.0, cm=1)
    U = S([128, 128], F32, "U")
    O.memset("pool", V(U), 1.0)
    O.aselect(V(U), V(U), [[1, 128]], ALU.is_ge, 0.0, cm=-1)
    mneg = S([128, 128], F32, "mneg")
    O.memset("pool", V(mneg), 0.0)
    O.aselect(V(mneg), V(mneg), [[-1, 128]], ALU.is_ge, NEG, cm=1)
    mnegT = S([128, 128], F32, "mnegT")
    O.memset("pool", V(mnegT), 0.0)
    O.aselect(V(mnegT), V(mnegT), [[1, 128]], ALU.is_ge, NEG, cm=-1)
    nLs = S([128, 128], F32, "nLs")
    O.memset("pool", V(nLs), -1.0)
    O.aselect(V(nLs), V(nLs), [[-1, 128]], ALU.is_gt, 0.0, cm=1)

    c_anorm = S([128, 8], F32, "c_anorm")
    O.dma(V(c_anorm), anorm)
    wf = S([128, 8, NWF], BF16, "wf_s")
    wt = S([128, 8, NWT], BF16, "wt_s")
    O.dma(V(wf), wf_d.rearrange("(k p) n -> p k n", p=128), q="pool")
    O.dma(V(wt), wt_d.rearrange("(k p) n -> p k n", p=128), q="pool")
    dn_sc = S([128, 2], F32, "dn_sc_s"); O.dma(V(dn_sc), dn_sc_d)
    dn_cw = S([128, 12], F32, "dn_cw_s"); O.dma(V(dn_cw), dn_cw_d)
    dn_ng = S([128, 128], F32, "dn_ng_s"); O.dma(V(dn_ng), dn_ng_d)
    gla_ng = S([128, 128], F32, "gla_ng_s"); O.dma(V(gla_ng), gla_ng_d)
    gla_w2 = S([16, 64], F32, "gla_w2_s"); O.dma(V(gla_w2), gla_w2_d)
    gla_b2 = S([64, 1], F32, "gla_b2_s"); O.dma(V(gla_b2), gla_b2_d)
    nb2 = S([64, 1], F32, "nb2")
    O.ts("dve", V(nb2), V(gla_b2), -1.0, ALU.mult)
    negA = S([128, 1], F32, "negA")
    O.act(V(negA), V(dn_sc, slice(None), slice(0, 1)), AF.Exp)
    O.ts("dve", V(negA), V(negA), -1.0, ALU.mult)
    s5d = S([128, 1], F32, "s5d"); O.dma(V(s5d), s5_d_d)

    lam = S([128, 12], F32, "lam"); O.dma(V(lam), s5_lam_d)
    lr_, li_, ls_ = (V(lam, slice(None), slice(0, 4)), V(lam, slice(None), slice(4, 8)),
                     V(lam, slice(None), slice(8, 12)))
    pp = S([128, 64], F32, "s5pp")

    def col(i):
        return V(pp, slice(None), slice(4 * i, 4 * i + 4))
    step, lrs, th, mag, c8, s8, t0, t1, cr, ci, den, nr, fr, fi, t2, t3 = [col(i) for i in range(16)]
    O.act(step, ls_, AF.Exp)
    O.tt("dve", lrs, lr_, step, ALU.mult)
    O.tt("dve", th, li_, step, ALU.mult)
    O.act(mag, lrs, AF.Exp)
    halfpi = S([128, 1], F32, "halfpi"); O.memset("pool", V(halfpi), math.pi / 2)
    O.act(s8, th, AF.Sin, scale=0.125)
    O.act(c8, th, AF.Sin, scale=-0.125, bias=V(halfpi))
    for _ in range(3):
        O.tt("dve", t0, c8, c8, ALU.mult)
        O.tt("dve", t1, s8, s8, ALU.mult)
        O.tt("dve", t2, c8, s8, ALU.mult)
        O.tt("dve", c8, t0, t1, ALU.subtract)
        O.ts("dve", s8, t2, 2.0, ALU.mult)
    O.tt("dve", cr, mag, c8, ALU.mult)
    O.tt("dve", ci, mag, s8, ALU.mult)
    O.tt("dve", t0, lr_, lr_, ALU.mult)
    O.tt("dve", t1, li_, li_, ALU.mult)
    O.tt("dve", den, t0, t1, ALU.add)
    O.recip(den, den)
    O.ts("dve", nr, cr, -1.0, ALU.add)
    O.tt("dve", t0, nr, lr_, ALU.mult)
    O.tt("dve", t1, ci, li_, ALU.mult)
    O.tt("dve", t0, t0, t1, ALU.add)
    O.tt("dve", fr, t0, den, ALU.mult)
    O.tt("dve", t0, ci, lr_, ALU.mult)
    O.tt("dve", t1, nr, li_, ALU.mult)
    O.tt("dve", t0, t0, t1, ALU.subtract)
    O.tt("dve", fi, t0, den, ALU.mult)

    Ct = [S([128, TB], F32, f"Ct{j}") for j in range(4)]
    St = [S([128, TB], F32, f"St{j}") for j in range(4)]
    Mg = [S([128, TB], F32, f"Mg{j}") for j in range(4)]
    rq = S([128, 16], F32, "rq")
    r512 = S([128, 8], F32, "r512")
    tmpT = S([128, TB // 2], F32, "tmpT")
    for j in range(4):
        cj, sj, ta, tb_ = [V(rq, slice(None), slice(4 * j + i, 4 * j + i + 1)) for i in range(4)]
        O.copy("dve", cj, V(pp, slice(None), slice(4 * 4 + j, 4 * 4 + j + 1)))
        O.copy("dve", sj, V(pp, slice(None), slice(4 * 5 + j, 4 * 5 + j + 1)))
        O.memset("pool", V(Ct[j], slice(None), slice(0, 1)), 1.0)
        O.memset("pool", V(St[j], slice(None), slice(0, 1)), 0.0)
        n = 1
        while n < TB:
            lo_c, lo_s = V(Ct[j], slice(None), slice(0, n)), V(St[j], slice(None), slice(0, n))
            hi_c, hi_s = V(Ct[j], slice(None), slice(n, 2 * n)), V(St[j], slice(None), slice(n, 2 * n))
            tm = V(tmpT, slice(None), slice(0, n))
            O.ts("dve", tm, lo_s, sj, ALU.mult)
            O.stt("dve", hi_c, lo_c, cj, tm, ALU.mult, ALU.subtract)
            O.ts("dve", tm, lo_c, sj, ALU.mult)
            O.stt("dve", hi_s, lo_s, cj, tm, ALU.mult, ALU.add)
            O.tt("dve", ta, cj, cj, ALU.mult)
            O.tt("dve", tb_, sj, sj, ALU.mult)
            O.tt("dve", sj, cj, sj, ALU.mult)
            O.ts("dve", sj, sj, 2.0, ALU.mult)
            O.tt("dve", cj, ta, tb_, ALU.subtract)
            n *= 2
        O.copy("dve", V(r512, slice(None), slice(2 * j, 2 * j + 1)), cj)
        O.copy("dve", V(r512, slice(None), slice(2 * j + 1, 2 * j + 2)), sj)
        O.ts("dve", V(Mg[j]), V(ones), V(pp, slice(None), slice(4 * 3 + j, 4 * 3 + j + 1)), ALU.mult)

    BreT = [S([128, 128], F32, f"BreT{j}") for j in range(4)]
    BimT = [S([128, 128], F32, f"BimT{j}") for j in range(4)]
    Cre = [S([128, 128], F32, f"Cre{j}") for j in range(4)]
    Cim = [S([128, 128], F32, f"Cim{j}") for j in range(4)]
    bst = S([128, 2, 16], F32, "bst")
    padr = S([128, 128], F32, "padr")
    padi = S([128, 128], F32, "padi")
    for j in range(4):
        O.dma(V(bst, slice(None), 0), s5_b_d[0, j])
        O.dma(V(bst, slice(None), 1), s5_b_d[1, j])
        O.memset("pool", V(padr), 0.0)
        O.memset("pool", V(padi), 0.0)
        O.memset("pool", V(Cre[j]), 0.0)
        O.memset("pool", V(Cim[j]), 0.0)
        frj = V(pp, slice(None), slice(4 * 12 + j, 4 * 12 + j + 1))
        fij = V(pp, slice(None), slice(4 * 13 + j, 4 * 13 + j + 1))
        for hf in range(2):
            ps_ = slice(64 * hf, 64 * hf + 64)
            cs_ = slice((2 * j + hf) * 16, (2 * j + hf) * 16 + 16)
            bre, bim = V(bst, ps_, 0), V(bst, ps_, 1)
            tm = V(tmpT, ps_, slice(0, 16))
            O.ts("dve", tm, bim, fij[ps_], ALU.mult)
            O.stt("dve", V(padr, ps_, cs_), bre, frj[ps_], tm, ALU.mult, ALU.subtract)
            O.ts("dve", tm, bre, fij[ps_], ALU.mult)
            O.stt("dve", V(padi, ps_, cs_), bim, frj[ps_], tm, ALU.mult, ALU.add)
            O.dma(V(Cre[j], ps_, cs_), s5_c_d[0, j, 64 * hf:64 * hf + 64, :])
            O.dma(V(Cim[j], ps_, cs_), s5_c_d[1, j, 64 * hf:64 * hf + 64, :])
        for (pad, dst) in ((padr, BreT[j]), (padi, BimT[j])):
            pt = psp.get()
            O.tr(V(pt, slice(None), slice(0, 128)), V(pad), V(ident))
            O.copy("act", V(dst), V(pt, slice(None), slice(0, 128)))
        O.ts("dve", V(Cim[j]), V(Cim[j]), -1.0, ALU.mult)

    Sdn = [S([128, 128], F32, f"Sdn{i}") for i in range(2)]
    Sgl = [S([64, 128], F32, f"Sgl{i}") for i in range(2)]
    O.memset("pool", V(Sdn[0]), 0.0)
    O.memset("pool", V(Sgl[0]), 0.0)
    s5c = [S([128, 2], F32, f"s5c{j}") for j in range(4)]
    for j in range(4):
        O.memset("pool", V(s5c[j]), 0.0)
    s5i = [S([128, 4], F32, f"s5i{j}") for j in range(4)]
    chist = [S([128, 3], F32, f"chist{c}") for c in range(3)]
    for c in range(3):
        O.memset("pool", V(chist[c]), 0.0)

    x_t = P.sb([128, 8, TB], F32, "x_t")
    xb = [Buf(x_t[:, k, :], f"x{k}") for k in range(8)]
    h_t = P.sb([128, 8, TB], BF16, "h_t")
    hb = [Buf(h_t[:, k, :], f"h{k}") for k in range(8)]
    sq = [S([128, TB], F32, f"sq{i}") for i in range(2)]
    rstd = S([128, TB], F32, "rstd")
    cbuf = [S([128, TB + 3], F32, f"cbuf{c}") for c in range(3)]
    cacc = [S([128, TB], F32, f"cacc{c}") for c in range(3)]
    qn = S([128, TB], F32, "qn")
    kn = S([128, TB], F32, "kn")
    tok = [S([128, NWT], F32, f"tok{b}") for b in range(NB)]
    uT = S([128, TB], F32, "uT")
    lrs_b = S([16, TB], F32, "lrs_b")
    gls = S([64, TB], F32, "gls")
    gc = S([64, TB], F32, "gc")
    gEQ = S([64, TB], F32, "gEQ")
    gEK = S([64, TB], F32, "gEK")
    gqe = S([64, TB], F32, "gqe")
    gke = S([64, TB], F32, "gke")
    gkr = S([64, TB], F32, "gkr")
    gqr = S([64, TB], F32, "gqr")
    s5w = [S([128, TB], F32, f"s5w{i}") for i in range(6)]
    yc_sb = S([128, TB], F32, "yc_sb")
    ys_t = P.sb([128, 3, TB], BF16, "ys_t")
    ys = [Buf(ys_t[:, i3, :], f"ys{i3}") for i3 in range(3)]

    def blkbufs(name, shape, n=NB):
        return [S(shape, F32, f"{name}{b}") for b in range(n)]
    sc = [[S([128, 1], F32, f"sc{b}_{i}") for i in range(12)] for b in range(NB)]
    Ug = blkbufs("Ug", [128, 128])
    Eb = blkbufs("Eb", [128, 128])
    ETb = blkbufs("ETb", [128, 128])
    qd = blkbufs("qd", [128, 128])
    bk = blkbufs("bk", [128, 128])
    kdec = blkbufs("kdec", [128, 128])
    bv = blkbufs("bv", [128, 128])
    M = blkbufs("M", [128, 256])
    X = blkbufs("X", [128, 128])
    attnT = blkbufs("attnT", [128, 128])
    un = blkbufs("un", [128, 128])
    wT = blkbufs("wT", [128, 128])
    ub = blkbufs("ub", [128, 128])
    sg = blkbufs("sg", [128, 128])
    yo = blkbufs("yo", [128, 128])
    junk = blkbufs("junk", [128, 128], 2)
    gjunk = blkbufs("gjunk", [128, 128], 2)
    gsc = [[S([64, 1], F32, f"gsc{b}_{i}") for i in range(2)] for b in range(NB)]
    gkdT = blkbufs("gkdT", [64, 128])
    gkd = blkbufs("gkd", [128, 64])
    gaT = blkbufs("gaT", [128, 128])
    gsg = blkbufs("gsg", [128, 128])
    gyo = blkbufs("gyo", [128, 128])
    gss = [[S([128, 1], F32, f"gss{b}_{i}") for i in range(2)] for b in range(NB)]
    ones64 = S([64, 128], F32, "ones64")
    O.memset("pool", V(ones64), 1.0)

    A = slice(None)

    def bsl(b):
        return slice(128 * b, 128 * b + 128)

    def out_norm(o_ps, ss_b, ng, sgate, ydst, jk):
        ssum, rs = ss_b
        O.memset("pool", V(ssum), 0.0)
        O.act(V(jk), o_ps, AF.Square, accum=V(ssum))
        O.act(V(rs), V(ssum), AF.Sqrt, bias=EPS, scale=1.0 / 128)
        O.recip(V(rs), V(rs))
        O.stt("dve", V(ydst), o_ps, V(rs), V(ng), ALU.mult, ALU.mult)
        O.tt("pool", V(ydst), V(ydst), V(sgate), ALU.mult)

    nsb = L // TB
    sdn_i = 0
    sgl_i = 0
    def gen_front(t0f):
        for k in range(8):
            O.dma(V(xb[k]), xsrc(k, t0f))
        ps = psp.get()
        for k in range(8):
            s = sq[k % 2]
            O.act(V(s), V(xb[k]), AF.Square)
            O.mm(V(ps), V(ones, A, slice(0, 128)), V(s), start=(k == 0), stop=(k == 7))
        O.act(V(rstd), V(ps), AF.Sqrt, bias=EPS, scale=1.0 / D)
        O.recip(V(rstd), V(rstd))
        for k in range(8):
            O.stt("dve", V(hb[k]), V(xb[k]), V(c_anorm, A, slice(k, k + 1)), V(rstd), ALU.mult, ALU.mult)
        yield
        for c in range(3):
            ps = psp.get()
            for k in range(8):
                O.mm(V(ps), V(wf, A, k, slice(c * 128, (c + 1) * 128)), V(hb[k]), start=(k == 0), stop=(k == 7))
            O.copy("pool", V(cbuf[c], A, slice(0, 3)), V(chist[c]))
            O.copy("act", V(cbuf[c], A, slice(3, TB + 3)), V(ps))
            O.copy("pool", V(chist[c]), V(cbuf[c], A, slice(TB, TB + 3)))
            yield
        ps = psp.get()
        for k in range(8):
            O.mm(V(ps), V(wf, A, k, slice(384, 512)), V(hb[k]), start=(k == 0), stop=(k == 7))
        O.copy("act", V(uT), V(ps))
        yield
        ps_gq = psp.get()
        for k in range(8):
            O.mm(V(ps_gq, slice(0, 64)), V(wf, A, k, slice(512, 576)), V(hb[k]), start=(k == 0), stop=(k == 7))
        O.copy("act", V(gqr), V(ps_gq, slice(0, 64)))
        ps_gk = psp.get()
        for k in range(8):
            O.mm(V(ps_gk, slice(0, 64)), V(wf, A, k, slice(576, 640)), V(hb[k]), start=(k == 0), stop=(k == 7))
        O.copy("act", V(gkr), V(ps_gk, slice(0, 64)))
        yield
        ps = psp.get()
        for k in range(8):
            O.mm(V(ps, slice(0, 16)), V(wf, A, k, slice(640, 656)), V(hb[k]), start=(k == 0), stop=(k == 7))
        O.copy("act", V(lrs_b), V(ps, slice(0, 16)))
        yield
        for b in range(NB):
            ps = psp.get()
            for k in range(8):
                O.mm(V(ps, A, slice(0, NWT)), V(hb[k], A, bsl(b)), V(wt, A, k), start=(k == 0), stop=(k == 7))
            O.copy("act", V(tok[b]), V(ps, A, slice(0, NWT)))
            yield
        for c in range(3):
            O.ts("dve", V(cacc[c]), V(cbuf[c], A, slice(0, TB)), V(dn_cw, A, slice(4 * c, 4 * c + 1)), ALU.mult)
            for t in range(1, 4):
                O.stt("dve", V(cacc[c]), V(cbuf[c], A, slice(t, t + TB)),
                      V(dn_cw, A, slice(4 * c + t, 4 * c + t + 1)), V(cacc[c]), ALU.mult, ALU.add)
            O.act(V(cacc[c]), V(cacc[c]), AF.Silu)
            yield
        for c, dst, scl in ((0, qn, 128 ** -0.5), (1, kn, 1.0)):
            s = sq[c]
            O.act(V(s), V(cacc[c]), AF.Square)
            ps = psp.get()
            O.mm(V(ps), V(ones, A, slice(0, 128)), V(s))
            O.act(V(s), V(ps), AF.Sqrt, bias=EPS, scale=1.0)
            O.recip(V(s), V(s))
            O.stt("dve", V(dst), V(cacc[c]), scl, V(s), ALU.mult, ALU.mult)
            yield

    for _ in gen_front(0):
        pass
    for sb_i in range(nsb):
        t0_ = sb_i * TB
        vs = cacc[2]

        def gen_s5():
            yps = yps_ded
            for j in range(4):
                pr = psp.get()
                pi = psp.get()
                O.mm(V(pr), V(BreT[j]), V(uT))
                O.mm(V(pi), V(BimT[j]), V(uT))
                w0, w1, w2, w3, w4, w5 = [V(s5w[i]) for i in range(6)]
                O.tt("dve", w0, V(pr), V(Ct[j]), ALU.mult)
                O.tt("dve", w1, V(pi), V(St[j]), ALU.mult)
                O.tt("pool", w0, w0, w1, ALU.add)
                O.tt("dve", w2, V(pi), V(Ct[j]), ALU.mult)
                O.tt("dve", w3, V(pr), V(St[j]), ALU.mult)
                O.tt("pool", w2, w2, w3, ALU.subtract)
                cr_, ci_ = V(s5c[j], A, slice(0, 1)), V(s5c[j], A, slice(1, 2))
                c5, s5_ = V(r512, A, slice(2 * j, 2 * j + 1)), V(r512, A, slice(2 * j + 1, 2 * j + 2))
                ir, ii, ta, tb_ = [V(s5i[j], A, slice(i, i + 1)) for i in range(4)]
                O.tt("dve", ta, ci_, s5_, ALU.mult)
                O.stt("dve", ir, cr_, c5, ta, ALU.mult, ALU.subtract)
                O.tt("dve", tb_, cr_, s5_, ALU.mult)
                O.stt("dve", ii, ci_, c5, tb_, ALU.mult, ALU.add)
                O.scan("dve", w4, V(Mg[j]), w0, ir)
                O.scan("dve", w5, V(Mg[j]), w2, ii)
                O.copy("pool", cr_, V(s5w[4], A, slice(TB - 1, TB)))
                O.copy("pool", ci_, V(s5w[5], A, slice(TB - 1, TB)))
                O.tt("dve", w0, w4, V(Ct[j]), ALU.mult)
                O.tt("pool", w1, w5, V(St[j]), ALU.mult)
                O.tt("dve", w0, w0, w1, ALU.subtract)
                O.tt("pool", w2, w5, V(Ct[j]), ALU.mult)
                O.tt("dve", w3, w4, V(St[j]), ALU.mult)
                O.tt("pool", w2, w2, w3, ALU.add)
                yield
                O.mm(V(yps), V(Cre[j]), w0, start=(j == 0), stop=False)
                O.mm(V(yps), V(Cim[j]), w2, start=False, stop=(j == 3))
                yield
            O.stt("dve", V(yc_sb), V(uT), V(s5d), V(yps), ALU.mult, ALU.add)
            O.act(V(ys[2]), V(yc_sb), AF.Gelu_apprx_tanh)

        def gen_gla():
            nonlocal sgl_i
            ps = psp.get()
            O.mm(V(ps, slice(0, 64)), V(gla_w2), V(lrs_b))
            O.act(V(gls), V(ps, slice(0, 64)), AF.Exp, scale=-1.0, bias=V(nb2))
            O.act(V(gls), V(gls), AF.Ln, bias=1.0)
            for b in range(NB):
                O.scan("dve", V(gc, A, bsl(b)), V(ones64), V(gls, A, bsl(b)), 0.0)
            O.act(V(gEQ), V(gc), AF.Exp, scale=-1.0 / 16, bias=math.log(1.0 / 8))
            O.act(V(gEK), V(gc), AF.Exp, scale=1.0 / 16)
            O.tt("dve", V(gqe), V(gqr), V(gEQ), ALU.mult)
            O.tt("dve", V(gke), V(gkr), V(gEK), ALU.mult)
            yield
            for b in range(NB):
                nbl, gend = V(gsc[b][0]), V(gsc[b][1])
                O.ts("dve", nbl, V(gc, A, slice(128 * b + 127, 128 * b + 128)), -1.0 / 16, ALU.mult)
                O.act(gend, nbl, AF.Exp)
                O.act(V(gkdT[b]), V(gc, A, bsl(b)), AF.Exp, scale=1.0 / 16, bias=nbl)
                O.tt("dve", V(gkdT[b]), V(gkdT[b]), V(gkr, A, bsl(b)), ALU.mult)
            yield
            for b in range(NB):
                pt = psp.get()
                O.tr(V(pt, A, slice(0, 64)), V(gkdT[b]), V(ident, slice(0, 64), slice(0, 64)))
                O.copy("act", V(gkd[b]), V(pt, A, slice(0, 64)))
                pa = psp.get()
                O.mm(V(pa, A, slice(0, 128)), V(gke, A, bsl(b)), V(gqe, A, bsl(b)))
                O.tt("dve", V(gaT[b]), V(pa, A, slice(0, 128)), V(U), ALU.mult)
                O.act(V(gsg[b]), V(tok[b], A, slice(258, 386)), AF.Silu)
                yield
            for b in range(NB):
                gv = V(tok[b], A, slice(130, 258))
                So, Sn = Sgl[sgl_i % 2], Sgl[(sgl_i + 1) % 2]
                sgl_i += 1
                po = psp.get()
                O.mm(V(po, A, slice(0, 128)), V(gqe, A, bsl(b)), V(So), start=True, stop=False)
                O.mm(V(po, A, slice(0, 128)), V(gaT[b]), gv, start=False, stop=True)
                pd = psp.get()
                O.mm(V(pd, slice(0, 64), slice(0, 128)), V(gkd[b]), gv)
                O.stt("dve", V(Sn), V(So), V(gsc[b][1]), V(pd, slice(0, 64), slice(0, 128)), ALU.mult, ALU.add)
                out_norm(V(po, A, slice(0, 128)), gss[b], gla_ng, gsg[b], gyo[b], gjunk[b % 2])
                yield
                ptr = psp.get()
                O.tr(V(ptr, A, slice(0, 128)), V(gyo[b]), V(ident))
                O.copy("act", V(ys[1], A, bsl(b)), V(ptr, A, slice(0, 128)))
                yield

        def gen_dn_pre(b):
            beta, glog, gam, glast, ngam, egam, bg, edl, gend, tmp = [V(sc[b][i]) for i in range(10)]
            O.act(beta, V(tok[b], A, slice(128, 129)), AF.Sigmoid)
            O.act(tmp, V(tok[b], A, slice(129, 130)), AF.Exp, bias=V(dn_sc, A, slice(1, 2)))
            O.act(tmp, tmp, AF.Ln, bias=1.0)
            O.tt("dve", glog, tmp, V(negA), ALU.mult)
            O.ts("dve", V(Ug[b]), V(U), glog, ALU.mult)
            yield
            pA = psp.get()
            O.mm(V(pA, A, slice(0, 128)), V(ones, A, slice(0, 128)), V(Ug[b]))
            O.mm(V(pA, A, slice(128, 129)), V(U), glog)
            O.mm(V(pA, A, slice(129, 130)), V(ones, A, slice(0, 128)), glog)
            O.copy("dve", gam, V(pA, A, slice(128, 129)))
            O.copy("dve", glast, V(pA, A, slice(129, 130)))
            O.ts("dve", ngam, gam, -1.0, ALU.mult)
            O.act(egam, gam, AF.Exp)
            O.tt("dve", bg, beta, egam, ALU.mult)
            O.act(edl, gam, AF.Exp, scale=-1.0, bias=glast)
            O.act(gend, glast, AF.Exp)
            gbc = V(pA, A, slice(0, 128))
            O.stt("dve", V(Eb[b]), gbc, -1.0, V(mneg), ALU.mult, ALU.add)
            O.act(V(Eb[b]), V(Eb[b]), AF.Exp, bias=gam)
            O.tt("dve", V(ETb[b]), gbc, V(mnegT), ALU.add)
            O.act(V(qd[b]), gbc, AF.Exp)
            yield
            O.act(V(ETb[b]), V(ETb[b]), AF.Exp, bias=ngam)
            O.tt("dve", V(qd[b]), V(qd[b]), V(qn, A, bsl(b)), ALU.mult)
            O.act(V(sg[b]), V(tok[b], A, slice(0, 128)), AF.Silu)
            yield
            pt = psp.get()
            O.tr(V(pt, A, slice(0, 128)), V(kn, A, bsl(b)), V(ident))
            O.tr(V(pt, A, slice(128, 256)), V(vs, A, bsl(b)), V(ident))
            O.ts("dve", V(bk[b]), V(pt, A, slice(0, 128)), bg, ALU.mult)
            O.act(V(kdec[b]), V(pt, A, slice(0, 128)), AF.Identity, scale=edl)
            O.ts("dve", V(bv[b]), V(pt, A, slice(128, 256)), beta, ALU.mult)
            yield
            pk = psp.get()
            O.mm(V(pk, A, slice(0, 128)), V(kn, A, bsl(b)), V(kn, A, bsl(b)))
            O.mm(V(pk, A, slice(128, 256)), V(kn, A, bsl(b)), V(qn, A, bsl(b)))
            O.stt("dve", V(M[b], A, slice(0, 128)), V(pk, A, slice(0, 128)), beta, V(Eb[b]), ALU.mult, ALU.mult)
            O.tt("pool", V(M[b], A, slice(0, 128)), V(M[b], A, slice(0, 128)), V(nLs), ALU.mult)
            O.tt("dve", V(attnT[b]), V(pk, A, slice(128, 256)), V(ETb[b]), ALU.mult)
            blk_flag[b] = True
            yield
            pt = psp.get()
            O.tr(V(pt, A, slice(0, 128)), V(M[b], A, slice(0, 128)), V(ident))
            O.copy("act", V(M[b], A, slice(128, 256)), V(pt, A, slice(0, 128)))
            O.tt("dve", V(X[b]), V(pt, A, slice(0, 128)), V(ident), ALU.add)
            yield
            for lev in range(1, 8):
                pl = psp.get()
                Mv, MTv = V(M[b], A, slice(0, 128)), V(M[b], A, slice(128, 256))
                if lev <= 6:
                    O.mm(V(pl, A, slice(0, 128)), MTv, Mv)
                    O.mm(V(pl, A, slice(128, 256)), Mv, MTv)
                if lev >= 2:
                    O.mm(V(pl, A, slice(256, 384)), Mv, V(X[b]))
                if lev <= 6:
                    O.copy("act", V(M[b]), V(pl, A, slice(0, 256)))
                if lev >= 2:
                    O.tt("dve", V(X[b]), V(X[b]), V(pl, A, slice(256, 384)), ALU.add)
                yield
            pe_ = psp.get()
            O.mm(V(pe_, A, slice(0, 128)), V(X[b]), V(bv[b]))
            O.mm(V(pe_, A, slice(128, 256)), V(bk[b]), V(X[b]))
            O.copy("act", V(un[b]), V(pe_, A, slice(0, 128)))
            O.copy("act", V(wT[b]), V(pe_, A, slice(128, 256)))
            yield

        def gen_dn_seq():
            nonlocal sdn_i
            for b in range(NB):
                So, Sn = Sdn[sdn_i % 2], Sdn[(sdn_i + 1) % 2]
                sdn_i += 1
                pw = psp.get()
                O.mm(V(pw, A, slice(0, 128)), V(wT[b]), V(So))
                O.tt("dve", V(ub[b]), V(un[b]), V(pw, A, slice(0, 128)), ALU.subtract)
                yield
                po = psp.get()
                O.mm(V(po, A, slice(0, 128)), V(qd[b]), V(So), start=True, stop=False)
                O.mm(V(po, A, slice(0, 128)), V(attnT[b]), V(ub[b]), start=False, stop=True)
                pd = psp.get()
                O.mm(V(pd, A, slice(0, 128)), V(kdec[b]), V(ub[b]))
                O.stt("dve", V(Sn), V(So), V(sc[b][8]), V(pd, A, slice(0, 128)), ALU.mult, ALU.add)
                out_norm(V(po, A, slice(0, 128)), (sc[b][10], sc[b][11]), dn_ng, sg[b], yo[b], junk[b % 2])
                yield
                ptr = psp.get()
                O.tr(V(ptr, A, slice(0, 128)), V(yo[b]), V(ident))
                O.copy("act", V(ys[0], A, bsl(b)), V(ptr, A, slice(0, 128)))
                yield

        blk_flag = [False] * NB
        g_pre = [gen_dn_pre(b_) for b_ in range(NB)]
        g_s5, g_gla = gen_s5(), gen_gla()
        g_seq = None
        tasks = g_pre + [g_s5, g_gla]
        nf = gen_front((sb_i + 1) * TB) if sb_i + 1 < nsb else None
        started = False
        rnd = 0
        while tasks:
            if g_seq is None and not any(g in tasks for g in g_pre):
                g_seq = gen_dn_seq()
                tasks.insert(0, g_seq)
            if nf is not None and not started and all(blk_flag) and g_s5 not in tasks and g_gla not in tasks:
                tasks.append(nf)
                started = True
            rnd += 1
            for g in list(tasks):
                if g not in tasks:
                    continue
                if g is g_s5 and rnd % 2 == 0:
                    continue
                try:
                    next(g)
                except StopIteration:
                    while g in tasks:
                        tasks.remove(g)
            if not tasks and g_seq is None:
                g_seq = gen_dn_seq()
                tasks.append(g_seq)
        if nf is not None and not started:
            for _ in nf:
                pass
        for i3 in range(3):
            yout(i3, t0_, ys[i3])
        after_sb(t0_)


def emit_dense(nc, P, TOK, last, sfx, src, NT=512):
    def din(name, shape):
        return nc.dram_tensor(name + sfx, list(shape), F32, kind="ExternalInput").ap()
    HT = TOK + 2
    pT = din("pT", [256, TOK])
    w_gate = din("w_gate", [D, 3 * D])
    b_gate = din("b_gate", [128, 24])
    w_branch = din("w_branch", [1536, D])
    w_o = din("w_o", [D, D])
    w_glu = din("w_glu", [512, 512])
    b_glu = din("b_glu", [128, 4])
    norms = din("norms", [128, 32])
    w_up = din("w_up", [D, 2 * DFF])
    cw = din("cw", [128, NF * 3])
    cb = din("cb", [128, NF])
    w_down = din("w_down", [DFF, D])
    w_pg = din("w_pg", [D, D])
    w_pp = din("w_pp", [256, D])
    psp = PsPool(P)
    ones = P.sbuf([128, 128], F32, "ones")
    P.op("pool", lambda e: e.memset(ones.ap, 1.0), writes=[ones])
    c_bg = P.sbuf([128, 24], F32, "c_bg")
    c_bglu = P.sbuf([128, 4], F32, "c_bglu")
    c_norm = P.sbuf([128, 32], F32, "c_norm")
    c_cw = P.sbuf([128, NF * 3], F32, "c_cw")
    c_cb = P.sbuf([128, NF], F32, "c_cb")
    hmask = P.sbuf([128, 1], F32, "hmask")
    for t, srcd in ((c_bg, b_gate), (c_bglu, b_glu), (c_norm, norms), (c_cw, cw), (c_cb, cb), (hmask, src["hmask"])):
        P.dma(t.ap, srcd, writes=[t])

    NSLAB = 5
    slab_t = [P.sb([128, 8, 1024], BF16, f"slab{i}") for i in range(NSLAB)]
    slabs = [Buf(t[:], f"slab{i}") for i, t in enumerate(slab_t)]
    slab_i = [0]

    def load_w(src, r0, nk, c0, ncol):
        s = slabs[slab_i[0] % NSLAB]
        slab_i[0] += 1
        view = s.ap[:, 0:nk, 0:ncol]
        srcv = src[r0:r0 + nk * 128, c0:c0 + ncol].rearrange("(k p) n -> p k n", p=128)
        P.dma(view, srcv, writes=[s], q="pool")
        return s

    x_t = P.sb([128, 8, NT], F32, "x_t")
    xb = [Buf(x_t[:, k, :], f"x{k}") for k in range(8)]
    h_t = P.sb([128, 8, NT], BF16, "h_t")
    hb = [Buf(h_t[:, k, :], f"h{k}") for k in range(8)]
    y_t = P.sb([128, 12, NT], BF16, "y_t")
    yb = [Buf(y_t[:, k, :], f"y{k}") for k in range(12)]
    yc_t = P.sb([128, 4, NT], BF16, "yc_t")
    ycb = [Buf(yc_t[:, k, :], f"yc{k}") for k in range(4)]
    m_t = P.sb([128, 8, NT], F32, "m_t")
    mb = [Buf(m_t[:, k, :], f"m{k}") for k in range(8)]
    mbf_t = P.sb([128, 8, NT], BF16, "mbf_t")
    mbfb = [Buf(mbf_t[:, k, :], f"mbf{k}") for k in range(8)]
    a_t = P.sb([128, NF, NT], BF16, "a_t")
    ab = [Buf(a_t[:, k, :], f"a{k}") for k in range(NF)]
    p_t = P.sb([128, 2, NT], BF16, "p_t")
    pb = [Buf(p_t[:, k, :], f"p{k}") for k in range(2)]
    hist_t = P.sb([128, NF, 2], F32, "hist_t")
    histb = [Buf(hist_t[:, k, :], f"hist{k}") for k in range(NF)]
    sq_t = P.sb([128, 2, NT], F32, "sq_t")
    sqb = [Buf(sq_t[:, k, :], f"sq{k}") for k in range(2)]
    rstd = P.sbuf([128, NT], F32, "rstd")
    NG = 3
    g_t = P.sb([128, NG, NT], F32, "g_t")
    gtb = [Buf(g_t[:, k, :], f"gt{k}") for k in range(NG)]
    gi = [0]
    gb_t = P.sb([128, 2, NT + 2], F32, "gb_t")
    gbb = [Buf(gb_t[:, k, :], f"gb{k}") for k in range(2)]
    cv_t = P.sb([128, 2, NT], F32, "cv_t")
    cvb = [Buf(cv_t[:, k, :], f"cv{k}") for k in range(2)]
    ob = mb

    def rmsnorm(w, ncol, outs):
        ps = psp.get()
        for k in range(8):
            s = sqb[k % 2]
            P.op("act", lambda e, s=s, k=k: e.activation(s.ap[:, :w], xb[k].ap[:, :w], AF.Square),
                 reads=[xb[k]], writes=[s])
            P.mm(ps.ap[:, :w], ones.ap, s.ap[:, :w], start=(k == 0), stop=(k == 7),
                 reads=[ones, s], writes=[ps])
        P.op("act", lambda e: e.activation(rstd.ap[:, :w], ps.ap[:, :w], AF.Sqrt, bias=EPS, scale=1.0 / D),
             reads=[ps], writes=[rstd])
        P.op("dve", lambda e: e.reciprocal(rstd.ap[:, :w], rstd.ap[:, :w]), reads=[rstd], writes=[rstd])
        for k in range(8):
            P.op("dve", lambda e, k=k: e.scalar_tensor_tensor(
                outs[k].ap[:, :w], xb[k].ap[:, :w], c_norm.ap[:, ncol + k:ncol + k + 1], rstd.ap[:, :w],
                op0=ALU.mult, op1=ALU.mult), reads=[xb[k], c_norm, rstd], writes=[outs[k]])

    def tile(c0, w, halo):
        for k in range(8):
            src["x"](P, xb[k], k, c0, w, halo)
        for k in range(12):
            src["y"](P, yb[k], k, c0, w, halo)
        if halo:
            for k in range(8):
                P.op("dve", lambda e, k=k: e.tensor_scalar(xb[k].ap[:, :w], xb[k].ap[:, :w], hmask.ap[:, 0:1], None,
                                                           op0=ALU.mult), reads=[xb[k], hmask], writes=[xb[k]])
            for k in range(12):
                P.op("dve", lambda e, k=k: e.tensor_scalar(yb[k].ap[:, :w], yb[k].ap[:, :w], hmask.ap[:, 0:1], None,
                                                           op0=ALU.mult), reads=[yb[k], hmask], writes=[yb[k]])
        if not halo:
            for k in range(2):
                P.dma(pb[k].ap[:, :w], pT[k * 128:(k + 1) * 128, c0 - 2:c0 - 2 + w], writes=[pb[k]], q="pool")
        rmsnorm(w, 0, hb)
        U = load_w(w_glu, 0, 4, 0, 512)
        for m in range(4):
            ps = psp.get()
            for k in range(4):
                P.mm(ps.ap[:, :w], U.ap[:, k, m * 128:(m + 1) * 128], yb[8 + k].ap[:, :w],
                     start=(k == 0), stop=(k == 3), reads=[U, yb[8 + k]], writes=[ps])
            g = gtb[gi[0] % NG]; gi[0] += 1
            P.op("act", lambda e, g=g, ps=ps, m=m: e.activation(g.ap[:, :w], ps.ap[:, :w], AF.Sigmoid,
                                                             bias=c_bglu.ap[:, m:m + 1]),
                 reads=[ps, c_bglu], writes=[g])
            P.op("dve", lambda e, g=g, m=m: e.tensor_tensor(ycb[m].ap[:, :w], yb[8 + m].ap[:, :w], g.ap[:, :w],
                                                          op=ALU.mult),
                 reads=[yb[8 + m], g], writes=[ycb[m]])
        for i in range(3):
            G = load_w(w_gate, 0, 8, i * D, D)
            B = load_w(w_branch, i * 512, 4, 0, D)
            ysrc = [yb[0], yb[1], yb[2], yb[3]] if i == 0 else ([yb[4], yb[5], yb[6], yb[7]] if i == 1 else ycb)
            for m in range(8):
                pg = psp.get()
                for k in range(8):
                    P.mm(pg.ap[:, :w], G.ap[:, k, m * 128:(m + 1) * 128], hb[k].ap[:, :w],
                         start=(k == 0), stop=(k == 7), reads=[G, hb[k]], writes=[pg])
                g = gtb[gi[0] % NG]; gi[0] += 1
                P.op("act", lambda e, g=g, pg=pg, i=i, m=m: e.activation(
                    g.ap[:, :w], pg.ap[:, :w], AF.Sigmoid, bias=c_bg.ap[:, i * 8 + m:i * 8 + m + 1]),
                    reads=[pg, c_bg], writes=[g])
                pp = psp.get()
                for k in range(4):
                    P.mm(pp.ap[:, :w], B.ap[:, k, m * 128:(m + 1) * 128], ysrc[k].ap[:, :w],
                         start=(k == 0), stop=(k == 3), reads=[B, ysrc[k]], writes=[pp])
                if i == 0:
                    P.op("dve", lambda e, g=g, pp=pp, m=m: e.tensor_tensor(
                        mb[m].ap[:, :w], pp.ap[:, :w], g.ap[:, :w], op=ALU.mult),
                        reads=[pp, g], writes=[mb[m]])
                else:
                    P.op("dve", lambda e, g=g, pp=pp, m=m: e.tensor_tensor(
                        g.ap[:, :w], pp.ap[:, :w], g.ap[:, :w], op=ALU.mult),
                        reads=[pp, g], writes=[g])
                    dst = mb[m] if i == 1 else mbfb[m]
                    P.op("pool", lambda e, g=g, m=m, dst=dst: e.tensor_tensor(
                        dst.ap[:, :w], mb[m].ap[:, :w], g.ap[:, :w], op=ALU.add),
                        reads=[mb[m], g], writes=[dst])
        O = load_w(w_o, 0, 8, 0, D)
        for m in range(8):
            ps = psp.get()
            for k in range(8):
                P.mm(ps.ap[:, :w], O.ap[:, k, m * 128:(m + 1) * 128], mbfb[k].ap[:, :w],
                     start=(k == 0), stop=(k == 7), reads=[O, mbfb[k]], writes=[ps])
            P.op("dve", lambda e, ps=ps, m=m: e.tensor_tensor(xb[m].ap[:, :w], xb[m].ap[:, :w], ps.ap[:, :w],
                                                            op=ALU.add),
                 reads=[xb[m], ps], writes=[xb[m]])
        rmsnorm(w, 8, hb)
        for j0 in range(0, NF, 8):
            nj = min(8, NF - j0)
            Wg = load_w(w_up, 0, 8, j0 * 128, nj * 128)
            Wu = None if halo else load_w(w_up, 0, 8, DFF + j0 * 128, nj * 128)
            for jj in range(nj):
                j = j0 + jj
                pg = psp.get()
                for k in range(8):
                    P.mm(pg.ap[:, :w], Wg.ap[:, k, jj * 128:(jj + 1) * 128], hb[k].ap[:, :w],
                         start=(k == 0), stop=(k == 7), reads=[Wg, hb[k]], writes=[pg])
                if halo:
                    P.op("act", lambda e, pg=pg, j=j: e.activation(histb[j].ap, pg.ap[:, 0:2], AF.Identity),
                         reads=[pg], writes=[histb[j]])
                    continue
                pu = psp.get()
                for k in range(8):
                    P.mm(pu.ap[:, :w], Wu.ap[:, k, jj * 128:(jj + 1) * 128], hb[k].ap[:, :w],
                         start=(k == 0), stop=(k == 7), reads=[Wu, hb[k]], writes=[pu])
                gb = gbb[j % 2]
                cv = cvb[j % 2]
                P.op("pool", lambda e, gb=gb, j=j: e.tensor_copy(gb.ap[:, 0:2], histb[j].ap),
                     reads=[histb[j]], writes=[gb])
                P.op("act", lambda e, gb=gb, pg=pg: e.activation(gb.ap[:, 2:2 + w], pg.ap[:, :w], AF.Identity),
                     reads=[pg], writes=[gb])
                P.op("pool", lambda e, gb=gb, j=j: e.tensor_copy(histb[j].ap, gb.ap[:, w:w + 2]),
                     reads=[gb], writes=[histb[j]])
                P.op("dve", lambda e, gb=gb, cv=cv, j=j: e.tensor_scalar(
                    cv.ap[:, :w], gb.ap[:, 0:w], c_cw.ap[:, 3 * j:3 * j + 1], c_cb.ap[:, j:j + 1],
                    op0=ALU.mult, op1=ALU.add), reads=[gb, c_cw, c_cb], writes=[cv])
                for t in (1, 2):
                    P.op("dve", lambda e, gb=gb, cv=cv, j=j, t=t: e.scalar_tensor_tensor(
                        cv.ap[:, :w], gb.ap[:, t:t + w], c_cw.ap[:, 3 * j + t:3 * j + t + 1], cv.ap[:, :w],
                        op0=ALU.mult, op1=ALU.add), reads=[gb, c_cw, cv], writes=[cv])
                P.op("act", lambda e, cv=cv: e.activation(cv.ap[:, :w], cv.ap[:, :w], AF.Gelu_apprx_tanh),
                     reads=[cv], writes=[cv])
                P.op("dve", lambda e, cv=cv, pu=pu, j=j: e.tensor_tensor(
                    ab[j].ap[:, :w], pu.ap[:, :w], cv.ap[:, :w], op=ALU.mult),
                    reads=[pu, cv], writes=[ab[j]])
        if halo:
            return
        Ds = [load_w(w_down, j0 * 128, min(8, NF - j0), 0, D) for j0 in range(0, NF, 8)]
        for m in range(8):
            ps = psp.get()
            for j in range(NF):
                Dj = Ds[j // 8]
                P.mm(ps.ap[:, :w], Dj.ap[:, j % 8, m * 128:(m + 1) * 128], ab[j].ap[:, :w],
                     start=(j == 0), stop=(j == NF - 1), reads=[Dj, ab[j]], writes=[ps])
            P.op("dve", lambda e, ps=ps, m=m: e.tensor_tensor(xb[m].ap[:, :w], xb[m].ap[:, :w], ps.ap[:, :w],
                                                            op=ALU.add),
                 reads=[xb[m], ps], writes=[xb[m]])
        rmsnorm(w, 16, hb)
        PG = load_w(w_pg, 0, 8, 0, D)
        PP = load_w(w_pp, 0, 2, 0, D)
        for m in range(8):
            pg = psp.get()
            for k in range(8):
                P.mm(pg.ap[:, :w], PG.ap[:, k, m * 128:(m + 1) * 128], hb[k].ap[:, :w],
                     start=(k == 0), stop=(k == 7), reads=[PG, hb[k]], writes=[pg])
            g = gtb[gi[0] % NG]; gi[0] += 1
            P.op("act", lambda e, g=g, pg=pg: e.activation(g.ap[:, :w], pg.ap[:, :w], AF.Sigmoid),
                 reads=[pg], writes=[g])
            pp = psp.get()
            for k in range(2):
                P.mm(pp.ap[:, :w], PP.ap[:, k, m * 128:(m + 1) * 128], pb[k].ap[:, :w],
                     start=(k == 0), stop=(k == 1), reads=[PP, pb[k]], writes=[pp])
            P.op("dve", lambda e, g=g, pp=pp: e.tensor_tensor(g.ap[:, :w], pp.ap[:, :w], g.ap[:, :w], op=ALU.mult),
                 reads=[pp, g], writes=[g])
            P.op("pool", lambda e, g=g, m=m: e.tensor_tensor(xb[m].ap[:, :w], xb[m].ap[:, :w], g.ap[:, :w],
                                                           op=ALU.add),
                 reads=[xb[m], g], writes=[xb[m]])
        if last:
            rmsnorm(w, 24, ob)
            src_o = ob
        else:
            src_o = xb
        for m in range(8):
            src["out"](P, src_o[m], m, c0 - 2, w)
        src["after_tile"](P, c0 - 2)

    tile(0, 2, True)
    for t0 in range(0, TOK, NT):
        tile(2 + t0, min(NT, TOK - t0), False)


IN_OFF = dict(q=0, k=512, v=1024, b=1536, a=1540, g=1544, gq=2056, gk=2312, gv=2568, lr=3080, gr=3096, s5=3608)


def mixer_inputs(layer, hd, x_b, W):
    i = layer
    w_in = W["w_in"][i]
    o = IN_OFF
    cols_f = np.concatenate([
        np.arange(o["q"] + hd * 128, o["q"] + hd * 128 + 128),
        np.arange(o["k"] + hd * 128, o["k"] + hd * 128 + 128),
        np.arange(o["v"] + hd * 128, o["v"] + hd * 128 + 128),
        np.arange(o["s5"] + hd * 128, o["s5"] + hd * 128 + 128),
        np.arange(o["gq"] + hd * 64, o["gq"] + hd * 64 + 64),
        np.arange(o["gk"] + hd * 64, o["gk"] + hd * 64 + 64),
        np.arange(o["lr"], o["lr"] + 16)])
    cols_t = np.concatenate([
        np.arange(o["g"] + hd * 128, o["g"] + hd * 128 + 128),
        [o["b"] + hd], [o["a"] + hd],
        np.arange(o["gv"] + hd * 128, o["gv"] + hd * 128 + 128),
        np.arange(o["gr"] + hd * 128, o["gr"] + hd * 128 + 128)])
    cwv = W["dn_conv_w"][i]
    dn_cw = np.stack([cwv[:, c * 512 + hd * 128: c * 512 + hd * 128 + 128].T for c in range(3)], axis=1)
    rep = lambda v: np.ascontiguousarray(np.broadcast_to(v[None, :], (128, v.shape[0])))
    g0 = hd * 8
    lam = np.zeros((128, 12), np.float32)
    sb_ = np.zeros((2, 4, 128, 16), np.float32)
    sc_ = np.zeros((2, 4, 128, 16), np.float32)
    for j in range(4):
        for hf in range(2):
            g = g0 + 2 * j + hf
            lam[64 * hf:64 * hf + 64, j] = W["s5_lam_re"][i][g]
            lam[64 * hf:64 * hf + 64, 4 + j] = W["s5_lam_im"][i][g]
            lam[64 * hf:64 * hf + 64, 8 + j] = W["s5_log_step"][i][g]
            sb_[0, j, 64 * hf:64 * hf + 64] = W["s5_b_re"][i][g]
            sb_[1, j, 64 * hf:64 * hf + 64] = W["s5_b_im"][i][g]
            sc_[0, j, 64 * hf:64 * hf + 64] = W["s5_c_re"][i][g].T
            sc_[1, j, 64 * hf:64 * hf + 64] = W["s5_c_im"][i][g].T
    return {
        "anorm": np.ascontiguousarray(W["attn_norm"][i].reshape(8, 128).T),
        "wf": np.ascontiguousarray(w_in[:, cols_f]), "wt": np.ascontiguousarray(w_in[:, cols_t]),
        "dn_sc": np.ascontiguousarray(np.stack([np.full(128, W["dn_a_log"][i][hd], np.float32),
                                               np.full(128, W["dn_dt_bias"][i][hd], np.float32)], axis=1)),
        "dn_cw": np.ascontiguousarray(dn_cw.reshape(128, 12)),
        "dn_ng": rep(W["dn_norm"][i]), "gla_ng": rep(W["gla_norm"][i]),
        "gla_w2": np.ascontiguousarray(W["gla_w2"][i][:, hd * 64:hd * 64 + 64]),
        "gla_b2": np.ascontiguousarray(W["gla_b2"][i][hd * 64:hd * 64 + 64].reshape(64, 1)),
        "s5_lam": lam, "s5_b": sb_, "s5_c": sc_,
        "s5_d": np.ascontiguousarray(W["s5_d"][i][hd * 128:hd * 128 + 128].reshape(128, 1)),
    }


def dense_inputs(layer, x_seg, y_seg, p_seg, W):
    i = layer
    f = np.float32
    col = lambda v: np.ascontiguousarray(v.reshape(-1, 128).T)
    norms = np.concatenate([col(W["attn_norm"][i]), col(W["ffn_norm"][i]), col(W["ple_norm"][i]),
                            col(W["final_norm"])], axis=1)
    cw = np.ascontiguousarray(W["ffn_conv_w"][i].reshape(3, NF, 128).transpose(2, 1, 0).reshape(128, NF * 3))
    return {
        "pT": np.ascontiguousarray(p_seg.T),
        "w_gate": W["w_gate"][i], "b_gate": col(W["b_gate"][i]),
        "w_branch": np.ascontiguousarray(W["w_branch"][i].reshape(1536, D)),
        "w_o": W["w_o"][i], "w_glu": W["s5_w_glu"][i], "b_glu": col(W["s5_b_glu"][i]),
        "norms": np.ascontiguousarray(norms), "w_up": W["w_up"][i], "cw": cw, "cb": col(W["ffn_conv_b"][i]),
        "w_down": W["w_down"][i], "w_pg": W["w_ple_gate"][i], "w_pp": W["w_ple_proj"][i],
    }


import os
NOCOLL = os.environ.get('FUSE_NOCOLL') == '1'
NODYN = os.environ.get('FUSE_NODYN') == '1'


def build_fused(L=SEQ):
    nc = bass.Bass("TRN2", target_bir_lowering=False, num_devices=NCORES)
    TOK = L // 4
    CH = 1024
    NCH = L // CH
    CPS = TOK // CH
    NT8 = TOK // 512
    assert CPS >= 1 and TOK % 512 == 0
    xT_b = nc.dram_tensor("xT_b", [D, L], F32, kind="ExternalInput").ap()
    xs0 = nc.dram_tensor("xs0", [D, TOK + 2], F32, kind="ExternalInput").ap()
    hmask_d = nc.dram_tensor("hmask", [128, 1], F32, kind="ExternalInput").ap()
    oT = nc.dram_tensor("oT", [D, TOK], F32, kind="ExternalOutput").ap()
    yloc = [[nc.dram_tensor(f"yloc{l}_{c}", [384, CH], BF16).ap() for c in range(NCH)] for l in range(2)]
    yall_t = [nc.dram_tensor(f"yall{l}", [NCH * 1536, CH], BF16).ap() for l in range(2)]
    x1c = [[nc.dram_tensor(f"x1c_{t}_{h}", [512, 512], F32).ap() for h in range(2)] for t in range(NT8)]
    xgc = [[nc.dram_tensor(f"xgc_{t}_{h}", [4 * 512, 512], F32).ap() for h in range(2)] for t in range(NT8)]
    yseg = nc.dram_tensor("yseg", [4 * 384, TOK + 2], BF16).ap()
    xh = nc.dram_tensor("xh", [D, 2], F32).ap()
    Byloc = [[Buf(yloc[l][c]) for c in range(NCH)] for l in range(2)]
    Byall = [[Buf(yall_t[l][c * 1536:(c + 1) * 1536, :]) for c in range(NCH)] for l in range(2)]
    Bx1c = [[Buf(x1c[t][h]) for h in range(2)] for t in range(NT8)]
    Bxgc = [[Buf(xgc[t][h]) for h in range(2)] for t in range(NT8)]
    Byseg, Bxh = Buf(yseg, "yseg"), Buf(xh, "xh")
    rv = {}
    P = Prog(nc)
    O = Ops(P)

    for layer in (0, 1):
        sfx = f"_{layer}"
        P.prefix = f"m{layer}_"
        P.begin_phase()
        if layer == 0:
            def xsrc(k, t0):
                return xT_b[k * 128:(k + 1) * 128, t0:t0 + TB]
        else:
            def xsrc(k, t0):
                sg_, tt = t0 // TOK, (t0 % TOK) // 512
                r0 = sg_ * 512 + (k % 4) * 128
                return View(Bxgc[tt][k // 4], xgc[tt][k // 4][r0:r0 + 128, :])

        def yout(i3, t0, ysb, layer=layer):
            ci, off = t0 // CH, t0 % CH
            P.dma(yloc[layer][ci][i3 * 128:(i3 + 1) * 128, off:off + TB], ysb.ap, reads=[ysb], writes=[Byloc[layer][ci]])

        def after_sb(t0, layer=layer):
            ci, off = t0 // CH, t0 % CH
            if off + TB == CH and not NOCOLL:
                P.coll("AllGather", yloc[layer][ci], yall_t[layer][ci * 1536:(ci + 1) * 1536, :], GROUPS,
                       reads=[Byloc[layer][ci]], writes=[Byall[layer][ci]])
        emit_mixer(nc, P, O, L, sfx, xsrc, yout, after_sb)
        P.end_phase()
        P.prefix = f"d{layer}_"
        P.begin_phase()
        if layer == 0:
            def setup(e, rv=rv):
                r = e.snap(e.partition_id() % 4, min_val=0, max_val=3)
                rv["yrow"] = e.snap(r * (CPS * 1536), min_val=0, max_val=3 * CPS * 1536)
                rv["hrow"] = e.snap(((r * CPS + (NCH - 1)) % NCH) * 1536, min_val=0, max_val=(NCH - 1) * 1536)
                rv["prow"] = e.snap(((r + 3) % 4) * 512, min_val=0, max_val=3 * 512)
                return None
            P.items["sp"].append(([], setup, None, 0))
        ya = yall_t[layer]
        for j in range(CPS):
            P.dma(yseg[:, 2 + j * CH:2 + (j + 1) * CH],
                  (lambda e, j=j, ya=ya: ya[(slice(j * 1536, (j + 1) * 1536) if NODYN else bass.ds(rv["yrow"] + j * 1536, 1536)), :]),
                  reads=Byall[layer], writes=[Byseg])
        P.dma(yseg[:, 0:2], (lambda e, ya=ya: ya[(slice(0, 1536) if NODYN else bass.ds(rv["hrow"], 1536)), CH - 2:CH]),
              reads=Byall[layer], writes=[Byseg])
        if layer == 1:
            for h in range(2):
                P.dma(xh[h * 512:(h + 1) * 512, :],
                      (lambda e, h=h: xgc[NT8 - 1][h][(slice(0, 512) if NODYN else bass.ds(rv["prow"], 512)), 510:512]),
                      reads=[Bxgc[NT8 - 1][h]], writes=[Bxh])

        def fx(P_, buf, k, c0, w, halo, layer=layer):
            if layer == 0:
                P_.dma(buf.ap[:, :w], xs0[k * 128:(k + 1) * 128, c0:c0 + w], writes=[buf])
            elif halo:
                P_.dma(buf.ap[:, :2], xh[k * 128:(k + 1) * 128, :], reads=[Bxh], writes=[buf])
            else:
                tt = (c0 - 2) // 512
                P_.dma(buf.ap[:, :w], x1c[tt][k // 4][(k % 4) * 128:(k % 4) * 128 + 128, 0:w],
                       reads=[Bx1c[tt][k // 4]], writes=[buf])

        def fy(P_, buf, k, c0, w, halo):
            row0 = (k % 4) * 384 + (k // 4) * 128
            P_.dma(buf.ap[:, :w], yseg[row0:row0 + 128, c0:c0 + w], reads=[Byseg], writes=[buf])

        def fo(P_, buf, m_, t, w, layer=layer):
            if layer == 0:
                tt = t // 512
                P_.dma(x1c[tt][m_ // 4][(m_ % 4) * 128:(m_ % 4) * 128 + 128, 0:w], buf.ap[:, :w],
                       reads=[buf], writes=[Bx1c[tt][m_ // 4]])
            else:
                P_.dma(oT[m_ * 128:(m_ + 1) * 128, t:t + w], buf.ap[:, :w], reads=[buf])

        def after_tile(P_, t, layer=layer):
            if layer == 0 and not NOCOLL:
                tt = t // 512
                for h in range(2):
                    P_.coll("AllGather", x1c[tt][h], xgc[tt][h], GROUPS, reads=[Bx1c[tt][h]], writes=[Bxgc[tt][h]])

        emit_dense(nc, P, TOK, layer == 1, sfx, dict(hmask=hmask_d, x=fx, y=fy, out=fo, after_tile=after_tile))
        P.end_phase(final=(layer == 1))
    P.close()
    return nc


def kernel(**inputs):
    W = {k: np.asarray(v, dtype=np.float32) for k, v in inputs.items()}
    x = np.ascontiguousarray(W.pop("x"))
    p = W.pop("p")
    Bsz, L, _ = x.shape
    depth = W["w_in"].shape[0]
    assert depth == 2 and Bsz == 2
    TOK = L // 4
    nc = build_fused(L)
    xT = [np.ascontiguousarray(x[b].T) for b in range(Bsz)]
    in_maps = []
    for c in range(NCORES):
        b, r = c // 4, c % 4
        s0 = r * TOK
        xs = np.zeros((D, TOK + 2), np.float32)
        lo = max(s0 - 2, 0)
        xs[:, 2 - (s0 - lo):] = xT[b][:, lo:s0 + TOK]
        im = {"xT_b": xT[b], "xs0": xs,
              "hmask": np.full((128, 1), 0.0 if r == 0 else 1.0, np.float32)}
        for i in range(depth):
            for k, v in mixer_inputs(i, r, None, W).items():
                im[f"{k}_{i}"] = v
            for k, v in dense_inputs(i, None, None, p[i, b, s0:s0 + TOK], W).items():
                im[f"{k}_{i}"] = v
        in_maps.append(im)
    res = run_bass_kernel_spmd(nc, in_maps, core_ids=list(range(NCORES)))
    out = np.empty_like(x)
    for c in range(NCORES):
        b, r = c // 4, c % 4
        out[b, r * TOK:(r + 1) * TOK] = res.results[c]["oT"].T
    return out
```

```python
import numpy as np
from contextlib import ExitStack
import concourse.bass as bass
import concourse.mybir as mybir
from concourse.bass_utils import run_bass_kernel_spmd

F32 = mybir.dt.float32
BF16 = mybir.dt.bfloat16
AF = mybir.ActivationFunctionType
ALU = mybir.AluOpType
AX = mybir.AxisListType

ENGS = ("pe", "act", "dve", "pool", "sp")
EPOCH = 16000
RING = 12


class Buf:
    __slots__ = ("ap", "w", "r", "name", "psum")

    def __init__(self, ap, name=""):
        self.ap = ap
        self.psum = False
        self.w = None
        self.r = []
        self.name = name

    def __getitem__(self, k):
        return self.ap[k]


class Prog:
    def __init__(self, nc, self_sync=True, prefix=""):
        self.nc = nc
        self.prefix = prefix
        self.es = ExitStack()
        self.es_sem = ExitStack()
        self.phase_finals = []
        self.items = {e: [] for e in ENGS}
        self.count = {e: 0 for e in ENGS}
        self.waited = {e: {} for e in ENGS}
        self.self_sync = self_sync
        self.semh = {}
        self.dma_n = {"sp": 0, "pool": 0, "act": 0}
        self.dma_last = {}
        self.nbuf = 0

    def sem(self, key):
        if key not in self.semh:
            nm = self.prefix + "s_" + "_".join(str(k) for k in key)
            self.semh[key] = self.es_sem.enter_context(self.nc.semaphore(nm))
        return self.semh[key]

    def sb(self, shape, dtype=F32, name=None):
        self.nbuf += 1
        name = self.prefix + (name or f"sb{self.nbuf}")
        t = self.es.enter_context(self.nc.sbuf_tensor(name, list(shape), dtype))
        return t

    def ps(self, shape, dtype=F32, name=None):
        self.nbuf += 1
        name = self.prefix + (name or f"ps{self.nbuf}")
        t = self.es.enter_context(self.nc.psum_tensor(name, list(shape), dtype))
        return t

    def buf(self, ap, name=""):
        return Buf(ap, name)

    def sbuf(self, shape, dtype=F32, name=None):
        t = self.sb(shape, dtype, name)
        return Buf(t[:], name or "")

    def psbuf(self, shape, dtype=F32, name=None):
        t = self.ps(shape, dtype, name)
        b = Buf(t[:], name or "")
        b.psum = True
        return b

    def _deps(self, reads, writes):
        deps = []
        for b in reads:
            if b.w is not None:
                deps.append(b.w)
        for b in writes:
            if b.w is not None:
                deps.append(b.w)
            deps.extend(b.r)
        return deps

    def _waits(self, eng, deps, own_key_prefix):
        waits = []
        wd = self.waited[eng]
        for (key, val) in deps:
            if key[0] == own_key_prefix and key[0] != "dma":
                if eng == "pe" or not self.self_sync:
                    continue
            if wd.get(key, 0) >= val:
                continue
            wd[key] = val
            waits.append((key, val))
        return waits

    def op(self, eng, fn, reads=(), writes=()):
        pr = [b for b in reads if b.psum]
        if pr:
            reads = [b for b in reads if not b.psum]
            writes = list(writes) + pr
        deps = self._deps(reads, writes)
        waits = self._waits(eng, deps, eng)
        self.count[eng] += 1
        c = self.count[eng]
        key = (eng, (c - 1) // EPOCH)
        val = (c - 1) % EPOCH + 1
        ev = (key, val)
        self.items[eng].append((waits, fn, ev, 1))
        for b in reads:
            b.r.append(ev)
        for b in writes:
            b.w = ev
            b.r = []
        return ev

    def dma(self, out_ap, in_ap, reads=(), writes=(), q="sp", **kw):
        deps = self._deps(reads, writes)
        j = self.dma_n[q]
        self.dma_n[q] += 1
        slot = j % RING
        key = ("dma", q, slot)
        if j >= RING:
            deps.append((key, 16 * (j // RING)))
        waits = self._waits(q, deps, "dma")
        val = 16 * (j // RING + 1)
        ev = (key, val)
        self.dma_last[key] = val

        def fn(e, out_ap=out_ap, in_ap=in_ap, kw=kw):
            o = out_ap(e) if callable(out_ap) else out_ap
            i = in_ap(e) if callable(in_ap) else in_ap
            try:
                return e.dma_start(out=o, in_=i, **kw)
            except Exception:
                print('DMA FAIL out', o, 'in', i, flush=True)
                raise
        self.items[q].append((waits, fn, ev, 16))
        for b in reads:
            b.r.append(ev)
        for b in writes:
            b.w = ev
            b.r = []
        return ev

    def _last_events(self):
        finals = []
        for key, val in self.dma_last.items():
            finals.append((key, val))
        for e in ("pe", "act", "dve", "pool"):
            c = self.count[e]
            if c > 0:
                finals.append(((e, (c - 1) // EPOCH), (c - 1) % EPOCH + 1))
        return finals

    def begin_phase(self):
        finals = self._last_events()
        for e in ENGS:
            waits = self._waits(e, finals, "__none__")
            if waits:
                self.items[e].append((waits, None, None, 0))

    def end_phase(self, final=False):
        nc = self.nc
        for e in ENGS:
            for (waits, fn, ev, inc) in self.items[e]:
                if ev is not None:
                    self.sem(ev[0])
                for (k, v) in waits:
                    self.sem(k)
        final_waits = self._last_events() if final else []
        for (k, v) in final_waits:
            self.sem(k)
        items = self.items
        semh = self.semh

        def run(e, lst, fin=False):
            for (waits, fn, ev, inc) in lst:
                for (k, v) in waits:
                    e.wait_ge(semh[k], v)
                if fn is None:
                    continue
                ins = fn(e)
                if ins is not None and ev is not None:
                    ins.then_inc(semh[ev[0]], inc)
            if fin:
                for (k, v) in final_waits:
                    e.wait_ge(semh[k], v)

        with nc.Block() as block:
            @block.sync
            def _(e):
                run(e, items["sp"], fin=final)

            @block.tensor
            def _(e):
                run(e, items["pe"])

            @block.scalar
            def _(e):
                run(e, items["act"])

            @block.vector
            def _(e):
                run(e, items["dve"])

            @block.gpsimd
            def _(e):
                run(e, items["pool"])
        self.items = {e: [] for e in ENGS}
        self.es.close()
        self.es = ExitStack()

    def emit(self):
        self.end_phase(final=True)

    def close(self):
        self.es.close()
        self.es_sem.close()

    def mm(self, out, lhsT, rhs, start=True, stop=True, reads=(), writes=()):
        return self.op("pe", lambda e: e.matmul(out, lhsT, rhs, start=start, stop=stop),
                       reads, writes)


class View:
    __slots__ = ("buf", "ap")

    def __init__(self, buf, ap):
        self.buf = buf
        self.ap = ap

    def __getitem__(self, k):
        return View(self.buf, self.ap[k])


def V(buf, *k):
    if not k:
        return View(buf, buf.ap)
    return View(buf, buf.ap[k if len(k) > 1 else k[0]])


def _ap(x):
    return x.ap if isinstance(x, View) else x


def _bufs(*xs):
    return [x.buf for x in xs if isinstance(x, View)]


class Ops:
    def __init__(self, P):
        self.P = P

    def mm(self, out, lhsT, rhs, start=True, stop=True):
        return self.P.op("pe", lambda e: e.matmul(out.ap, lhsT.ap, rhs.ap, start=start, stop=stop),
                         _bufs(lhsT, rhs), _bufs(out))

    def tr(self, out, in_, ident):
        return self.P.op("pe", lambda e: e.transpose(out.ap, in_.ap, ident.ap), _bufs(in_, ident), _bufs(out))

    def act(self, out, in_, func, bias=None, scale=None, accum=None, eng="act"):
        kw = {}
        if bias is not None:
            kw["bias"] = _ap(bias)
        if scale is not None:
            kw["scale"] = _ap(scale)
        if accum is not None:
            kw["accum_out"] = _ap(accum)
        return self.P.op("act", lambda e: e.activation(out.ap, in_.ap, func, **kw),
                         _bufs(in_, bias, scale), _bufs(out, accum))

    def tt(self, eng, out, in0, in1, op):
        return self.P.op(eng, lambda e: e.tensor_tensor(out.ap, in0.ap, in1.ap, op=op), _bufs(in0, in1), _bufs(out))

    def ts(self, eng, out, in0, s1, op0, s2=None, op1=None):
        if op1 is None:
            return self.P.op(eng, lambda e: e.tensor_scalar(out.ap, in0.ap, _ap(s1), None, op0=op0),
                             _bufs(in0, s1), _bufs(out))
        return self.P.op(eng, lambda e: e.tensor_scalar(out.ap, in0.ap, _ap(s1), _ap(s2), op0=op0, op1=op1),
                         _bufs(in0, s1, s2), _bufs(out))

    def stt(self, eng, out, in0, scalar, in1, op0, op1):
        return self.P.op(eng, lambda e: e.scalar_tensor_tensor(out.ap, in0.ap, _ap(scalar), in1.ap, op0=op0, op1=op1),
                         _bufs(in0, scalar, in1), _bufs(out))

    def copy(self, eng, out, in_):
        if eng == "act":
            return self.act(out, in_, AF.Identity)
        return self.P.op(eng, lambda e: e.tensor_copy(out.ap, in_.ap), _bufs(in_), _bufs(out))

    def recip(self, out, in_):
        return self.P.op("dve", lambda e: e.reciprocal(out.ap, in_.ap), _bufs(in_), _bufs(out))

    def scan(self, eng, out, d0, d1, init, op0=None, op1=None):
        op0 = op0 or ALU.mult
        op1 = op1 or ALU.add
        return self.P.op(eng, lambda e: e.tensor_tensor_scan(out.ap, d0.ap, d1.ap, _ap(init), op0=op0, op1=op1),
                         _bufs(d0, d1, init), _bufs(out))

    def memset(self, eng, out, val):
        return self.P.op(eng, lambda e: e.memset(out.ap, val), [], _bufs(out))

    def aselect(self, out, in_, pattern, cmp, fill, base=0, cm=1):
        return self.P.op("pool", lambda e: e.affine_select(out.ap, in_.ap, pattern=pattern, compare_op=cmp, fill=fill,
                                                          base=base, channel_multiplier=cm),
                         _bufs(in_), _bufs(out))

    def dma(self, out, in_, q="sp"):
        return self.P.dma(_ap(out), _ap(in_), reads=_bufs(in_), writes=_bufs(out), q=q)


def _prog_coll(self, kind, in_ap, out_ap, groups, reads=(), writes=()):
    q = "pool"
    deps = self._deps(reads, writes)
    self.ncoll = getattr(self, "ncoll", 0) + 1
    key = ("cc", self.ncoll)
    waits = self._waits(q, deps, "dma")
    ev = (key, 1)
    self.dma_last[key] = 1

    def fn(e):
        return e.collective_compute(kind, ALU.bypass, groups, [in_ap.opt()], [out_ap.opt()])
    self.items[q].append((waits, fn, ev, 1))
    for b in reads:
        b.r.append(ev)
    for b in writes:
        b.w = ev
        b.r = []
    return ev


Prog.coll = _prog_coll

import math

D = 1024
DFF = 2816
NF = 22
EPS = 1e-6
NEG = -1.0e30
TB = 512
NB = 4
NWF = 656
NWT = 386
SEQ = 16384
NCORES = 8
GROUPS = [[0, 1, 2, 3], [4, 5, 6, 7]]


class PsPool:
    def __init__(self, P, n=8, name="psp"):
        self.bufs = [P.psbuf([128, 512], F32, f"{name}{i}") for i in range(n)]
        self.i = 0

    def get(self):
        b = self.bufs[self.i % len(self.bufs)]
        self.i += 1
        return b

def emit_mixer(nc, P, O, L, sfx, xsrc, yout, after_sb):
    def din(name, shape):
        return nc.dram_tensor(name + sfx, list(shape), F32, kind="ExternalInput").ap()
    anorm = din("anorm", [128, 8])
    wf_d = din("wf", [D, NWF])
    wt_d = din("wt", [D, NWT])
    dn_sc_d = din("dn_sc", [128, 2])
    dn_cw_d = din("dn_cw", [128, 12])
    dn_ng_d = din("dn_ng", [128, 128])
    gla_ng_d = din("gla_ng", [128, 128])
    gla_w2_d = din("gla_w2", [16, 64])
    gla_b2_d = din("gla_b2", [64, 1])
    s5_lam_d = din("s5_lam", [128, 12])
    s5_b_d = din("s5_b", [2, 4, 128, 16])
    s5_c_d = din("s5_c", [2, 4, 128, 16])
    s5_d_d = din("s5_d", [128, 1])
    psp = PsPool(P, n=7)
    yps_ded = P.psbuf([128, 512], F32, "yps_ded")

    def S(shape, dt=F32, name=None):
        return P.sbuf(shape, dt, name)

    ones = S([128, 512], F32, "ones")
    O.memset("pool", V(ones), 1.0)
    ident = S([128, 128], F32, "ident")
    O.memset("pool", V(ident), 1.0)
    O.aselect(V(ident), V(ident), [[-1, 128]], ALU.is_equal, 0.0, cm=1)
    U = S([128, 128], F32, "U")
    O.memset("pool", V(U), 1.0)
    O.aselect(V(U), V(U), [[1, 128]], ALU.is_ge, 0.0, cm=-1)
    mneg = S([128, 128], F32, "mneg")
    O.memset("pool", V(mneg), 0.0)
    O.aselect(V(mneg), V(mneg), [[-1, 128]], ALU.is_ge, NEG, cm=1)
    mnegT = S([128, 128], F32, "mnegT")
    O.memset("pool", V(mnegT), 0.0)
    O.aselect(V(mnegT), V(mnegT), [[1, 128]], ALU.is_ge, NEG, cm=-1)
    nLs = S([128, 128], F32, "nLs")
    O.memset("pool", V(nLs), -1.0)
    O.aselect(V(nLs), V(nLs), [[-1, 128]], ALU.is_gt, 0.0, cm=1)

    c_anorm = S([128, 8], F32, "c_anorm")
    O.dma(V(c_anorm), anorm)
    wf = S([128, 8, NWF], BF16, "wf_s")
    wt = S([128, 8, NWT], BF16, "wt_s")
    O.dma(V(wf), wf_d.rearrange("(k p) n -> p k n", p=128), q="pool")
    O.dma(V(wt), wt_d.rearrange("(k p) n -> p k n", p=128), q="pool")
    dn_sc = S([128, 2], F32, "dn_sc_s"); O.dma(V(dn_sc), dn_sc_d)
    dn_cw = S([128, 12], F32, "dn_cw_s"); O.dma(V(dn_cw), dn_cw_d)
    dn_ng = S([128, 128], F32, "dn_ng_s"); O.dma(V(dn_ng), dn_ng_d)
    gla_ng = S([128, 128], F32, "gla_ng_s"); O.dma(V(gla_ng), gla_ng_d)
    gla_w2 = S([16, 64], F32, "gla_w2_s"); O.dma(V(gla_w2), gla_w2_d)
    gla_b2 = S([64, 1], F32, "gla_b2_s"); O.dma(V(gla_b2), gla_b2_d)
    nb2 = S([64, 1], F32, "nb2")
    O.ts("dve", V(nb2), V(gla_b2), -1.0, ALU.mult)
    negA = S([128, 1], F32, "negA")
    O.act(V(negA), V(dn_sc, slice(None), slice(0, 1)), AF.Exp)
    O.ts("dve", V(negA), V(negA), -1.0, ALU.mult)
    s5d = S([128, 1], F32, "s5d"); O.dma(V(s5d), s5_d_d)

    lam = S([128, 12], F32, "lam"); O.dma(V(lam), s5_lam_d)
    lr_, li_, ls_ = (V(lam, slice(None), slice(0, 4)), V(lam, slice(None), slice(4, 8)),
                     V(lam, slice(None), slice(8, 12)))
    pp = S([128, 64], F32, "s5pp")

    def col(i):
        return V(pp, slice(None), slice(4 * i, 4 * i + 4))
    step, lrs, th, mag, c8, s8, t0, t1, cr, ci, den, nr, fr, fi, t2, t3 = [col(i) for i in range(16)]
    O.act(step, ls_, AF.Exp)
    O.tt("dve", lrs, lr_, step, ALU.mult)
    O.tt("dve", th, li_, step, ALU.mult)
    O.act(mag, lrs, AF.Exp)
    halfpi = S([128, 1], F32, "halfpi"); O.memset("pool", V(halfpi), math.pi / 2)
    O.act(s8, th, AF.Sin, scale=0.125)
    O.act(c8, th, AF.Sin, scale=-0.125, bias=V(halfpi))
    for _ in range(3):
        O.tt("dve", t0, c8, c8, ALU.mult)
        O.tt("dve", t1, s8, s8, ALU.mult)
        O.tt("dve", t2, c8, s8, ALU.mult)
        O.tt("dve", c8, t0, t1, ALU.subtract)
        O.ts("dve", s8, t2, 2.0, ALU.mult)
    O.tt("dve", cr, mag, c8, ALU.mult)
    O.tt("dve", ci, mag, s8, ALU.mult)
    O.tt("dve", t0, lr_, lr_, ALU.mult)
    O.tt("dve", t1, li_, li_, ALU.mult)
    O.tt("dve", den, t0, t1, ALU.add)
    O.recip(den, den)
    O.ts("dve", nr, cr, -1.0, ALU.add)
    O.tt("dve", t0, nr, lr_, ALU.mult)
    O.tt("dve", t1, ci, li_, ALU.mult)
    O.tt("dve", t0, t0, t1, ALU.add)
    O.tt("dve", fr, t0, den, ALU.mult)
    O.tt("dve", t0, ci, lr_, ALU.mult)
    O.tt("dve", t1, nr, li_, ALU.mult)
    O.tt("dve", t0, t0, t1, ALU.subtract)
    O.tt("dve", fi, t0, den, ALU.mult)

    Ct = [S([128, TB], F32, f"Ct{j}") for j in range(4)]
    St = [S([128, TB], F32, f"St{j}") for j in range(4)]
    Mg = [S([128, TB], F32, f"Mg{j}") for j in range(4)]
    rq = S([128, 16], F32, "rq")
    r512 = S([128, 8], F32, "r512")
    tmpT = S([128, TB // 2], F32, "tmpT")
    for j in range(4):
        cj, sj, ta, tb_ = [V(rq, slice(None), slice(4 * j + i, 4 * j + i + 1)) for i in range(4)]
        O.copy("dve", cj, V(pp, slice(None), slice(4 * 4 + j, 4 * 4 + j + 1)))
        O.copy("dve", sj, V(pp, slice(None), slice(4 * 5 + j, 4 * 5 + j + 1)))
        O.memset("pool", V(Ct[j], slice(None), slice(0, 1)), 1.0)
        O.memset("pool", V(St[j], slice(None), slice(0, 1)), 0.0)
        n = 1
        while n < TB:
            lo_c, lo_s = V(Ct[j], slice(None), slice(0, n)), V(St[j], slice(None), slice(0, n))
            hi_c, hi_s = V(Ct[j], slice(None), slice(n, 2 * n)), V(St[j], slice(None), slice(n, 2 * n))
            tm = V(tmpT, slice(None), slice(0, n))
            O.ts("dve", tm, lo_s, sj, ALU.mult)
            O.stt("dve", hi_c, lo_c, cj, tm, ALU.mult, ALU.subtract)
            O.ts("dve", tm, lo_c, sj, ALU.mult)
            O.stt("dve", hi_s, lo_s, cj, tm, ALU.mult, ALU.add)
            O.tt("dve", ta, cj, cj, ALU.mult)
            O.tt("dve", tb_, sj, sj, ALU.mult)
            O.tt("dve", sj, cj, sj, ALU.mult)
            O.ts("dve", sj, sj, 2.0, ALU.mult)
            O.tt("dve", cj, ta, tb_, ALU.subtract)
            n *= 2
        O.copy("dve", V(r512, slice(None), slice(2 * j, 2 * j + 1)), cj)
        O.copy("dve", V(r512, slice(None), slice(2 * j + 1, 2 * j + 2)), sj)
        O.ts("dve", V(Mg[j]), V(ones), V(pp, slice(None), slice(4 * 3 + j, 4 * 3 + j + 1)), ALU.mult)

    BreT = [S([128, 128], F32, f"BreT{j}") for j in range(4)]
    BimT = [S([128, 128], F32, f"BimT{j}") for j in range(4)]
    Cre = [S([128, 128], F32, f"Cre{j}") for j in range(4)]
    Cim = [S([128, 128], F32, f"Cim{j}") for j in range(4)]
    bst = S([128, 2, 16], F32, "bst")
    padr = S([128, 128], F32, "padr")
    padi = S([128, 128], F32, "padi")
    for j in range(4):
        O.dma(V(bst, slice(None), 0), s5_b_d[0, j])
        O.dma(V(bst, slice(None), 1), s5_b_d[1, j])
        O.memset("pool", V(padr), 0.0)
        O.memset("pool", V(padi), 0.0)
        O.memset("pool", V(Cre[j]), 0.0)
        O.memset("pool", V(Cim[j]), 0.0)
        frj = V(pp, slice(None), slice(4 * 12 + j, 4 * 12 + j + 1))
        fij = V(pp, slice(None), slice(4 * 13 + j, 4 * 13 + j + 1))
        for hf in range(2):
            ps_ = slice(64 * hf, 64 * hf + 64)
            cs_ = slice((2 * j + hf) * 16, (2 * j + hf) * 16 + 16)
            bre, bim = V(bst, ps_, 0), V(bst, ps_, 1)
            tm = V(tmpT, ps_, slice(0, 16))
            O.ts("dve", tm, bim, fij[ps_], ALU.mult)
            O.stt("dve", V(padr, ps_, cs_), bre, frj[ps_], tm, ALU.mult, ALU.subtract)
            O.ts("dve", tm, bre, fij[ps_], ALU.mult)
            O.stt("dve", V(padi, ps_, cs_), bim, frj[ps_], tm, ALU.mult, ALU.add)
            O.dma(V(Cre[j], ps_, cs_), s5_c_d[0, j, 64 * hf:64 * hf + 64, :])
            O.dma(V(Cim[j], ps_, cs_), s5_c_d[1, j, 64 * hf:64 * hf + 64, :])
        for (pad, dst) in ((padr, BreT[j]), (padi, BimT[j])):
            pt = psp.get()
            O.tr(V(pt, slice(None), slice(0, 128)), V(pad), V(ident))
            O.copy("act", V(dst), V(pt, slice(None), slice(0, 128)))
        O.ts("dve", V(Cim[j]), V(Cim[j]), -1.0, ALU.mult)

    Sdn = [S([128, 128], F32, f"Sdn{i}") for i in range(2)]
    Sgl = [S([64, 128], F32, f"Sgl{i}") for i in range(2)]
    O.memset("pool", V(Sdn[0]), 0.0)
    O.memset("pool", V(Sgl[0]), 0.0)
    s5c = [S([128, 2], F32, f"s5c{j}") for j in range(4)]
    for j in range(4):
        O.memset("pool", V(s5c[j]), 0.0)
    s5i = [S([128, 4], F32, f"s5i{j}") for j in range(4)]
    chist = [S([128, 3], F32, f"chist{c}") for c in range(3)]
    for c in range(3):
        O.memset("pool", V(chist[c]), 0.0)

    x_t = P.sb([128, 8, TB], F32, "x_t")
    xb = [Buf(x_t[:, k, :], f"x{k}") for k in range(8)]
    h_t = P.sb([128, 8, TB], BF16, "h_t")
    hb = [Buf(h_t[:, k, :], f"h{k}") for k in range(8)]
    sq = [S([128, TB], F32, f"sq{i}") for i in range(2)]
    rstd = S([128, TB], F32, "rstd")
    cbuf = [S([128, TB + 3], F32, f"cbuf{c}") for c in range(3)]
    cacc = [S([128, TB], F32, f"cacc{c}") for c in range(3)]
    qn = S([128, TB], F32, "qn")
    kn = S([128, TB], F32, "kn")
    tok = [S([128, NWT], F32, f"tok{b}") for b in range(NB)]
    uT = S([128, TB], F32, "uT")
    lrs_b = S([16, TB], F32, "lrs_b")
    gls = S([64, TB], F32, "gls")
    gc = S([64, TB], F32, "gc")
    gEQ = S([64, TB], F32, "gEQ")
    gEK = S([64, TB], F32, "gEK")
    gqe = S([64, TB], F32, "gqe")
    gke = S([64, TB], F32, "gke")
    gkr = S([64, TB], F32, "gkr")
    gqr = S([64, TB], F32, "gqr")
    s5w = [S([128, TB], F32, f"s5w{i}") for i in range(6)]
    yc_sb = S([128, TB], F32, "yc_sb")
    ys_t = P.sb([128, 3, TB], BF16, "ys_t")
    ys = [Buf(ys_t[:, i3, :], f"ys{i3}") for i3 in range(3)]

    def blkbufs(name, shape, n=NB):
        return [S(shape, F32, f"{name}{b}") for b in range(n)]
    sc = [[S([128, 1], F32, f"sc{b}_{i}") for i in range(12)] for b in range(NB)]
    Ug = blkbufs("Ug", [128, 128])
    Eb = blkbufs("Eb", [128, 128])
    ETb = blkbufs("ETb", [128, 128])
    qd = blkbufs("qd", [128, 128])
    bk = blkbufs("bk", [128, 128])
    kdec = blkbufs("kdec", [128, 128])
    bv = blkbufs("bv", [128, 128])
    M = blkbufs("M", [128, 256])
    X = blkbufs("X", [128, 128])
    attnT = blkbufs("attnT", [128, 128])
    un = blkbufs("un", [128, 128])
    wT = blkbufs("wT", [128, 128])
    ub = blkbufs("ub", [128, 128])
    sg = blkbufs("sg", [128, 128])
    yo = blkbufs("yo", [128, 128])
    junk = blkbufs("junk", [128, 128], 2)
    gjunk = blkbufs("gjunk", [128, 128], 2)
    gsc = [[S([64, 1], F32, f"gsc{b}_{i}") for i in range(2)] for b in range(NB)]
    gkdT = blkbufs("gkdT", [64, 128])
    gkd = blkbufs("gkd", [128, 64])
    gaT = blkbufs("gaT", [128, 128])
    gsg = blkbufs("gsg", [128, 128])
    gyo = blkbufs("gyo", [128, 128])
    gss = [[S([128, 1], F32, f"gss{b}_{i}") for i in range(2)] for b in range(NB)]
    ones64 = S([64, 128], F32, "ones64")
    O.memset("pool", V(ones64), 1.0)

    A = slice(None)

    def bsl(b):
        return slice(128 * b, 128 * b + 128)

    def out_norm(o_ps, ss_b, ng, sgate, ydst, jk):
        ssum, rs = ss_b
        O.memset("pool", V(ssum), 0.0)
        O.act(V(jk), o_ps, AF.Square, accum=V(ssum))
        O.act(V(rs), V(ssum), AF.Sqrt, bias=EPS, scale=1.0 / 128)
        O.recip(V(rs), V(rs))
        O.stt("dve", V(ydst), o_ps, V(rs), V(ng), ALU.mult, ALU.mult)
        O.tt("pool", V(ydst), V(ydst), V(sgate), ALU.mult)

    nsb = L // TB
    sdn_i = 0
    sgl_i = 0
    def gen_front(t0f):
        for k in range(8):
            O.dma(V(xb[k]), xsrc(k, t0f))
        ps = psp.get()
        for k in range(8):
            s = sq[k % 2]
            O.act(V(s), V(xb[k]), AF.Square)
            O.mm(V(ps), V(ones, A, slice(0, 128)), V(s), start=(k == 0), stop=(k == 7))
        O.act(V(rstd), V(ps), AF.Sqrt, bias=EPS, scale=1.0 / D)
        O.recip(V(rstd), V(rstd))
        for k in range(8):
            O.stt("dve", V(hb[k]), V(xb[k]), V(c_anorm, A, slice(k, k + 1)), V(rstd), ALU.mult, ALU.mult)
        yield
        for c in range(3):
            ps = psp.get()
            for k in range(8):
                O.mm(V(ps), V(wf, A, k, slice(c * 128, (c + 1) * 128)), V(hb[k]), start=(k == 0), stop=(k == 7))
            O.copy("pool", V(cbuf[c], A, slice(0, 3)), V(chist[c]))
            O.copy("act", V(cbuf[c], A, slice(3, TB + 3)), V(ps))
            O.copy("pool", V(chist[c]), V(cbuf[c], A, slice(TB, TB + 3)))
            yield
        ps = psp.get()
        for k in range(8):
            O.mm(V(ps), V(wf, A, k, slice(384, 512)), V(hb[k]), start=(k == 0), stop=(k == 7))
        O.copy("act", V(uT), V(ps))
        yield
        ps_gq = psp.get()
        for k in range(8):
            O.mm(V(ps_gq, slice(0, 64)), V(wf, A, k, slice(512, 576)), V(hb[k]), start=(k == 0), stop=(k == 7))
        O.copy("act", V(gqr), V(ps_gq, slice(0, 64)))
        ps_gk = psp.get()
        for k in range(8):
            O.mm(V(ps_gk, slice(0, 64)), V(wf, A, k, slice(576, 640)), V(hb[k]), start=(k == 0), stop=(k == 7))
        O.copy("act", V(gkr), V(ps_gk, slice(0, 64)))
        yield
        ps = psp.get()
        for k in range(8):
            O.mm(V(ps, slice(0, 16)), V(wf, A, k, slice(640, 656)), V(hb[k]), start=(k == 0), stop=(k == 7))
        O.copy("act", V(lrs_b), V(ps, slice(0, 16)))
        yield
        for b in range(NB):
            ps = psp.get()
            for k in range(8):
                O.mm(V(ps, A, slice(0, NWT)), V(hb[k], A, bsl(b)), V(wt, A, k), start=(k == 0), stop=(k == 7))
            O.copy("act", V(tok[b]), V(ps, A, slice(0, NWT)))
            yield
        for c in range(3):
            O.ts("dve", V(cacc[c]), V(cbuf[c], A, slice(0, TB)), V(dn_cw, A, slice(4 * c, 4 * c + 1)), ALU.mult)
            for t in range(1, 4):
                O.stt("dve", V(cacc[c]), V(cbuf[c], A, slice(t, t + TB)),
                      V(dn_cw, A, slice(4 * c + t, 4 * c + t + 1)), V(cacc[c]), ALU.mult, ALU.add)
            O.act(V(cacc[c]), V(cacc[c]), AF.Silu)
            yield
        for c, dst, scl in ((0, qn, 128 ** -0.5), (1, kn, 1.0)):
            s = sq[c]
            O.act(V(s), V(cacc[c]), AF.Square)
            ps = psp.get()
            O.mm(V(ps), V(ones, A, slice(0, 128)), V(s))
            O.act(V(s), V(ps), AF.Sqrt, bias=EPS, scale=1.0)
            O.recip(V(s), V(s))
            O.stt("dve", V(dst), V(cacc[c]), scl, V(s), ALU.mult, ALU.mult)
            yield

    for _ in gen_front(0):
        pass
    for sb_i in range(nsb):
        t0_ = sb_i * TB
        vs = cacc[2]

        def gen_s5():
            yps = yps_ded
            for j in range(4):
                pr = psp.get()
                pi = psp.get()
                O.mm(V(pr), V(BreT[j]), V(uT))
                O.mm(V(pi), V(BimT[j]), V(uT))
                w0, w1, w2, w3, w4, w5 = [V(s5w[i]) for i in range(6)]
                O.tt("dve", w0, V(pr), V(Ct[j]), ALU.mult)
                O.tt("dve", w1, V(pi), V(St[j]), ALU.mult)
                O.tt("pool", w0, w0, w1, ALU.add)
                O.tt("dve", w2, V(pi), V(Ct[j]), ALU.mult)
                O.tt("dve", w3, V(pr), V(St[j]), ALU.mult)
                O.tt("pool", w2, w2, w3, ALU.subtract)
                cr_, ci_ = V(s5c[j], A, slice(0, 1)), V(s5c[j], A, slice(1, 2))
                c5, s5_ = V(r512, A, slice(2 * j, 2 * j + 1)), V(r512, A, slice(2 * j + 1, 2 * j + 2))
                ir, ii, ta, tb_ = [V(s5i[j], A, slice(i, i + 1)) for i in range(4)]
                O.tt("dve", ta, ci_, s5_, ALU.mult)
                O.stt("dve", ir, cr_, c5, ta, ALU.mult, ALU.subtract)
                O.tt("dve", tb_, cr_, s5_, ALU.mult)
                O.stt("dve", ii, ci_, c5, tb_, ALU.mult, ALU.add)
                O.scan("dve", w4, V(Mg[j]), w0, ir)
                O.scan("dve", w5, V(Mg[j]), w2, ii)
                O.copy("pool", cr_, V(s5w[4], A, slice(TB - 1, TB)))
                O.copy("pool", ci_, V(s5w[5], A, slice(TB - 1, TB)))
                O.tt("dve", w0, w4, V(Ct[j]), ALU.mult)
                O.tt("pool", w1, w5, V(St[j]), ALU.mult)
                O.tt("dve", w0, w0, w1, ALU.subtract)
                O.tt("pool", w2, w5, V(Ct[j]), ALU.mult)
                O.tt("dve", w3, w4, V(St[j]), ALU.mult)
                O.tt("pool", w2, w2, w3, ALU.add)
                yield
                O.mm(V(yps), V(Cre[j]), w0, start=(j == 0), stop=False)
                O.mm(V(yps), V(Cim[j]), w2, start=False, stop=(j == 3))
                yield
            O.stt("dve", V(yc_sb), V(uT), V(s5d), V(yps), ALU.mult, ALU.add)
            O.act(V(ys[2]), V(yc_sb), AF.Gelu_apprx_tanh)

        def gen_gla():
            nonlocal sgl_i
            ps = psp.get()
            O.mm(V(ps, slice(0, 64)), V(gla_w2), V(lrs_b))
            O.act(V(gls), V(ps, slice(0, 64)), AF.Exp, scale=-1.0, bias=V(nb2))
            O.act(V(gls), V(gls), AF.Ln, bias=1.0)
            for b in range(NB):
                O.scan("dve", V(gc, A, bsl(b)), V(ones64), V(gls, A, bsl(b)), 0.0)
            O.act(V(gEQ), V(gc), AF.Exp, scale=-1.0 / 16, bias=math.log(1.0 / 8))
            O.act(V(gEK), V(gc), AF.Exp, scale=1.0 / 16)
            O.tt("dve", V(gqe), V(gqr), V(gEQ), ALU.mult)
            O.tt("dve", V(gke), V(gkr), V(gEK), ALU.mult)
            yield
            for b in range(NB):
                nbl, gend = V(gsc[b][0]), V(gsc[b][1])
                O.ts("dve", nbl, V(gc, A, slice(128 * b + 127, 128 * b + 128)), -1.0 / 16, ALU.mult)
                O.act(gend, nbl, AF.Exp)
                O.act(V(gkdT[b]), V(gc, A, bsl(b)), AF.Exp, scale=1.0 / 16, bias=nbl)
                O.tt("dve", V(gkdT[b]), V(gkdT[b]), V(gkr, A, bsl(b)), ALU.mult)
            yield
            for b in range(NB):
                pt = psp.get()
                O.tr(V(pt, A, slice(0, 64)), V(gkdT[b]), V(ident, slice(0, 64), slice(0, 64)))
                O.copy("act", V(gkd[b]), V(pt, A, slice(0, 64)))
                pa = psp.get()
                O.mm(V(pa, A, slice(0, 128)), V(gke, A, bsl(b)), V(gqe, A, bsl(b)))
                O.tt("dve", V(gaT[b]), V(pa, A, slice(0, 128)), V(U), ALU.mult)
                O.act(V(gsg[b]), V(tok[b], A, slice(258, 386)), AF.Silu)
                yield
            for b in range(NB):
                gv = V(tok[b], A, slice(130, 258))
                So, Sn = Sgl[sgl_i % 2], Sgl[(sgl_i + 1) % 2]
                sgl_i += 1
                po = psp.get()
                O.mm(V(po, A, slice(0, 128)), V(gqe, A, bsl(b)), V(So), start=True, stop=False)
                O.mm(V(po, A, slice(0, 128)), V(gaT[b]), gv, start=False, stop=True)
                pd = psp.get()
                O.mm(V(pd, slice(0, 64), slice(0, 128)), V(gkd[b]), gv)
                O.stt("dve", V(Sn), V(So), V(gsc[b][1]), V(pd, slice(0, 64), slice(0, 128)), ALU.mult, ALU.add)
                out_norm(V(po, A, slice(0, 128)), gss[b], gla_ng, gsg[b], gyo[b], gjunk[b % 2])
                yield
                ptr = psp.get()
                O.tr(V(ptr, A, slice(0, 128)), V(gyo[b]), V(ident))
                O.copy("act", V(ys[1], A, bsl(b)), V(ptr, A, slice(0, 128)))
                yield

        def gen_dn_pre(b):
            beta, glog, gam, glast, ngam, egam, bg, edl, gend, tmp = [V(sc[b][i]) for i in range(10)]
            O.act(beta, V(tok[b], A, slice(128, 129)), AF.Sigmoid)
            O.act(tmp, V(tok[b], A, slice(129, 130)), AF.Exp, bias=V(dn_sc, A, slice(1, 2)))
            O.act(tmp, tmp, AF.Ln, bias=1.0)
            O.tt("dve", glog, tmp, V(negA), ALU.mult)
            O.ts("dve", V(Ug[b]), V(U), glog, ALU.mult)
            yield
            pA = psp.get()
            O.mm(V(pA, A, slice(0, 128)), V(ones, A, slice(0, 128)), V(Ug[b]))
            O.mm(V(pA, A, slice(128, 129)), V(U), glog)
            O.mm(V(pA, A, slice(129, 130)), V(ones, A, slice(0, 128)), glog)
            O.copy("dve", gam, V(pA, A, slice(128, 129)))
            O.copy("dve", glast, V(pA, A, slice(129, 130)))
            O.ts("dve", ngam, gam, -1.0, ALU.mult)
            O.act(egam, gam, AF.Exp)
            O.tt("dve", bg, beta, egam, ALU.mult)
            O.act(edl, gam, AF.Exp, scale=-1.0, bias=glast)
            O.act(gend, glast, AF.Exp)
            gbc = V(pA, A, slice(0, 128))
            O.stt("dve", V(Eb[b]), gbc, -1.0, V(mneg), ALU.mult, ALU.add)
            O.act(V(Eb[b]), V(Eb[b]), AF.Exp, bias=gam)
            O.tt("dve", V(ETb[b]), gbc, V(mnegT), ALU.add)
            O.act(V(qd[b]), gbc, AF.Exp)
            yield
            O.act(V(ETb[b]), V(ETb[b]), AF.Exp, bias=ngam)
            O.tt("dve", V(qd[b]), V(qd[b]), V(qn, A, bsl(b)), ALU.mult)
            O.act(V(sg[b]), V(tok[b], A, slice(0, 128)), AF.Silu)
            yield
            pt = psp.get()
            O.tr(V(pt, A, slice(0, 128)), V(kn, A, bsl(b)), V(ident))
            O.tr(V(pt, A, slice(128, 256)), V(vs, A, bsl(b)), V(ident))
            O.ts("dve", V(bk[b]), V(pt, A, slice(0, 128)), bg, ALU.mult)
            O.act(V(kdec[b]), V(pt, A, slice(0, 128)), AF.Identity, scale=edl)
            O.ts("dve", V(bv[b]), V(pt, A, slice(128, 256)), beta, ALU.mult)
            yield
            pk = psp.get()
            O.mm(V(pk, A, slice(0, 128)), V(kn, A, bsl(b)), V(kn, A, bsl(b)))
            O.mm(V(pk, A, slice(128, 256)), V(kn, A, bsl(b)), V(qn, A, bsl(b)))
            O.stt("dve", V(M[b], A, slice(0, 128)), V(pk, A, slice(0, 128)), beta, V(Eb[b]), ALU.mult, ALU.mult)
            O.tt("pool", V(M[b], A, slice(0, 128)), V(M[b], A, slice(0, 128)), V(nLs), ALU.mult)
            O.tt("dve", V(attnT[b]), V(pk, A, slice(128, 256)), V(ETb[b]), ALU.mult)
            blk_flag[b] = True
            yield
            pt = psp.get()
            O.tr(V(pt, A, slice(0, 128)), V(M[b], A, slice(0, 128)), V(ident))
            O.copy("act", V(M[b], A, slice(128, 256)), V(pt, A, slice(0, 128)))
            O.tt("dve", V(X[b]), V(pt, A, slice(0, 128)), V(ident), ALU.add)
            yield
            for lev in range(1, 8):
                pl = psp.get()
                Mv, MTv = V(M[b], A, slice(0, 128)), V(M[b], A, slice(128, 256))
                if lev <= 6:
                    O.mm(V(pl, A, slice(0, 128)), MTv, Mv)
                    O.mm(V(pl, A, slice(128, 256)), Mv, MTv)
                if lev >= 2:
                    O.mm(V(pl, A, slice(256, 384)), Mv, V(X[b]))
                if lev <= 6:
                    O.copy("act", V(M[b]), V(pl, A, slice(0, 256)))
                if lev >= 2:
                    O.tt("dve", V(X[b]), V(X[b]), V(pl, A, slice(256, 384)), ALU.add)
                yield
            pe_ = psp.get()
            O.mm(V(pe_, A, slice(0, 128)), V(X[b]), V(bv[b]))
            O.mm(V(pe_, A, slice(128, 256)), V(bk[b]), V(X[b]))
            O.copy("act", V(un[b]), V(pe_, A, slice(0, 128)))
            O.copy("act", V(wT[b]), V(pe_, A, slice(128, 256)))
            yield

        def gen_dn_seq():
            nonlocal sdn_i
            for b in range(NB):
                So, Sn = Sdn[sdn_i % 2], Sdn[(sdn_i + 1) % 2]
                sdn_i += 1
                pw = psp.get()
                O.mm(V(pw, A, slice(0, 128)), V(wT[b]), V(So))
                O.tt("dve", V(ub[b]), V(un[b]), V(pw, A, slice(0, 128)), ALU.subtract)
                yield
                po = psp.get()
                O.mm(V(po, A, slice(0, 128)), V(qd[b]), V(So), start=True, stop=False)
                O.mm(V(po, A, slice(0, 128)), V(attnT[b]), V(ub[b]), start=False, stop=True)
                pd = psp.get()
                O.mm(V(pd, A, slice(0, 128)), V(kdec[b]), V(ub[b]))
                O.stt("dve", V(Sn), V(So), V(sc[b][8]), V(pd, A, slice(0, 128)), ALU.mult, ALU.add)
                out_norm(V(po, A, slice(0, 128)), (sc[b][10], sc[b][11]), dn_ng, sg[b], yo[b], junk[b % 2])
                yield
                ptr = psp.get()
                O.tr(V(ptr, A, slice(0, 128)), V(yo[b]), V(ident))
                O.copy("act", V(ys[0], A, bsl(b)), V(ptr, A, slice(0, 128)))
                yield

        blk_flag = [False] * NB
        g_pre = [gen_dn_pre(b_) for b_ in range(NB)]
        g_s5, g_gla = gen_s5(), gen_gla()
        g_seq = None
        tasks = g_pre + [g_s5, g_gla]
        nf = gen_front((sb_i + 1) * TB) if sb_i + 1 < nsb else None
        started = False
        rnd = 0
        while tasks:
            if g_seq is None and not any(g in tasks for g in g_pre):
                g_seq = gen_dn_seq()
                tasks.insert(0, g_seq)
            if nf is not None and not started and all(blk_flag) and g_s5 not in tasks and g_gla not in tasks:
                tasks.append(nf)
                started = True
            rnd += 1
            for g in list(tasks):
                if g not in tasks:
                    continue
                if g is g_gla and rnd % 2 == 0:
                    continue
                try:
                    next(g)
                except StopIteration:
                    while g in tasks:
                        tasks.remove(g)
            if not tasks and g_seq is None:
                g_seq = gen_dn_seq()
                tasks.append(g_seq)
        if nf is not None and not started:
            for _ in nf:
                pass
        for i3 in range(3):
            yout(i3, t0_, ys[i3])
        after_sb(t0_)


def emit_dense(nc, P, TOK, last, sfx, src, NT=512):
    def din(name, shape):
        return nc.dram_tensor(name + sfx, list(shape), F32, kind="ExternalInput").ap()
    HT = TOK + 2
    pT = din("pT", [256, TOK])
    w_gate = din("w_gate", [D, 3 * D])
    b_gate = din("b_gate", [128, 24])
    w_branch = din("w_branch", [1536, D])
    w_o = din("w_o", [D, D])
    w_glu = din("w_glu", [512, 512])
    b_glu = din("b_glu", [128, 4])
    norms = din("norms", [128, 32])
    w_up = din("w_up", [D, 2 * DFF])
    cw = din("cw", [128, NF * 3])
    cb = din("cb", [128, NF])
    w_down = din("w_down", [DFF, D])
    w_pg = din("w_pg", [D, D])
    w_pp = din("w_pp", [256, D])
    psp = PsPool(P)
    ones = P.sbuf([128, 128], F32, "ones")
    P.op("pool", lambda e: e.memset(ones.ap, 1.0), writes=[ones])
    c_bg = P.sbuf([128, 24], F32, "c_bg")
    c_bglu = P.sbuf([128, 4], F32, "c_bglu")
    c_norm = P.sbuf([128, 32], F32, "c_norm")
    c_cw = P.sbuf([128, NF * 3], F32, "c_cw")
    c_cb = P.sbuf([128, NF], F32, "c_cb")
    hmask = P.sbuf([128, 1], F32, "hmask")
    for t, srcd in ((c_bg, b_gate), (c_bglu, b_glu), (c_norm, norms), (c_cw, cw), (c_cb, cb), (hmask, src["hmask"])):
        P.dma(t.ap, srcd, writes=[t])

    NSLAB = 5
    slab_t = [P.sb([128, 8, 1024], BF16, f"slab{i}") for i in range(NSLAB)]
    slabs = [Buf(t[:], f"slab{i}") for i, t in enumerate(slab_t)]
    slab_i = [0]

    def load_w(src, r0, nk, c0, ncol):
        s = slabs[slab_i[0] % NSLAB]
        slab_i[0] += 1
        view = s.ap[:, 0:nk, 0:ncol]
        srcv = src[r0:r0 + nk * 128, c0:c0 + ncol].rearrange("(k p) n -> p k n", p=128)
        P.dma(view, srcv, writes=[s], q="pool")
        return s

    x_t = P.sb([128, 8, NT], F32, "x_t")
    xb = [Buf(x_t[:, k, :], f"x{k}") for k in range(8)]
    h_t = P.sb([128, 8, NT], BF16, "h_t")
    hb = [Buf(h_t[:, k, :], f"h{k}") for k in range(8)]
    y_t = P.sb([128, 12, NT], BF16, "y_t")
    yb = [Buf(y_t[:, k, :], f"y{k}") for k in range(12)]
    yc_t = P.sb([128, 4, NT], BF16, "yc_t")
    ycb = [Buf(yc_t[:, k, :], f"yc{k}") for k in range(4)]
    m_t = P.sb([128, 8, NT], F32, "m_t")
    mb = [Buf(m_t[:, k, :], f"m{k}") for k in range(8)]
    mbf_t = P.sb([128, 8, NT], BF16, "mbf_t")
    mbfb = [Buf(mbf_t[:, k, :], f"mbf{k}") for k in range(8)]
    a_t = P.sb([128, NF, NT], BF16, "a_t")
    ab = [Buf(a_t[:, k, :], f"a{k}") for k in range(NF)]
    p_t = P.sb([128, 2, NT], BF16, "p_t")
    pb = [Buf(p_t[:, k, :], f"p{k}") for k in range(2)]
    hist_t = P.sb([128, NF, 2], F32, "hist_t")
    histb = [Buf(hist_t[:, k, :], f"hist{k}") for k in range(NF)]
    sq_t = P.sb([128, 2, NT], F32, "sq_t")
    sqb = [Buf(sq_t[:, k, :], f"sq{k}") for k in range(2)]
    rstd = P.sbuf([128, NT], F32, "rstd")
    NG = 3
    g_t = P.sb([128, NG, NT], F32, "g_t")
    gtb = [Buf(g_t[:, k, :], f"gt{k}") for k in range(NG)]
    gi = [0]
    gb_t = P.sb([128, 2, NT + 2], F32, "gb_t")
    gbb = [Buf(gb_t[:, k, :], f"gb{k}") for k in range(2)]
    cv_t = P.sb([128, 2, NT], F32, "cv_t")
    cvb = [Buf(cv_t[:, k, :], f"cv{k}") for k in range(2)]
    ob = mb

    def rmsnorm(w, ncol, outs):
        ps = psp.get()
        for k in range(8):
            s = sqb[k % 2]
            P.op("act", lambda e, s=s, k=k: e.activation(s.ap[:, :w], xb[k].ap[:, :w], AF.Square),
                 reads=[xb[k]], writes=[s])
            P.mm(ps.ap[:, :w], ones.ap, s.ap[:, :w], start=(k == 0), stop=(k == 7),
                 reads=[ones, s], writes=[ps])
        P.op("act", lambda e: e.activation(rstd.ap[:, :w], ps.ap[:, :w], AF.Sqrt, bias=EPS, scale=1.0 / D),
             reads=[ps], writes=[rstd])
        P.op("dve", lambda e: e.reciprocal(rstd.ap[:, :w], rstd.ap[:, :w]), reads=[rstd], writes=[rstd])
        for k in range(8):
            P.op("dve", lambda e, k=k: e.scalar_tensor_tensor(
                outs[k].ap[:, :w], xb[k].ap[:, :w], c_norm.ap[:, ncol + k:ncol + k + 1], rstd.ap[:, :w],
                op0=ALU.mult, op1=ALU.mult), reads=[xb[k], c_norm, rstd], writes=[outs[k]])

    def tile(c0, w, halo):
        for k in range(8):
            src["x"](P, xb[k], k, c0, w, halo)
        for k in range(12):
            src["y"](P, yb[k], k, c0, w, halo)
        if halo:
            for k in range(8):
                P.op("dve", lambda e, k=k: e.tensor_scalar(xb[k].ap[:, :w], xb[k].ap[:, :w], hmask.ap[:, 0:1], None,
                                                           op0=ALU.mult), reads=[xb[k], hmask], writes=[xb[k]])
            for k in range(12):
                P.op("dve", lambda e, k=k: e.tensor_scalar(yb[k].ap[:, :w], yb[k].ap[:, :w], hmask.ap[:, 0:1], None,
                                                           op0=ALU.mult), reads=[yb[k], hmask], writes=[yb[k]])
        if not halo:
            for k in range(2):
                P.dma(pb[k].ap[:, :w], pT[k * 128:(k + 1) * 128, c0 - 2:c0 - 2 + w], writes=[pb[k]], q="pool")
        rmsnorm(w, 0, hb)
        U = load_w(w_glu, 0, 4, 0, 512)
        for m in range(4):
            ps = psp.get()
            for k in range(4):
                P.mm(ps.ap[:, :w], U.ap[:, k, m * 128:(m + 1) * 128], yb[8 + k].ap[:, :w],
                     start=(k == 0), stop=(k == 3), reads=[U, yb[8 + k]], writes=[ps])
            g = gtb[gi[0] % NG]; gi[0] += 1
            P.op("act", lambda e, g=g, ps=ps, m=m: e.activation(g.ap[:, :w], ps.ap[:, :w], AF.Sigmoid,
                                                             bias=c_bglu.ap[:, m:m + 1]),
                 reads=[ps, c_bglu], writes=[g])
            P.op("dve", lambda e, g=g, m=m: e.tensor_tensor(ycb[m].ap[:, :w], yb[8 + m].ap[:, :w], g.ap[:, :w],
                                                          op=ALU.mult),
                 reads=[yb[8 + m], g], writes=[ycb[m]])
        for i in range(3):
            G = load_w(w_gate, 0, 8, i * D, D)
            B = load_w(w_branch, i * 512, 4, 0, D)
            ysrc = [yb[0], yb[1], yb[2], yb[3]] if i == 0 else ([yb[4], yb[5], yb[6], yb[7]] if i == 1 else ycb)
            for m in range(8):
                pg = psp.get()
                for k in range(8):
                    P.mm(pg.ap[:, :w], G.ap[:, k, m * 128:(m + 1) * 128], hb[k].ap[:, :w],
                         start=(k == 0), stop=(k == 7), reads=[G, hb[k]], writes=[pg])
                g = gtb[gi[0] % NG]; gi[0] += 1
                P.op("act", lambda e, g=g, pg=pg, i=i, m=m: e.activation(
                    g.ap[:, :w], pg.ap[:, :w], AF.Sigmoid, bias=c_bg.ap[:, i * 8 + m:i * 8 + m + 1]),
                    reads=[pg, c_bg], writes=[g])
                pp = psp.get()
                for k in range(4):
                    P.mm(pp.ap[:, :w], B.ap[:, k, m * 128:(m + 1) * 128], ysrc[k].ap[:, :w],
                         start=(k == 0), stop=(k == 3), reads=[B, ysrc[k]], writes=[pp])
                if i == 0:
                    P.op("dve", lambda e, g=g, pp=pp, m=m: e.tensor_tensor(
                        mb[m].ap[:, :w], pp.ap[:, :w], g.ap[:, :w], op=ALU.mult),
                        reads=[pp, g], writes=[mb[m]])
                else:
                    P.op("dve", lambda e, g=g, pp=pp, m=m: e.tensor_tensor(
                        g.ap[:, :w], pp.ap[:, :w], g.ap[:, :w], op=ALU.mult),
                        reads=[pp, g], writes=[g])
                    dst = mb[m] if i == 1 else mbfb[m]
                    P.op("pool", lambda e, g=g, m=m, dst=dst: e.tensor_tensor(
                        dst.ap[:, :w], mb[m].ap[:, :w], g.ap[:, :w], op=ALU.add),
                        reads=[mb[m], g], writes=[dst])
        O = load_w(w_o, 0, 8, 0, D)
        for m in range(8):
            ps = psp.get()
            for k in range(8):
                P.mm(ps.ap[:, :w], O.ap[:, k, m * 128:(m + 1) * 128], mbfb[k].ap[:, :w],
                     start=(k == 0), stop=(k == 7), reads=[O, mbfb[k]], writes=[ps])
            P.op("dve", lambda e, ps=ps, m=m: e.tensor_tensor(xb[m].ap[:, :w], xb[m].ap[:, :w], ps.ap[:, :w],
                                                            op=ALU.add),
                 reads=[xb[m], ps], writes=[xb[m]])
        rmsnorm(w, 8, hb)
        for j0 in range(0, NF, 8):
            nj = min(8, NF - j0)
            Wg = load_w(w_up, 0, 8, j0 * 128, nj * 128)
            Wu = None if halo else load_w(w_up, 0, 8, DFF + j0 * 128, nj * 128)
            for jj in range(nj):
                j = j0 + jj
                pg = psp.get()
                for k in range(8):
                    P.mm(pg.ap[:, :w], Wg.ap[:, k, jj * 128:(jj + 1) * 128], hb[k].ap[:, :w],
                         start=(k == 0), stop=(k == 7), reads=[Wg, hb[k]], writes=[pg])
                if halo:
                    P.op("act", lambda e, pg=pg, j=j: e.activation(histb[j].ap, pg.ap[:, 0:2], AF.Identity),
                         reads=[pg], writes=[histb[j]])
                    continue
                pu = psp.get()
                for k in range(8):
                    P.mm(pu.ap[:, :w], Wu.ap[:, k, jj * 128:(jj + 1) * 128], hb[k].ap[:, :w],
                         start=(k == 0), stop=(k == 7), reads=[Wu, hb[k]], writes=[pu])
                gb = gbb[j % 2]
                cv = cvb[j % 2]
                P.op("pool", lambda e, gb=gb, j=j: e.tensor_copy(gb.ap[:, 0:2], histb[j].ap),
                     reads=[histb[j]], writes=[gb])
                P.op("act", lambda e, gb=gb, pg=pg: e.activation(gb.ap[:, 2:2 + w], pg.ap[:, :w], AF.Identity),
                     reads=[pg], writes=[gb])
                P.op("pool", lambda e, gb=gb, j=j: e.tensor_copy(histb[j].ap, gb.ap[:, w:w + 2]),
                     reads=[gb], writes=[histb[j]])
                P.op("dve", lambda e, gb=gb, cv=cv, j=j: e.tensor_scalar(
                    cv.ap[:, :w], gb.ap[:, 0:w], c_cw.ap[:, 3 * j:3 * j + 1], c_cb.ap[:, j:j + 1],
                    op0=ALU.mult, op1=ALU.add), reads=[gb, c_cw, c_cb], writes=[cv])
                for t in (1, 2):
                    P.op("dve", lambda e, gb=gb, cv=cv, j=j, t=t: e.scalar_tensor_tensor(
                        cv.ap[:, :w], gb.ap[:, t:t + w], c_cw.ap[:, 3 * j + t:3 * j + t + 1], cv.ap[:, :w],
                        op0=ALU.mult, op1=ALU.add), reads=[gb, c_cw, cv], writes=[cv])
                P.op("act", lambda e, cv=cv: e.activation(cv.ap[:, :w], cv.ap[:, :w], AF.Gelu_apprx_tanh),
                     reads=[cv], writes=[cv])
                P.op("dve", lambda e, cv=cv, pu=pu, j=j: e.tensor_tensor(
                    ab[j].ap[:, :w], pu.ap[:, :w], cv.ap[:, :w], op=ALU.mult),
                    reads=[pu, cv], writes=[ab[j]])
        if halo:
            return
        Ds = [load_w(w_down, j0 * 128, min(8, NF - j0), 0, D) for j0 in range(0, NF, 8)]
        for m in range(8):
            ps = psp.get()
            for j in range(NF):
                Dj = Ds[j // 8]
                P.mm(ps.ap[:, :w], Dj.ap[:, j % 8, m * 128:(m + 1) * 128], ab[j].ap[:, :w],
                     start=(j == 0), stop=(j == NF - 1), reads=[Dj, ab[j]], writes=[ps])
            P.op("dve", lambda e, ps=ps, m=m: e.tensor_tensor(xb[m].ap[:, :w], xb[m].ap[:, :w], ps.ap[:, :w],
                                                            op=ALU.add),
                 reads=[xb[m], ps], writes=[xb[m]])
        rmsnorm(w, 16, hb)
        PG = load_w(w_pg, 0, 8, 0, D)
        PP = load_w(w_pp, 0, 2, 0, D)
        for m in range(8):
            pg = psp.get()
            for k in range(8):
                P.mm(pg.ap[:, :w], PG.ap[:, k, m * 128:(m + 1) * 128], hb[k].ap[:, :w],
                     start=(k == 0), stop=(k == 7), reads=[PG, hb[k]], writes=[pg])
            g = gtb[gi[0] % NG]; gi[0] += 1
            P.op("act", lambda e, g=g, pg=pg: e.activation(g.ap[:, :w], pg.ap[:, :w], AF.Sigmoid),
                 reads=[pg], writes=[g])
            pp = psp.get()
            for k in range(2):
                P.mm(pp.ap[:, :w], PP.ap[:, k, m * 128:(m + 1) * 128], pb[k].ap[:, :w],
                     start=(k == 0), stop=(k == 1), reads=[PP, pb[k]], writes=[pp])
            P.op("dve", lambda e, g=g, pp=pp: e.tensor_tensor(g.ap[:, :w], pp.ap[:, :w], g.ap[:, :w], op=ALU.mult),
                 reads=[pp, g], writes=[g])
            P.op("pool", lambda e, g=g, m=m: e.tensor_tensor(xb[m].ap[:, :w], xb[m].ap[:, :w], g.ap[:, :w],
                                                           op=ALU.add),
                 reads=[xb[m], g], writes=[xb[m]])
        if last:
            rmsnorm(w, 24, ob)
            src_o = ob
        else:
            src_o = xb
        for m in range(8):
            src["out"](P, src_o[m], m, c0 - 2, w)
        src["after_tile"](P, c0 - 2)

    tile(0, 2, True)
    for t0 in range(0, TOK, NT):
        tile(2 + t0, min(NT, TOK - t0), False)


IN_OFF = dict(q=0, k=512, v=1024, b=1536, a=1540, g=1544, gq=2056, gk=2312, gv=2568, lr=3080, gr=3096, s5=3608)


def mixer_inputs(layer, hd, x_b, W):
    i = layer
    w_in = W["w_in"][i]
    o = IN_OFF
    cols_f = np.concatenate([
        np.arange(o["q"] + hd * 128, o["q"] + hd * 128 + 128),
        np.arange(o["k"] + hd * 128, o["k"] + hd * 128 + 128),
        np.arange(o["v"] + hd * 128, o["v"] + hd * 128 + 128),
        np.arange(o["s5"] + hd * 128, o["s5"] + hd * 128 + 128),
        np.arange(o["gq"] + hd * 64, o["gq"] + hd * 64 + 64),
        np.arange(o["gk"] + hd * 64, o["gk"] + hd * 64 + 64),
        np.arange(o["lr"], o["lr"] + 16)])
    cols_t = np.concatenate([
        np.arange(o["g"] + hd * 128, o["g"] + hd * 128 + 128),
        [o["b"] + hd], [o["a"] + hd],
        np.arange(o["gv"] + hd * 128, o["gv"] + hd * 128 + 128),
        np.arange(o["gr"] + hd * 128, o["gr"] + hd * 128 + 128)])
    cwv = W["dn_conv_w"][i]
    dn_cw = np.stack([cwv[:, c * 512 + hd * 128: c * 512 + hd * 128 + 128].T for c in range(3)], axis=1)
    rep = lambda v: np.ascontiguousarray(np.broadcast_to(v[None, :], (128, v.shape[0])))
    g0 = hd * 8
    lam = np.zeros((128, 12), np.float32)
    sb_ = np.zeros((2, 4, 128, 16), np.float32)
    sc_ = np.zeros((2, 4, 128, 16), np.float32)
    for j in range(4):
        for hf in range(2):
            g = g0 + 2 * j + hf
            lam[64 * hf:64 * hf + 64, j] = W["s5_lam_re"][i][g]
            lam[64 * hf:64 * hf + 64, 4 + j] = W["s5_lam_im"][i][g]
            lam[64 * hf:64 * hf + 64, 8 + j] = W["s5_log_step"][i][g]
            sb_[0, j, 64 * hf:64 * hf + 64] = W["s5_b_re"][i][g]
            sb_[1, j, 64 * hf:64 * hf + 64] = W["s5_b_im"][i][g]
            sc_[0, j, 64 * hf:64 * hf + 64] = W["s5_c_re"][i][g].T
            sc_[1, j, 64 * hf:64 * hf + 64] = W["s5_c_im"][i][g].T
    return {
        "anorm": np.ascontiguousarray(W["attn_norm"][i].reshape(8, 128).T),
        "wf": np.ascontiguousarray(w_in[:, cols_f]), "wt": np.ascontiguousarray(w_in[:, cols_t]),
        "dn_sc": np.ascontiguousarray(np.stack([np.full(128, W["dn_a_log"][i][hd], np.float32),
                                               np.full(128, W["dn_dt_bias"][i][hd], np.float32)], axis=1)),
        "dn_cw": np.ascontiguousarray(dn_cw.reshape(128, 12)),
        "dn_ng": rep(W["dn_norm"][i]), "gla_ng": rep(W["gla_norm"][i]),
        "gla_w2": np.ascontiguousarray(W["gla_w2"][i][:, hd * 64:hd * 64 + 64]),
        "gla_b2": np.ascontiguousarray(W["gla_b2"][i][hd * 64:hd * 64 + 64].reshape(64, 1)),
        "s5_lam": lam, "s5_b": sb_, "s5_c": sc_,
        "s5_d": np.ascontiguousarray(W["s5_d"][i][hd * 128:hd * 128 + 128].reshape(128, 1)),
    }


def dense_inputs(layer, x_seg, y_seg, p_seg, W):
    i = layer
    f = np.float32
    col = lambda v: np.ascontiguousarray(v.reshape(-1, 128).T)
    norms = np.concatenate([col(W["attn_norm"][i]), col(W["ffn_norm"][i]), col(W["ple_norm"][i]),
                            col(W["final_norm"])], axis=1)
    cw = np.ascontiguousarray(W["ffn_conv_w"][i].reshape(3, NF, 128).transpose(2, 1, 0).reshape(128, NF * 3))
    return {
        "pT": np.ascontiguousarray(p_seg.T),
        "w_gate": W["w_gate"][i], "b_gate": col(W["b_gate"][i]),
        "w_branch": np.ascontiguousarray(W["w_branch"][i].reshape(1536, D)),
        "w_o": W["w_o"][i], "w_glu": W["s5_w_glu"][i], "b_glu": col(W["s5_b_glu"][i]),
        "norms": np.ascontiguousarray(norms), "w_up": W["w_up"][i], "cw": cw, "cb": col(W["ffn_conv_b"][i]),
        "w_down": W["w_down"][i], "w_pg": W["w_ple_gate"][i], "w_pp": W["w_ple_proj"][i],
    }


import os
NOCOLL = os.environ.get('FUSE_NOCOLL') == '1'
NODYN = os.environ.get('FUSE_NODYN') == '1'


def build_fused(L=SEQ):
    nc = bass.Bass("TRN2", target_bir_lowering=False, num_devices=NCORES)
    TOK = L // 4
    CH = 1024
    NCH = L // CH
    CPS = TOK // CH
    NT8 = TOK // 512
    assert CPS >= 1 and TOK % 512 == 0
    xT_b = nc.dram_tensor("xT_b", [D, L], F32, kind="ExternalInput").ap()
    xs0 = nc.dram_tensor("xs0", [D, TOK + 2], F32, kind="ExternalInput").ap()
    hmask_d = nc.dram_tensor("hmask", [128, 1], F32, kind="ExternalInput").ap()
    oT = nc.dram_tensor("oT", [D, TOK], F32, kind="ExternalOutput").ap()
    yloc = [[nc.dram_tensor(f"yloc{l}_{c}", [384, CH], BF16).ap() for c in range(NCH)] for l in range(2)]
    yall_t = [nc.dram_tensor(f"yall{l}", [NCH * 1536, CH], BF16).ap() for l in range(2)]
    x1c = [[nc.dram_tensor(f"x1c_{t}_{h}", [512, 512], F32).ap() for h in range(2)] for t in range(NT8)]
    xgc = [[nc.dram_tensor(f"xgc_{t}_{h}", [4 * 512, 512], F32).ap() for h in range(2)] for t in range(NT8)]
    yseg = nc.dram_tensor("yseg", [4 * 384, TOK + 2], BF16).ap()
    xh = nc.dram_tensor("xh", [D, 2], F32).ap()
    Byloc = [[Buf(yloc[l][c]) for c in range(NCH)] for l in range(2)]
    Byall = [[Buf(yall_t[l][c * 1536:(c + 1) * 1536, :]) for c in range(NCH)] for l in range(2)]
    Bx1c = [[Buf(x1c[t][h]) for h in range(2)] for t in range(NT8)]
    Bxgc = [[Buf(xgc[t][h]) for h in range(2)] for t in range(NT8)]
    Byseg, Bxh = Buf(yseg, "yseg"), Buf(xh, "xh")
    rv = {}
    P = Prog(nc)
    O = Ops(P)

    for layer in (0, 1):
        sfx = f"_{layer}"
        P.prefix = f"m{layer}_"
        P.begin_phase()
        if layer == 0:
            def xsrc(k, t0):
                return xT_b[k * 128:(k + 1) * 128, t0:t0 + TB]
        else:
            def xsrc(k, t0):
                sg_, tt = t0 // TOK, (t0 % TOK) // 512
                r0 = sg_ * 512 + (k % 4) * 128
                return View(Bxgc[tt][k // 4], xgc[tt][k // 4][r0:r0 + 128, :])

        def yout(i3, t0, ysb, layer=layer):
            ci, off = t0 // CH, t0 % CH
            P.dma(yloc[layer][ci][i3 * 128:(i3 + 1) * 128, off:off + TB], ysb.ap, reads=[ysb], writes=[Byloc[layer][ci]])

        def after_sb(t0, layer=layer):
            ci, off = t0 // CH, t0 % CH
            if off + TB == CH and not NOCOLL:
                P.coll("AllGather", yloc[layer][ci], yall_t[layer][ci * 1536:(ci + 1) * 1536, :], GROUPS,
                       reads=[Byloc[layer][ci]], writes=[Byall[layer][ci]])
        emit_mixer(nc, P, O, L, sfx, xsrc, yout, after_sb)
        P.end_phase()
        P.prefix = f"d{layer}_"
        P.begin_phase()
        if layer == 0:
            def setup(e, rv=rv):
                r = e.snap(e.partition_id() % 4, min_val=0, max_val=3)
                rv["yrow"] = e.snap(r * (CPS * 1536), min_val=0, max_val=3 * CPS * 1536)
                rv["hrow"] = e.snap(((r * CPS + (NCH - 1)) % NCH) * 1536, min_val=0, max_val=(NCH - 1) * 1536)
                rv["prow"] = e.snap(((r + 3) % 4) * 512, min_val=0, max_val=3 * 512)
                return None
            P.items["sp"].append(([], setup, None, 0))
        ya = yall_t[layer]
        for j in range(CPS):
            P.dma(yseg[:, 2 + j * CH:2 + (j + 1) * CH],
                  (lambda e, j=j, ya=ya: ya[(slice(j * 1536, (j + 1) * 1536) if NODYN else bass.ds(rv["yrow"] + j * 1536, 1536)), :]),
                  reads=Byall[layer], writes=[Byseg])
        P.dma(yseg[:, 0:2], (lambda e, ya=ya: ya[(slice(0, 1536) if NODYN else bass.ds(rv["hrow"], 1536)), CH - 2:CH]),
              reads=Byall[layer], writes=[Byseg])
        if layer == 1:
            for h in range(2):
                P.dma(xh[h * 512:(h + 1) * 512, :],
                      (lambda e, h=h: xgc[NT8 - 1][h][(slice(0, 512) if NODYN else bass.ds(rv["prow"], 512)), 510:512]),
                      reads=[Bxgc[NT8 - 1][h]], writes=[Bxh])

        def fx(P_, buf, k, c0, w, halo, layer=layer):
            if layer == 0:
                P_.dma(buf.ap[:, :w], xs0[k * 128:(k + 1) * 128, c0:c0 + w], writes=[buf])
            elif halo:
                P_.dma(buf.ap[:, :2], xh[k * 128:(k + 1) * 128, :], reads=[Bxh], writes=[buf])
            else:
                tt = (c0 - 2) // 512
                P_.dma(buf.ap[:, :w], x1c[tt][k // 4][(k % 4) * 128:(k % 4) * 128 + 128, 0:w],
                       reads=[Bx1c[tt][k // 4]], writes=[buf])

        def fy(P_, buf, k, c0, w, halo):
            row0 = (k % 4) * 384 + (k // 4) * 128
            P_.dma(buf.ap[:, :w], yseg[row0:row0 + 128, c0:c0 + w], reads=[Byseg], writes=[buf])

        def fo(P_, buf, m_, t, w, layer=layer):
            if layer == 0:
                tt = t // 512
                P_.dma(x1c[tt][m_ // 4][(m_ % 4) * 128:(m_ % 4) * 128 + 128, 0:w], buf.ap[:, :w],
                       reads=[buf], writes=[Bx1c[tt][m_ // 4]])
            else:
                P_.dma(oT[m_ * 128:(m_ + 1) * 128, t:t + w], buf.ap[:, :w], reads=[buf])

        def after_tile(P_, t, layer=layer):
            if layer == 0 and not NOCOLL:
                tt = t // 512
                for h in range(2):
                    P_.coll("AllGather", x1c[tt][h], xgc[tt][h], GROUPS, reads=[Bx1c[tt][h]], writes=[Bxgc[tt][h]])

        emit_dense(nc, P, TOK, layer == 1, sfx, dict(hmask=hmask_d, x=fx, y=fy, out=fo, after_tile=after_tile))
        P.end_phase(final=(layer == 1))
    P.close()
    return nc


def kernel(**inputs):
    W = {k: np.asarray(v, dtype=np.float32) for k, v in inputs.items()}
    x = np.ascontiguousarray(W.pop("x"))
    p = W.pop("p")
    Bsz, L, _ = x.shape
    depth = W["w_in"].shape[0]
    assert depth == 2 and Bsz == 2
    TOK = L // 4
    nc = build_fused(L)
    xT = [np.ascontiguousarray(x[b].T) for b in range(Bsz)]
    in_maps = []
    for c in range(NCORES):
        b, r = c // 4, c % 4
        s0 = r * TOK
        xs = np.zeros((D, TOK + 2), np.float32)
        lo = max(s0 - 2, 0)
        xs[:, 2 - (s0 - lo):] = xT[b][:, lo:s0 + TOK]
        im = {"xT_b": xT[b], "xs0": xs,
              "hmask": np.full((128, 1), 0.0 if r == 0 else 1.0, np.float32)}
        for i in range(depth):
            for k, v in mixer_inputs(i, r, None, W).items():
                im[f"{k}_{i}"] = v
            for k, v in dense_inputs(i, None, None, p[i, b, s0:s0 + TOK], W).items():
                im[f"{k}_{i}"] = v
        in_maps.append(im)
    res = run_bass_kernel_spmd(nc, in_maps, core_ids=list(range(NCORES)))
    out = np.empty_like(x)
    for c in range(NCORES):
        b, r = c // 4, c % 4
        out[b, r * TOK:(r + 1) * TOK] = res.results[c]["oT"].T
    return out
```
